# Optimizing a Trainium2 kernel written in Bass

```python
import jax, jax.numpy as jnp
from jax import lax
import numpy as np

D_MODEL = 2048
BATCH = 16
SEQ = 2048
DEPTH = 2

CTX_LEN = 256
GRID_W = 64
EPS = 1e-6
ROPE_BASE = 10000.0
Q_BLOCK = 128

A_WIDTH = 512
A_CONV = 3
MLA_HEADS = 4
MLA_NOPE = 128
MLA_ROPE = 64
MLA_V = 128
MLA_Q_RANK = 384
MLA_KV_RANK = 256
B_WIDTH = MLA_HEADS * MLA_V
MLA_SCALE = (MLA_NOPE + MLA_ROPE) ** -0.5
NA_HEADS = 8
NA_HEAD_DIM = 64
C_WIDTH = NA_HEADS * NA_HEAD_DIM
NA_WIN_R = 8
NA_WIN_C = 16
NA_SCALE = NA_HEAD_DIM ** -0.5
D_WIDTH = 512
D_CONV = 31

MIX_WIDTH = A_WIDTH + B_WIDTH + C_WIDTH + D_WIDTH
IN_SPLITS = (('a_x', A_WIDTH), ('a_b', A_WIDTH), ('a_c', A_WIDTH), ('a_g', A_WIDTH),
             ('b_qa', MLA_Q_RANK), ('b_kva', MLA_KV_RANK), ('b_kpe', MLA_ROPE), ('b_g', B_WIDTH),
             ('c_q', C_WIDTH), ('c_k', C_WIDTH), ('c_v', C_WIDTH), ('c_g', C_WIDTH),
             ('d_glu', 2 * D_WIDTH), ('d_g', D_WIDTH))
IN_WIDTH = 4 * A_WIDTH + MLA_Q_RANK + MLA_KV_RANK + MLA_ROPE + B_WIDTH + 4 * C_WIDTH + 3 * D_WIDTH
CTX_KV_NAMES = ('b_kva', 'b_kpe', 'c_k', 'c_v')

kernel_name = 'hybrid_dit_parallel_heads_mla_natten_conv'


def rmsnorm(x, g):
    xf = x.astype(jnp.float32)
    y = xf * lax.rsqrt(jnp.mean(xf * xf, axis=-1, keepdims=True) + EPS)
    return (y * g.astype(jnp.float32)).astype(x.dtype)


def layernorm(x, g, b):
    xf = x.astype(jnp.float32)
    mu = jnp.mean(xf, axis=-1, keepdims=True)
    var = jnp.mean(jnp.square(xf - mu), axis=-1, keepdims=True)
    y = (xf - mu) * lax.rsqrt(var + EPS) * g.astype(jnp.float32) + b.astype(jnp.float32)
    return y.astype(x.dtype)


def _split_in(p):
    parts, start = {}, 0
    for name, width in IN_SPLITS:
        parts[name] = p[..., start:start + width]
        start += width
    return parts


def _project_cols(h, w_in, names):
    parts, start = {}, 0
    for name, width in IN_SPLITS:
        if name in names:
            parts[name] = h @ w_in[:, start:start + width]
        start += width
    return parts


def _heads(t, n_heads):
    return t.reshape(t.shape[:-1] + (n_heads, t.shape[-1] // n_heads))


def dwconv(x, w):
    k = w.shape[0]
    return lax.conv_general_dilated(x, w[:, None, :].astype(x.dtype), window_strides=(1,),
                                    padding=[(k // 2, k // 2)],
                                    dimension_numbers=('NWC', 'WIO', 'NWC'),
                                    feature_group_count=x.shape[-1])


def axial_rope(n):
    t = jnp.arange(n)
    row = (t // GRID_W).astype(jnp.float32)
    col = (t % GRID_W).astype(jnp.float32)
    per_axis = MLA_ROPE // 2
    freqs = ROPE_BASE ** (-(jnp.arange(per_axis // 2, dtype=jnp.float32) * 2.0 / per_axis))
    ang = jnp.concatenate([row[:, None] * freqs, col[:, None] * freqs], axis=-1)
    return jnp.cos(ang), jnp.sin(ang)


def apply_rope(x, cos, sin):
    half = x.shape[-1] // 2
    x1, x2 = x[..., :half], x[..., half:]
    cos, sin = cos.astype(x.dtype), sin.astype(x.dtype)
    return jnp.concatenate([x1 * cos - x2 * sin, x1 * sin + x2 * cos], axis=-1)


def block_attention(q, k, v, scale):
    bsz, s, h, dk = q.shape
    nb = s // Q_BLOCK
    qb = q.reshape(bsz, nb, Q_BLOCK, h, dk).transpose(1, 0, 2, 3, 4)

    def one(qblk):
        sc = jnp.einsum('bqhd,bkhd->bhqk', qblk, k).astype(jnp.float32) * scale
        pr = jax.nn.softmax(sc, axis=-1).astype(v.dtype)
        return jnp.einsum('bhqk,bkhd->bqhd', pr, v)

    o = lax.map(one, qb)
    return o.transpose(1, 0, 2, 3, 4).reshape(bsz, s, h * v.shape[-1])


def neighbourhood_attention(q, k, v, k_ctx, v_ctx, rpb):
    bsz, s, h, d = q.shape
    rows = s // GRID_W
    wr = min(NA_WIN_R, rows)
    wc = NA_WIN_C
    qg = q.reshape(bsz, rows, GRID_W, h, d)
    kg = k.reshape(bsz, rows, GRID_W, h, d)
    vg = v.reshape(bsz, rows, GRID_W, h, d)
    col = jnp.arange(GRID_W)
    cstart = jnp.clip(col - wc // 2, 0, GRID_W - wc)
    cidx = cstart[:, None] + jnp.arange(wc)[None, :]
    dc = cidx - col[:, None] + (NA_WIN_C - 1)

    def one_row(r):
        rstart = jnp.clip(r - wr // 2, 0, rows - wr)
        kb = lax.dynamic_slice_in_dim(kg, rstart, wr, axis=1)
        vb = lax.dynamic_slice_in_dim(vg, rstart, wr, axis=1)
        kw = kb[:, :, cidx]
        vw = vb[:, :, cidx]
        qr = lax.dynamic_index_in_dim(qg, r, axis=1, keepdims=False)
        dr = rstart + jnp.arange(wr) - r + (NA_WIN_R - 1)
        bias = rpb[:, dr][:, :, dc].transpose(0, 2, 1, 3)
        s_loc = jnp.einsum('bqhd,brqjhd->bhqrj', qr, kw).astype(jnp.float32) * NA_SCALE
        s_loc = s_loc + bias[None].astype(jnp.float32)
        s_ctx = jnp.einsum('bqhd,bkhd->bhqk', qr, k_ctx).astype(jnp.float32) * NA_SCALE
        sc = jnp.concatenate([s_loc.reshape(bsz, h, GRID_W, wr * wc), s_ctx], axis=-1)
        pr = jax.nn.softmax(sc, axis=-1).astype(v.dtype)
        p_loc = pr[..., :wr * wc].reshape(bsz, h, GRID_W, wr, wc)
        p_ctx = pr[..., wr * wc:]
        return (jnp.einsum('bhqrj,brqjhd->bqhd', p_loc, vw)
                + jnp.einsum('bhqk,bkhd->bqhd', p_ctx, v_ctx))

    o = lax.map(one_row, jnp.arange(rows))
    return o.transpose(1, 0, 2, 3, 4).reshape(bsz, s, h * d)


def short_conv_mixer(p, w_conv):
    return p['a_b'] * dwconv(p['a_c'] * p['a_x'], w_conv)


def conformer_conv(p, w_dw, b_dw, ln_g, ln_b, w_pw, b_pw):
    a, g = jnp.split(p['d_glu'], 2, axis=-1)
    u = a * jax.nn.sigmoid(g)
    u = dwconv(u, w_dw) + b_dw
    u = jax.nn.silu(layernorm(u, ln_g, ln_b))
    return u @ w_pw + b_pw


def mla_queries(p, q_norm_g, w_uq):
    cq = rmsnorm(p['b_qa'], q_norm_g)
    q = _heads(cq @ w_uq, MLA_HEADS)
    return q[..., :MLA_NOPE], q[..., MLA_NOPE:]


def mla_keys_values(p, kv_norm_g, w_ukv):
    ckv = rmsnorm(p['b_kva'], kv_norm_g)
    kv = _heads(ckv @ w_ukv, MLA_HEADS)
    return kv[..., :MLA_NOPE], kv[..., MLA_NOPE:], p['b_kpe']


def join_rope_key(k_nope, k_pe):
    k_pe = jnp.broadcast_to(k_pe[:, :, None, :], k_nope.shape[:-1] + (MLA_ROPE,))
    return jnp.concatenate([k_nope, k_pe], axis=-1)


def merge_groups(p, y_a, y_b, y_c, y_d, w_out):
    y = jnp.concatenate([y_a * jax.nn.silu(p['a_g']), y_b * jax.nn.silu(p['b_g']),
                         y_c * jax.nn.silu(p['c_g']), y_d * jax.nn.silu(p['d_g'])], axis=-1)
    return y @ w_out


def setup_inputs(seed: int = 0) -> dict:
    key = jax.random.key(seed)
    ks = jax.random.split(key, 24)
    f32 = jnp.float32
    L, D = DEPTH, D_MODEL

    def nrm(k, shape, scale):
        return jax.random.normal(k, shape, f32) * scale

    return {
        'x': nrm(ks[0], (BATCH, SEQ, D), 1.0),
        'c': nrm(ks[1], (BATCH, D), 1.0),
        'ctx': nrm(ks[2], (BATCH, CTX_LEN, D), 1.0),
        'c_ctx': nrm(ks[3], (D,), 1.0),
        'norm_g': 1.0 + nrm(ks[4], (L, D), 0.02),
        'w_ada': nrm(ks[5], (L, D, 3 * D), 0.5 * D ** -0.5),
        'b_ada': nrm(ks[6], (L, 3 * D), 0.02),
        'w_in': nrm(ks[7], (L, D, IN_WIDTH), D ** -0.5),
        'a_conv_w': nrm(ks[8], (L, A_CONV, A_WIDTH), A_CONV ** -0.5),
        'mla_q_norm': 1.0 + nrm(ks[9], (L, MLA_Q_RANK), 0.02),
        'mla_w_uq': nrm(ks[10], (L, MLA_Q_RANK, MLA_HEADS * (MLA_NOPE + MLA_ROPE)), MLA_Q_RANK ** -0.5),
        'mla_kv_norm': 1.0 + nrm(ks[11], (L, MLA_KV_RANK), 0.02),
        'mla_w_ukv': nrm(ks[12], (L, MLA_KV_RANK, MLA_HEADS * (MLA_NOPE + MLA_V)), MLA_KV_RANK ** -0.5),
        'na_rpb': nrm(ks[13], (L, NA_HEADS, 2 * NA_WIN_R - 1, 2 * NA_WIN_C - 1), 0.1),
        'd_dw_w': nrm(ks[14], (L, D_CONV, D_WIDTH), D_CONV ** -0.5),
        'd_dw_b': nrm(ks[15], (L, D_WIDTH), 0.02),
        'd_ln_g': 1.0 + nrm(ks[16], (L, D_WIDTH), 0.02),
        'd_ln_b': nrm(ks[17], (L, D_WIDTH), 0.02),
        'd_pw_w': nrm(ks[18], (L, D_WIDTH, D_WIDTH), D_WIDTH ** -0.5),
        'd_pw_b': nrm(ks[19], (L, D_WIDTH), 0.02),
        'w_out': nrm(ks[20], (L, MIX_WIDTH, D), MIX_WIDTH ** -0.5),
        'final_norm_g': 1.0 + nrm(ks[21], (D,), 0.02),
    }


def reference(x, c, ctx, c_ctx, norm_g, w_ada, b_ada, w_in, a_conv_w, mla_q_norm, mla_w_uq,
              mla_kv_norm, mla_w_ukv, na_rpb, d_dw_w, d_dw_b, d_ln_g, d_ln_b, d_pw_w, d_pw_b,
              w_out, final_norm_g):
    s = x.shape[1]
    cos, sin = axial_rope(s)
    silu_c = jax.nn.silu(c)
    silu_cc = jax.nn.silu(c_ctx)
    xc = ctx
    for l in range(DEPTH):
        last = l == DEPTH - 1
        shift, scale, gate = jnp.split((silu_c @ w_ada[l] + b_ada[l])[:, None, :], 3, axis=-1)
        shift_c, scale_c, gate_c = jnp.split(silu_cc @ w_ada[l] + b_ada[l], 3, axis=-1)
        h = rmsnorm(x, norm_g[l]) * (1.0 + scale) + shift
        hc = rmsnorm(xc, norm_g[l]) * (1.0 + scale_c) + shift_c

        p = _split_in(h @ w_in[l])
        pc = _project_cols(hc, w_in[l], CTX_KV_NAMES) if last else _split_in(hc @ w_in[l])

        kc_nope, vc_mla, kc_pe = mla_keys_values(pc, mla_kv_norm[l], mla_w_ukv[l])
        kc_mla = join_rope_key(kc_nope, kc_pe)
        kc_na = _heads(pc['c_k'], NA_HEADS)
        vc_na = _heads(pc['c_v'], NA_HEADS)

        y_a = short_conv_mixer(p, a_conv_w[l])
        q_nope, q_pe = mla_queries(p, mla_q_norm[l], mla_w_uq[l])
        q_mla = jnp.concatenate([q_nope, apply_rope(q_pe, cos[:, None, :], sin[:, None, :])], axis=-1)
        k_nope, v_mla, k_pe = mla_keys_values(p, mla_kv_norm[l], mla_w_ukv[l])
        k_mla = join_rope_key(k_nope, apply_rope(k_pe, cos, sin))
        y_b = block_attention(q_mla, jnp.concatenate([kc_mla, k_mla], axis=1),
                              jnp.concatenate([vc_mla, v_mla], axis=1), MLA_SCALE)
        y_c = neighbourhood_attention(_heads(p['c_q'], NA_HEADS), _heads(p['c_k'], NA_HEADS),
                                      _heads(p['c_v'], NA_HEADS), kc_na, vc_na, na_rpb[l])
        y_d = conformer_conv(p, d_dw_w[l], d_dw_b[l], d_ln_g[l], d_ln_b[l], d_pw_w[l], d_pw_b[l])
        x = x + gate * merge_groups(p, y_a, y_b, y_c, y_d, w_out[l])

        if not last:
            yc_a = short_conv_mixer(pc, a_conv_w[l])
            qc_nope, qc_pe = mla_queries(pc, mla_q_norm[l], mla_w_uq[l])
            yc_b = block_attention(jnp.concatenate([qc_nope, qc_pe], axis=-1), kc_mla, vc_mla, MLA_SCALE)
            yc_c = block_attention(_heads(pc['c_q'], NA_HEADS), kc_na, vc_na, NA_SCALE)
            yc_d = conformer_conv(pc, d_dw_w[l], d_dw_b[l], d_ln_g[l], d_ln_b[l], d_pw_w[l], d_pw_b[l])
            xc = xc + gate_c * merge_groups(pc, yc_a, yc_b, yc_c, yc_d, w_out[l])
    return rmsnorm(x, final_norm_g)
```

```python
import numpy as np
import concourse.bass as bass
import concourse.mybir as mybir
from concourse.bass_utils import run_bass_kernel_spmd

F32 = mybir.dt.float32
BF16 = mybir.dt.bfloat16
AF = mybir.ActivationFunctionType
ALU = mybir.AluOpType

NCORES = 8
D = 2048
S = 2048
CT = 256
NT = S + CT
L = 2
NB = 2
GW = 64
EPS = 1e-6
MLA_SCALE = float(192 ** -0.5)
NA_SCALE = 0.125
NEG = -30000.0
TILES = [(0, 512), (512, 512), (1024, 512), (1536, 512), (2048, 256)]
UW = 2368
U_LAT = 15
U_CTX = 15 + 2048 + 30

R_QA, R_KPE, R_KSW, R_KVA, R_BG, R_CQ, R_CK, R_CG = 0, 384, 448, 512, 768, 1280, 1792, 2304
PT_ROWS = 2816

DEBUG = False


class Buf:
    __slots__ = ("w", "r", "pr", "strict", "disjoint")

    def __init__(self, strict=False, disjoint=False):
        self.w = {}
        self.r = {}
        self.pr = {}
        self.strict = strict
        self.disjoint = disjoint


def DBuf():
    return Buf(disjoint=True)


COMPUTE = ("pe", "act", "dve", "pool")
STRICT_SAME_ENGINE = True


class Prog:
    def __init__(self, ndma=20):
        self.ops = {e: [] for e in ("pe", "act", "dve", "pool", "sp")}
        self.cnt = {e: 0 for e in COMPUTE}
        self.seen = {e: {} for e in self.ops}
        self.issued = {}
        self.ndma = ndma
        self.dma_rr = {"sp": 0, "pool": 0}
        self.dma_val = {}

    def _waits(self, eng, reads, writes, extra):
        need = {}

        def add(d, strict):
            for k, v in d.items():
                if k == eng and not strict and (eng == "pe" or not STRICT_SAME_ENGINE):
                    continue
                if need.get(k, 0) < v:
                    need[k] = v

        for b in reads:
            add(b.w, b.strict)
        for b in writes:
            if not b.disjoint:
                add(b.w, b.strict)
            else:
                add(b.pr, b.strict)
            add(b.r, b.strict)
        for t in extra:
            if t is not None:
                if need.get(t[0], 0) < t[1]:
                    need[t[0]] = t[1]
        out = []
        sn = self.seen[eng]
        for k, v in need.items():
            if sn.get(k, 0) < v:
                sn[k] = v
                out.append((k, v))
        return out

    def _commit(self, tok, reads, writes):
        k, v = tok
        for b in reads:
            if b.r.get(k, 0) < v:
                b.r[k] = v
        for b in writes:
            if b.r:
                b.pr = b.r
                b.r = {}
                b.w = {}
            if b.w.get(k, 0) < v:
                b.w[k] = v
        if self.issued.get(k, 0) < v:
            self.issued[k] = v

    def op(self, eng, fn, reads=(), writes=(), extra=(), sig=True):
        wl = self._waits(eng, reads, writes, extra)
        tok = None
        if sig:
            self.cnt[eng] += 1
            tok = (eng, self.cnt[eng])
            self._commit(tok, reads, writes)
        self.ops[eng].append((fn, wl, 1 if sig else 0, eng))
        return tok

    def group(self, eng, fns, reads=(), writes=(), extra=()):
        wl = self._waits(eng, reads, writes, extra)
        self.cnt[eng] += 1
        tok = (eng, self.cnt[eng])
        self._commit(tok, reads, writes)
        n = len(fns)
        for i, fn in enumerate(fns):
            self.ops[eng].append((fn, wl if i == 0 else [], 1 if i == n - 1 else 0, eng))
        return tok

    def dma(self, q, out, in_, reads=(), writes=(), extra=()):
        i = self.dma_rr[q]
        self.dma_rr[q] = (i + 1) % self.ndma
        key = ("dma", q, i)
        prev = self.dma_val.get(key, 0)
        ex = list(extra)
        if prev:
            ex.append((key, prev))
        wl = self._waits(q, reads, writes, ex)
        val = prev + 16
        self.dma_val[key] = val
        tok = (key, val)
        self._commit(tok, reads, writes)
        self.ops[q].append((lambda e, o=out, i_=in_: e.dma_start(out=o, in_=i_), wl, 16, key))
        return tok

    def barrier(self):
        snap = dict(self.issued)
        for eng in self.ops:
            wl = []
            sn = self.seen[eng]
            for k, v in snap.items():
                if k == eng:
                    continue
                if sn.get(k, 0) < v:
                    sn[k] = v
                    wl.append((k, v))
            if wl:
                self.ops[eng].append((None, wl, 0, eng))

    def emit(self, nc, engines, sems):
        with nc.Block() as block:
            def run(name):
                def body(e):
                    for fn, wl, inc, key in self.ops[name]:
                        for k, v in wl:
                            e.wait_ge(sems[k], v)
                        if fn is None:
                            continue
                        ins = fn(e)
                        if inc:
                            ins.then_inc(sems[key], inc)
                return body
            block.tensor(run("pe"))
            block.scalar(run("act"))
            block.vector(run("dve"))
            block.gpsimd(run("pool"))
            block.sync(run("sp"))


class Arena:
    def __init__(self, t, nwords):
        self.t = t
        self.nbytes = nwords * 4
        self.off = 0

    def reset(self, off=0):
        self.off = off

    def _take(self, nbytes):
        nb = (nbytes + 63) // 64 * 64
        assert self.off + nb <= self.nbytes, ("arena overflow", self.off, nb, self.nbytes)
        o = self.off
        self.off += nb
        return o

    def f(self, n, parts=128):
        o = self._take(n * 4)
        return self.t[0:parts, o // 4:o // 4 + n]

    def b(self, n, parts=128):
        assert n % 2 == 0
        o = self._take(n * 2)
        return self.t[0:parts, o // 4:o // 4 + n // 2].bitcast(BF16)


def v3(ap, b):
    return ap.rearrange("p (a b) -> p a b", b=b)


def na_tables():
    rows = S // GW
    rstart = [min(max(r - 4, 0), rows - 8) for r in range(rows)]
    Q = {}
    for kr in range(rows):
        qs = [qr for qr in range(rows) if rstart[qr] <= kr <= rstart[qr] + 7]
        assert qs == list(range(qs[0], qs[-1] + 1))
        Q[kr] = (qs[0], qs[-1] + 1)
    return Q


def build_nc():
    nc = bass.Bass("TRN2", target_bir_lowering=False)
    P = Prog()

    def din(name, shape, dt=F32):
        return nc.dram_tensor(name, list(shape), dt, kind="ExternalInput").ap()

    x_in = din("x", [NB, S, D])
    ctx_in = din("ctx", [NB, CT, D])
    cT_in = din("cT", [128, 16 * 3])
    ng3_in = din("ng3", [L, 128, 16 * 3])
    bada3_in = din("bada3", [L, 128, 48 * 3])
    wada_in = din("w_ada", [L, D, 3 * D])
    win_in = din("win_r", [L, D, 6912])
    wout_in = din("w_out", [L, D, D])
    wuq_in = din("wuq_r", [L, 384, 1024])
    wukv_in = din("wukv_r", [L, 256, 1024])
    wpw_in = din("d_pw_w", [L, 512, 512])
    vec_in = din("vecs", [L, 128, 64])
    dww_in = din("dww", [L, 128, 4 * 31])
    fng_in = din("fng", [128, 16])
    rope_in = din("rope", [2, 64, NT])
    ub_in = din("ub", [L, 4, 128, 960])
    ident_in = din("ident", [128, 128])
    onesbd_in = din("onesbd", [128, 128])
    out = nc.dram_tensor("out", [NB, S, D], F32, kind="ExternalOutput").ap()

    skind = "ExternalOutput" if DEBUG else "Internal"
    XT = [nc.dram_tensor(f"XT{b}", [D, NT], F32, kind=skind).ap() for b in range(NB)]
    pT = [nc.dram_tensor(f"pT{b}", [PT_ROWS, NT], BF16, kind=skind).ap() for b in range(NB)]
    vtok = [nc.dram_tensor(f"vtok{b}", [8, 64, 36, 64], BF16, kind=skind).ap() for b in range(NB)]
    yT = [nc.dram_tensor(f"yT{b}", [D, NT], BF16, kind=skind).ap() for b in range(NB)]
    hTd = [nc.dram_tensor(f"hTd{b}", [D, NT], BF16, kind=skind).ap() for b in range(NB)]
    B_hTd = [DBuf() for _ in range(NB)]
    KBD = [nc.dram_tensor(f"KBD{b}", [128, 4 * 36 * 128], BF16, kind=skind).ap() for b in range(NB)]
    VBD = [nc.dram_tensor(f"VBD{b}", [128, 4 * 36 * 128], BF16, kind=skind).ap() for b in range(NB)]
    B_KBD = [DBuf() for _ in range(NB)]
    B_VBD = [DBuf() for _ in range(NB)]
    B_XT = [DBuf() for _ in range(NB)]
    B_pT = [DBuf() for _ in range(NB)]
    B_vtok = [DBuf() for _ in range(NB)]
    B_yT = [DBuf() for _ in range(NB)]

    NW = 51200
    from contextlib import ExitStack
    es = ExitStack()
    ar_t = es.enter_context(nc.sbuf_tensor("arena", [128, NW], F32))
    cst_t = es.enter_context(nc.sbuf_tensor("cst_f", [128, 1400], F32))
    cstb_t = es.enter_context(nc.sbuf_tensor("cst_b", [128, 600], BF16))
    banks = [es.enter_context(nc.psum_tensor(f"ps{i}", [128, 512], F32)) for i in range(8)]
    BK = [Buf() for _ in range(8)]
    AR = Arena(ar_t, NW)

    keys = list(COMPUTE) + [("dma", q, i) for q in ("sp", "pool") for i in range(P.ndma)]
    sems = {}
    for k in keys:
        nm = k if isinstance(k, str) else f"d_{k[1]}_{k[2]}"
        sems[k] = es.enter_context(nc.semaphore("s_" + nm))

    class _CA:
        def __init__(self, t):
            self.t = t
            self.off = 0

        def alloc(self, n):
            ap = self.t[:, self.off:self.off + n]
            self.off += (n + 15) // 16 * 16
            return ap
    CF = _CA(cst_t)
    CB = _CA(cstb_t)
    ident_f = CF.alloc(128)
    eps_t = CF.alloc(1)
    cT = CF.alloc(48)
    modv_l = [CF.alloc(144) for _ in range(L)]
    gsv_l = [CF.alloc(48) for _ in range(L)]
    ng3_l = [CF.alloc(48) for _ in range(L)]
    bada3_l = [CF.alloc(144) for _ in range(L)]
    vecs = CF.alloc(64)
    dww = CF.alloc(124)
    fng = CF.alloc(16)
    onesbd_f = CF.alloc(128)
    ident_b = CB.alloc(128)
    ones_b = CB.alloc(128)
    onesbd_b = CB.alloc(128)
    scT = CB.alloc(48)
    B_const = Buf(strict=True)
    B_mod_l = [Buf(strict=True) for _ in range(L)]
    B_adl = Buf(strict=True)
    B_lay = Buf(strict=True)

    P.dma("sp", ident_f, ident_in, writes=[B_const])
    P.dma("sp", onesbd_f, onesbd_in, writes=[B_const])
    P.dma("sp", cT, cT_in, writes=[B_const])
    P.dma("sp", fng, fng_in, writes=[B_const])
    P.op("dve", lambda e: e.memset(eps_t, EPS), writes=[B_const])
    P.op("dve", lambda e: e.memset(ones_b, 1.0), writes=[B_const])
    P.op("dve", lambda e: e.tensor_copy(out=ident_b, in_=ident_f), reads=[B_const], writes=[B_const])
    P.op("dve", lambda e: e.tensor_copy(out=onesbd_b, in_=onesbd_f), reads=[B_const], writes=[B_const])
    P.op("act", lambda e: e.activation(out=scT, in_=cT, func=AF.Silu), reads=[B_const], writes=[B_const])

    def vcol(i):
        return vecs[:, i:i + 1]
    V_ACW, V_DWB, V_LNG, V_LNB, V_PWB, V_QNG, V_KVG = 0, 12, 16, 20, 24, 28, 31

    cp_rr = [0]
    act_only = [False]

    def evac_copy(out_ap, in_ap, reads, writes, scale=None):
        cp_rr[0] ^= 1
        if cp_rr[0] or act_only[0]:
            if scale is None:
                return P.op("act", lambda e: e.activation(out=out_ap, in_=in_ap, func=AF.Copy), reads=reads, writes=writes)
            return P.op("act", lambda e: e.activation(out=out_ap, in_=in_ap, func=AF.Copy, scale=scale), reads=reads, writes=writes)
        if scale is None:
            return P.op("dve", lambda e: e.tensor_copy(out=out_ap, in_=in_ap), reads=reads, writes=writes)
        return P.op("dve", lambda e: e.tensor_scalar(out=out_ap, in0=in_ap, scalar1=scale, scalar2=None, op0=ALU.mult), reads=reads, writes=writes)

    def mm(out_ap, lhsT, rhs, start, stop):
        return lambda e: e.matmul(out_ap, lhsT=lhsT, rhs=rhs, start=start, stop=stop)

    bk_rr = [0]

    def next_bank(lo=0, hi=6):
        i = lo + bk_rr[0] % (hi - lo)
        bk_rr[0] += 1
        return i


    def ACT(out_, in_, func, reads, writes, bias=None, scale=None):
        kw = {}
        if bias is not None:
            kw["bias"] = bias
        if scale is not None:
            kw["scale"] = scale
        return P.op("act", lambda e: e.activation(out=out_, in_=in_, func=func, **kw), reads=reads, writes=writes)

    def TT(eng, out_, in0, in1, op, reads, writes):
        return P.op(eng, lambda e: e.tensor_tensor(out=out_, in0=in0, in1=in1, op=op), reads=reads, writes=writes)

    def TS(eng, out_, in0, s1, op0, reads, writes):
        return P.op(eng, lambda e: e.tensor_scalar(out=out_, in0=in0, scalar1=s1, scalar2=None, op0=op0), reads=reads, writes=writes)

    def STT(out_, in0, scalar, in1, op0, op1, reads, writes):
        return P.op("dve", lambda e: e.scalar_tensor_tensor(out=out_, in0=in0, scalar=scalar, in1=in1, op0=op0, op1=op1),
                    reads=reads, writes=writes)

    def RCP(out_, in_, reads, writes):
        return P.op("dve", lambda e: e.reciprocal(out=out_, in_=in_), reads=reads, writes=writes)

    def MSET(eng, ap, val, writes):
        return P.op(eng, lambda e: e.memset(ap, val), writes=writes)

    def TRN(out_, in_):
        return lambda e: e.transpose(out_, in_, ident_f)

    AR.reset()
    wb = [v3(AR.b(16 * 512), 512) for _ in range(2)]
    B_wb = [Buf(), Buf()]
    gi_ = 0
    for l in range(L):
        P.dma("sp", ng3_l[l], ng3_in[l], writes=[B_adl])
        P.dma("sp", bada3_l[l], bada3_in[l], writes=[B_adl])
        mbank = 6 + l
        wadav = wada_in[l].rearrange("(c p) n -> p c n", p=128)
        for g in range(12):
            s = gi_ % 2
            gi_ += 1
            P.dma("pool", wb[s], wadav[:, :, g * 512:(g + 1) * 512], writes=[B_wb[s]])
            for j in range(4):
                oc = g * 4 + j
                fns = [mm(banks[mbank][:, oc * 4:oc * 4 + 3], wb[s][:, kc, j * 128:(j + 1) * 128], scT[:, kc * 3:kc * 3 + 3],
                          kc == 0, kc == 15) for kc in range(16)]
                P.group("pe", fns, reads=[B_wb[s], B_const], writes=[BK[mbank]])
        TT("dve", v3(modv_l[l], 3), v3(banks[mbank][:, 0:192], 4)[:, :, 0:3], v3(bada3_l[l], 3), ALU.add, [BK[mbank], B_adl], [B_mod_l[l]])
        TS("dve", gsv_l[l], modv_l[l][:, 48:96], 1.0, ALU.add, [B_mod_l[l]], [B_mod_l[l]])
        TT("dve", gsv_l[l], gsv_l[l], ng3_l[l], ALU.mult, [B_mod_l[l], B_adl], [B_mod_l[l]])
    P.barrier()

    def shv(c, mi, l_):
        return modv_l[l_][:, c * 3 + mi:c * 3 + mi + 1]

    def gsc(c, mi, l_):
        return gsv_l[l_][:, c * 3 + mi:c * 3 + mi + 1]

    def gtv(c, mi, l_):
        return modv_l[l_][:, 96 + c * 3 + mi:96 + c * 3 + mi + 1]

    ntmp = [0]

    def norm_squares(xs, B_xs, n, sq, B_sq, sq_on_act):
        for c in range(16):
            if sq_on_act or c % 4 == 3:
                ACT(sq[:, c, 0:n], xs[:, c, 0:n], AF.Square, [B_xs], [B_sq])
            else:
                TT("dve", sq[:, c, 0:n], xs[:, c, 0:n], xs[:, c, 0:n], ALU.mult, [B_xs], [B_sq])

    def norm_stats(n, sq, B_sq, rs, B_rs):
        bi = next_bank()
        P.group("pe", [mm(banks[bi][:, 0:n], ones_b, sq[:, c, 0:n], c == 0, c == 15) for c in range(16)],
                reads=[B_sq, B_const], writes=[BK[bi]])
        ACT(rs[:, 0:n], banks[bi][:, 0:n], AF.Sqrt, [BK[bi], B_const], [B_rs], bias=eps_t, scale=1.0 / D)
        RCP(rs[:, 0:n], rs[:, 0:n], [B_rs], [B_rs])

    def norm_chunk(xs, B_xs, n, c, mi, l_, rs, B_rs, tmps, B_tmps, ht, B_ht):
        s_ = ntmp[0] % len(tmps)
        ntmp[0] += 1
        TT("dve", tmps[s_][:, 0:n], xs[:, c, 0:n], rs[:, 0:n], ALU.mult, [B_xs, B_rs], [B_tmps[s_]])
        ACT(ht[:, c, 0:n], tmps[s_][:, 0:n], AF.Identity, [B_tmps[s_], B_mod_l[l_]], [B_ht], bias=shv(c, mi, l_), scale=gsc(c, mi, l_))

    def norm_store(n, b_, t0, ht, B_ht):
        P.dma("sp", hTd[b_].rearrange("(c p) t -> p c t", p=128)[:, :, t0:t0 + n], ht[:, :, 0:n], reads=[B_ht], writes=[B_hTd[b_]])

    def emit_norm_tile(xs, B_xs, n, b_, t0, mi, l_, sq, B_sq, rs, B_rs, tmps, B_tmps, ht, B_ht, sq_on_act):
        norm_squares(xs, B_xs, n, sq, B_sq, sq_on_act)
        norm_stats(n, sq, B_sq, rs, B_rs)
        for c in range(16):
            norm_chunk(xs, B_xs, n, c, mi, l_, rs, B_rs, tmps, B_tmps, ht, B_ht)
        norm_store(n, b_, t0, ht, B_ht)

    AR.reset()
    zt = AR.b(4 * 36 * 128)
    B_zt = Buf()
    MSET("dve", zt, 0.0, [B_zt])
    for b in range(NB):
        P.dma("sp", KBD[b], zt, reads=[B_zt], writes=[B_KBD[b]])
        P.dma("sp", VBD[b], zt, reads=[B_zt], writes=[B_VBD[b]])
    xin = [AR.f(2048) for _ in range(4)]
    xst = [v3(AR.f(16 * 512), 512) for _ in range(2)]
    B_xin = [Buf() for _ in range(4)]
    B_xst = [DBuf(), DBuf()]
    p_sq = v3(AR.b(16 * 512), 512)
    p_ht = [v3(AR.b(16 * 512), 512) for _ in range(2)]
    p_rs = [AR.f(512) for _ in range(2)]
    p_tmp = [AR.f(512) for _ in range(4)]
    B_psq_ = DBuf()
    B_prs_ = [Buf(), Buf()]
    pend = [None]

    def pro_back():
        if pend[0] is None:
            return
        (xs_, B_xs_, n_, b_, t0_, mi_, rs_, B_rs_, ht_, B_ht_) = pend[0]
        for c in range(16):
            norm_chunk(xs_, B_xs_, n_, c, mi_, 0, rs_, B_rs_, p_tmp, B_ptmp, ht_, B_ht_)
        norm_store(n_, b_, t0_, ht_, B_ht_)
        pend[0] = None
    B_pht = [DBuf(), DBuf()]
    B_ptmp = [Buf() for _ in range(4)]
    it = 0
    gi = 0
    for b in range(NB):
        XTv0 = XT[b].rearrange("(c p) t -> p c t", p=128)
        for (g0, ntt) in ((0, 4), (4, 4), (8, 4), (12, 4), (16, 2)):
            sg = gi % 2
            gi += 1
            for q in range(ntt):
                tt = g0 + q
                s = it % 4
                it += 1
                src = x_in[b, tt * 128:(tt + 1) * 128, :] if tt < 16 else ctx_in[b, (tt - 16) * 128:(tt - 15) * 128, :]
                P.dma("sp", xin[s], src, writes=[B_xin[s]])
                for g in range(4):
                    bi = next_bank(0, 8)
                    fns = [TRN(banks[bi][:, j * 128:(j + 1) * 128], xin[s][:, (g * 4 + j) * 128:(g * 4 + j + 1) * 128]) for j in range(4)]
                    P.group("pe", fns, reads=[B_xin[s], B_const], writes=[BK[bi]])
                    evac_copy(xst[sg][:, g * 4:(g + 1) * 4, q * 128:(q + 1) * 128], v3(banks[bi][:, :], 128), [BK[bi]], [B_xst[sg]])
            P.dma("sp", XTv0[:, :, g0 * 128:(g0 + ntt) * 128], xst[sg][:, :, 0:ntt * 128], reads=[B_xst[sg]], writes=[B_XT[b]])
            norm_squares(xst[sg], B_xst[sg], ntt * 128, p_sq, B_psq_, False)
            norm_stats(ntt * 128, p_sq, B_psq_, p_rs[sg], B_prs_[sg])
            pro_back()
            pend[0] = (xst[sg], B_xst[sg], ntt * 128, b, g0 * 128, (b if g0 < 16 else 2), p_rs[sg], B_prs_[sg], p_ht[sg], B_pht[sg])
    pro_back()
    P.barrier()

    Qr = na_tables()

    for l in range(L):
        last = (l == L - 1)
        AR.reset()
        P.dma("sp", vecs, vec_in[l], writes=[B_lay])
        P.dma("sp", dww, dww_in[l], writes=[B_lay])
        B_mod = B_mod_l[l]
        P.barrier()

        for b in range(NB):
            XTv = XT[b].rearrange("(c p) t -> p c t", p=128)
            yTv = yT[b].rearrange("(c p) t -> p c t", p=128)
            pv = pT[b].rearrange("(c p) t -> p c t", p=128)
            AR.reset()
            u_pad = v3(AR.b(4 * UW), UW)
            dg = v3(AR.b(4 * NT), NT)
            HT_OFF = AR.off
            hT = v3(AR.b(16 * NT), NT)
            BF_BASE = AR.off
            B_u = Buf()
            B_dg = Buf()
            B_hT = [DBuf() for _ in TILES]
            hTdv = hTd[b].rearrange("(c p) t -> p c t", p=128)
            for ti, (t0, n) in enumerate(TILES):
                P.dma("sp", hT[:, :, t0:t0 + n], hTdv[:, :, t0:t0 + n], reads=[B_hTd[b]], writes=[B_hT[ti]])

            AR.reset(BF_BASE)
            wbuf = [v3(AR.b(16 * 256), 256) for _ in range(2)]
            B_wbuf = [Buf(), Buf()]
            stage = [AR.b(NT) for _ in range(3)]
            B_stage = [DBuf() for _ in range(3)]
            vst = [v3(AR.b(18 * 256), 256) for _ in range(2)]
            B_vst = [DBuf(), DBuf()]
            T = [AR.f(NT) for _ in range(4)]
            B_T = [Buf() for _ in range(4)]
            st_rr = [0]
            act_only[0] = True
            MSET("pool", u_pad, 0.0, [B_u])

            winv = win_in[l].rearrange("(c p) n -> p c n", p=128)
            grp = [0]

            def load_group(g):
                s_ = grp[0] % 2
                P.dma("pool", wbuf[s_], winv[:, :, g * 256:(g + 1) * 256], writes=[B_wbuf[s_]])
                grp[0] += 1
                return s_

            ntl = [len(TILES)]

            def proj_block(s_, o, m, handler, kv=False):
                tiles = TILES if (kv or not last) else TILES[:4]
                ntl[0] = len(tiles)
                for ti, (t0, n) in enumerate(tiles):
                    bi = next_bank()
                    P.group("pe", [mm(banks[bi][0:m, 0:n], wbuf[s_][:, kc, o:o + m], hT[:, kc, t0:t0 + n], kc == 0, kc == 15)
                                   for kc in range(16)], reads=[B_wbuf[s_], B_hT[ti]], writes=[BK[bi]])
                    handler(banks[bi][0:m, 0:n], ti, t0, n, BK[bi])

            def h_copy_T(k):
                def h(ps, ti, t0, n, bk):
                    evac_copy(T[k][:, t0:t0 + n], ps, [bk], [B_T[k]])
                return h

            def h_act_T(k, func):
                def h(ps, ti, t0, n, bk):
                    ACT(T[k][:, t0:t0 + n], ps, func, [bk], [B_T[k]])
                return h

            def h_act_ap(dst3, j, func, bufw):
                def h(ps, ti, t0, n, bk):
                    ACT(dst3[:, j, t0:t0 + n], ps, func, [bk], [bufw])
                return h

            def h_dram(row0, m, func=None):
                s_ = st_rr[0] % 3
                st_rr[0] += 1

                def h(ps, ti, t0, n, bk):
                    if func is None:
                        evac_copy(stage[s_][0:m, t0:t0 + n], ps, [bk], [B_stage[s_]])
                    else:
                        ACT(stage[s_][0:m, t0:t0 + n], ps, func, [bk], [B_stage[s_]])
                    if ti == ntl[0] - 1:
                        P.dma("sp", pT[b][row0:row0 + m, :], stage[s_][0:m, :], reads=[B_stage[s_]], writes=[B_pT[b]])
                return h

            KBDv = KBD[b].rearrange("p (h r k) -> p h r k", h=4, r=36)
            VBDv = VBD[b].rearrange("p (h r k) -> p h r k", h=4, r=36)

            def h_kbd(hp):
                s_ = st_rr[0] % 3
                st_rr[0] += 1

                def h(ps, ti, t0, n, bk):
                    evac_copy(stage[s_][:, t0:t0 + n], ps, [bk], [B_stage[s_]])
                    if ti == ntl[0] - 1:
                        for lo in (0, 64):
                            P.dma("sp", KBDv[lo:lo + 64, hp, :, lo:lo + 64], stage[s_][lo:lo + 64, :].rearrange("p (r k) -> p r k", k=64),
                                  reads=[B_stage[s_]], writes=[B_KBD[b]])
                return h

            def unit_ck(g, hp0):
                s = load_group(g)
                proj_block(s, 0, 128, h_kbd(hp0), kv=True)
                proj_block(s, 128, 128, h_kbd(hp0 + 1), kv=True)

            def unit_A(j):
                s = load_group(2 * j)
                proj_block(s, 0, 128, h_copy_T(0))
                proj_block(s, 128, 128, h_copy_T(1))
                s = load_group(2 * j + 1)
                proj_block(s, 0, 128, h_copy_T(2))
                proj_block(s, 128, 128, h_act_T(3, AF.Silu))
                TT("dve", T[0], T[0], T[1], ALU.mult, [B_T[1]], [B_T[0]])
                w0, w1, w2 = (vcol(V_ACW + j * 3 + k) for k in range(3))
                TS("dve", T[1], T[0], w1, ALU.mult, [B_T[0], B_lay], [B_T[1]])
                for (a, n_) in ((0, S), (S, CT)):
                    STT(T[1][:, a + 1:a + n_], T[0][:, a:a + n_ - 1], w0, T[1][:, a + 1:a + n_], ALU.mult, ALU.add, [B_T[0], B_lay], [B_T[1]])
                    STT(T[1][:, a:a + n_ - 1], T[0][:, a + 1:a + n_], w2, T[1][:, a:a + n_ - 1], ALU.mult, ALU.add, [B_T[0], B_lay], [B_T[1]])
                TT("dve", T[2], T[2], T[3], ALU.mult, [B_T[3]], [B_T[2]])
                ss = st_rr[0] % 3
                st_rr[0] += 1
                TT("dve", stage[ss], T[1], T[2], ALU.mult, [B_T[1], B_T[2]], [B_stage[ss]])
                P.dma("sp", yT[b][j * 128:(j + 1) * 128, :], stage[ss], reads=[B_stage[ss]], writes=[B_yT[b]])

            def unit_D(j):
                s = load_group(8 + j)
                proj_block(s, 0, 128, h_copy_T(0))
                proj_block(s, 128, 128, h_act_T(1, AF.Sigmoid))
                TT("dve", u_pad[:, j, U_LAT:U_LAT + S], T[0][:, 0:S], T[1][:, 0:S], ALU.mult, [B_T[0], B_T[1]], [B_u])
                TT("dve", u_pad[:, j, U_CTX:U_CTX + CT], T[0][:, S:NT], T[1][:, S:NT], ALU.mult, [B_T[0], B_T[1]], [B_u])

            def unit_dg(jj):
                s = load_group(12 + jj)
                for q in range(2):
                    proj_block(s, q * 128, 128, h_act_ap(dg, jj * 2 + q, AF.Silu, B_dg))

            def unit_pair(g, row0, func=None):
                s = load_group(g)
                proj_block(s, 0, 128, h_dram(row0, 128, func), kv=(row0 == R_KVA))
                proj_block(s, 128, 128, h_dram(row0 + 128, 128, func), kv=(row0 == R_KVA))

            def unit_qa2():
                s = load_group(15)
                proj_block(s, 0, 128, h_dram(R_QA + 256, 128))
                proj_block(s, 128, 64, h_dram(R_KPE, 64), kv=True)
                proj_block(s, 192, 64, h_dram(R_KSW, 64), kv=True)

            def unit_v(half):
                s = load_group(25 + half)
                for tt in range(18):
                    bi = next_bank()
                    P.group("pe", [mm(banks[bi][:, 0:256], hT[:, kc, tt * 128:(tt + 1) * 128], wbuf[s][:, kc, :], kc == 0, kc == 15)
                                   for kc in range(16)], reads=[B_wbuf[s], B_hT[min(tt // 4, 4)]], writes=[BK[bi]])
                    evac_copy(vst[half][:, tt, :], banks[bi][:, 0:256], [BK[bi]], [B_vst[half]])
                for rl in range(2):
                    for hl in range(4):
                        hh = 4 * half + hl
                        hp_, hf = hh // 2, hh % 2
                        dst = VBDv[hf * 64:(hf + 1) * 64, hp_, :, hf * 64:(hf + 1) * 64].rearrange("k (t r) d -> r k t d", r=2)[rl]
                        src = vst[half][rl * 64:(rl + 1) * 64, :, hl * 64:(hl + 1) * 64]
                        P.dma("sp", dst, src, reads=[B_vst[half]], writes=[B_VBD[b]])

            unit_A(0)
            unit_pair(14, R_QA)
            unit_qa2()
            unit_A(1)
            unit_pair(16, R_KVA)
            unit_pair(17, R_BG, AF.Silu)
            unit_A(2)
            unit_pair(18, R_BG + 256, AF.Silu)
            unit_pair(19, R_CQ)
            unit_A(3)
            unit_pair(20, R_CQ + 256)
            unit_ck(21, 0)
            unit_D(0)
            unit_ck(22, 2)
            unit_D(1)
            unit_pair(23, R_CG, AF.Silu)
            unit_D(2)
            unit_pair(24, R_CG + 256, AF.Silu)
            unit_D(3)
            unit_dg(0)
            unit_dg(1)
            unit_v(0)
            unit_v(1)
            assert grp[0] == 27
            act_only[0] = False
            P.barrier()

            MLA_PF = 147456
            AR.reset(MLA_PF)
            qa = v3(AR.b(3 * NT), NT)
            kva = v3(AR.b(2 * NT), NT)
            kpe = AR.b(NT)
            ksw = AR.b(NT)
            CC = AR.f(NT)
            SSn = AR.f(NT)
            B_ld, B_w, B_bg = Buf(), Buf(), Buf()
            B_cqt = [Buf() for _ in TILES]
            B_ckvt = [Buf() for _ in TILES]
            P.dma("sp", qa, pv[:, 0:3, :], reads=[B_pT[b]], writes=B_cqt)
            P.dma("sp", kva, pv[:, 4:6, :], reads=[B_pT[b]], writes=B_ckvt)
            P.dma("sp", kpe[0:64, :], pT[b][R_KPE:R_KPE + 64, :], reads=[B_pT[b]], writes=[B_ld])
            P.dma("sp", ksw[0:64, :], pT[b][R_KSW:R_KSW + 64, :], reads=[B_pT[b]], writes=[B_ld])
            P.dma("sp", CC[0:64, :], rope_in[0], writes=[B_ld])
            P.dma("sp", SSn[0:64, :], rope_in[1], writes=[B_ld])
            pt1 = [AR.f(512) for _ in range(2)]
            pt2 = AR.f(512)
            B_pt1 = [Buf(), Buf()]
            B_pt2 = Buf()
            assert AR.off <= NW * 4, AR.off
            B_kpe = Buf()
            MSET("pool", kpe[64:128, :], 0.0, [B_kpe])
            AR.reset(HT_OFF)
            diag = AR.b(4 * 31 * 128)
            wpw = v3(AR.b(4 * 512), 512)
            vb2 = [v3(AR.b(4 * 512), 512)] * 2
            sqb2 = [v3(AR.b(4 * 512), 512)] * 2
            zb2 = [v3(AR.b(4 * 512), 512) for _ in range(2)]
            ys = [v3(AR.b(4 * 512), 512) for _ in range(2)]
            B_diagc = [Buf() for _ in range(4)]
            B_wpw = Buf()
            B_vb2, B_sqb2, B_zb2 = [Buf()] * 2, [Buf()] * 2, [Buf(), Buf()]
            B_ys = [Buf(), Buf()]
            vf2 = [v3(AR.f(4 * 512), 512)] * 2
            B_vf2 = [Buf()] * 2
            st2 = [[AR.f(512) for _ in range(3)] + [None] for _ in range(2)]
            psq = [v3(AR.b(3 * 512), 512) for _ in range(2)]
            B_psq = [DBuf(), DBuf()]
            prqs = [AR.f(512) for _ in range(4)]
            B_prqs = [Buf() for _ in range(4)]
            B_mean2, B_var2 = [Buf(), Buf()], [Buf(), Buf()]
            dt2 = [AR.f(512) for _ in range(2)]
            B_dt2 = [Buf(), Buf()]
            P.dma("pool", wpw, wpw_in[l].rearrange("(c p) n -> p c n", p=128), writes=[B_wpw])
            for c in range(4):
                for k in range(31):
                    TS("dve", diag[:, (c * 31 + k) * 128:(c * 31 + k + 1) * 128], ident_f, dww[:, c * 31 + k:c * 31 + k + 1], ALU.mult,
                       [B_const, B_lay], [B_diagc[c]])
            W_PF = AR.off
            wuq = v3(AR.b(3 * 1024), 1024)
            wukv = v3(AR.b(2 * 1024), 1024)
            P.dma("pool", wuq, wuq_in[l].rearrange("(c p) n -> p c n", p=128), writes=[B_w])
            P.dma("pool", wukv, wukv_in[l].rearrange("(c p) n -> p c n", p=128), writes=[B_w])
            assert AR.off <= MLA_PF, AR.off
            dtiles = [(t0, n, t0) for (t0, n) in TILES[:4]] + ([] if last else [(S, CT, U_CTX - 15)])
            dti = [0]

            CB_ = [0, 1, 2, 3]

            def dconv_conv_pe(di):
                t0, n, uo = dtiles[di]
                for c in range(4):
                    bi = CB_[c]
                    P.group("pe", [mm(banks[bi][:, 0:n], diag[:, (c * 31 + k) * 128:(c * 31 + k + 1) * 128], u_pad[:, c, uo + k:uo + k + n], k == 0, k == 30)
                                   for k in range(31)], reads=[B_diagc[c], B_u], writes=[BK[bi]])

            def dconv_conv_evac(di):
                t0, n, uo = dtiles[di]
                p = di % 2
                vf, vb, sqb = vf2[p], vb2[p], sqb2[p]
                for c in range(4):
                    bi = CB_[c]
                    ACT(vf[:, c, 0:n], banks[bi][:, 0:n], AF.Identity, [BK[bi], B_lay], [B_vf2[p]], bias=vcol(V_DWB + c))
                    ACT(sqb[:, c, 0:n], banks[bi][:, 0:n], AF.Square, [BK[bi], B_lay], [B_sqb2[p]], bias=vcol(V_DWB + c))
                    P.op("dve", (lambda o, i_: (lambda e: e.tensor_copy(out=o, in_=i_)))(vb[:, c, 0:n], vf[:, c, 0:n]), reads=[B_vf2[p]], writes=[B_vb2[p]])

            def dconv_stats_pe(di):
                t0, n, uo = dtiles[di]
                p = di % 2
                P.group("pe", [mm(banks[4][:, 0:n], ones_b, vb2[p][:, c, 0:n], c == 0, c == 3) for c in range(4)], reads=[B_vb2[p], B_const], writes=[BK[4]])
                P.group("pe", [mm(banks[5][:, 0:n], ones_b, sqb2[p][:, c, 0:n], c == 0, c == 3) for c in range(4)], reads=[B_sqb2[p], B_const], writes=[BK[5]])

            def dconv_ln(di):
                t0, n, uo = dtiles[di]
                p = di % 2
                vf, zb = vf2[p], zb2[p]
                B_vf, B_zb = B_vf2[p], B_zb2[p]
                mean, msq, var, _unused = st2[p]
                B_mean, B_var = B_mean2[p], B_var2[p]
                TS("dve", mean[:, 0:n], banks[4][:, 0:n], 1.0 / 512, ALU.mult, [BK[4]], [B_mean])
                TT("dve", msq[:, 0:n], mean[:, 0:n], mean[:, 0:n], ALU.mult, [B_mean], [B_var])
                STT(var[:, 0:n], banks[5][:, 0:n], 1.0 / 512, msq[:, 0:n], ALU.mult, ALU.subtract, [BK[5], B_var], [B_var])
                ACT(var[:, 0:n], var[:, 0:n], AF.Sqrt, [B_var, B_const], [B_var], bias=eps_t)
                RCP(var[:, 0:n], var[:, 0:n], [B_var], [B_var])
                for c in range(4):
                    q = dti[0] % 2
                    dti[0] += 1
                    TT("dve", dt2[q][:, 0:n], vf[:, c, 0:n], mean[:, 0:n], ALU.subtract, [B_vf, B_mean], [B_dt2[q]])
                    TT("pool", dt2[q][:, 0:n], dt2[q][:, 0:n], var[:, 0:n], ALU.mult, [B_var], [B_dt2[q]])
                    ACT(zb[:, c, 0:n], dt2[q][:, 0:n], AF.Silu, [B_dt2[q], B_lay], [B_zb], bias=vcol(V_LNB + c), scale=vcol(V_LNG + c))

            def dconv_pw(di):
                t0, n, uo = dtiles[di]
                p = di % 2
                ysb, zb = ys[p], zb2[p]
                for co in range(4):
                    bi = 6 + co % 2
                    P.group("pe", [mm(banks[bi][:, 0:n], wpw[:, ci, co * 128:(co + 1) * 128], zb[:, ci, 0:n], ci == 0, ci == 3) for ci in range(4)],
                            reads=[B_wpw, B_zb2[p]], writes=[BK[bi]])
                    STT(ysb[:, co, 0:n], banks[bi][:, 0:n], vcol(V_PWB + co), dg[:, co, t0:t0 + n], ALU.add, ALU.mult,
                        [BK[bi], B_dg, B_lay], [B_ys[p]])
                P.dma("sp", yTv[:, 12:16, t0:t0 + n], ysb[:, :, 0:n], reads=[B_ys[p]], writes=[B_yT[b]])

            ptr = [0]

            def prep_srcs(ti):
                lst = []
                for k, (src, nchunk, Bs, gidx, dim) in enumerate(((qa, 3, B_cqt[ti], V_QNG, 384.0), (kva, 2, B_ckvt[ti], V_KVG, 256.0))):
                    if last and ti == 4 and k == 0:
                        continue
                    lst.append((k, src, nchunk, Bs, gidx, dim))
                return lst

            def prep_sq(ti):
                t0, n = TILES[ti]
                for (k, src, nchunk, Bs, gidx, dim) in prep_srcs(ti):
                    for c in range(nchunk):
                        ACT(psq[k][:, c, 0:n], src[:, c, t0:t0 + n], AF.Square, [Bs], [B_psq[k]])

            def prep_mm(ti):
                t0, n = TILES[ti]
                for (k, src, nchunk, Bs, gidx, dim) in prep_srcs(ti):
                    bi = 6 + k
                    rq, B_rq = prqs[(ti % 2) * 2 + k], B_prqs[(ti % 2) * 2 + k]
                    P.group("pe", [mm(banks[bi][:, 0:n], ones_b, psq[k][:, c, 0:n], c == 0, c == nchunk - 1) for c in range(nchunk)],
                            reads=[B_psq[k], B_const], writes=[BK[bi]])
                    ACT(rq[:, 0:n], banks[bi][:, 0:n], AF.Sqrt, [BK[bi], B_const], [B_rq], bias=eps_t, scale=1.0 / dim)

            def prep_fin(ti):
                t0, n = TILES[ti]
                for (k, src, nchunk, Bs, gidx, dim) in prep_srcs(ti):
                    rq, B_rq = prqs[(ti % 2) * 2 + k], B_prqs[(ti % 2) * 2 + k]
                    RCP(rq[:, 0:n], rq[:, 0:n], [B_rq], [B_rq])
                    for c in range(nchunk):
                        s_ = ptr[0] % 2
                        ptr[0] += 1
                        TT("dve", pt1[s_][:, 0:n], src[:, c, t0:t0 + n], rq[:, 0:n], ALU.mult, [Bs, B_rq], [B_pt1[s_]])
                        ACT(src[:, c, t0:t0 + n], pt1[s_][:, 0:n], AF.Identity, [B_pt1[s_], B_lay], [Bs], scale=vcol(gidx + c))
                s_ = ptr[0] % 2
                ptr[0] += 1
                TT("dve", pt1[s_][0:64, 0:n], kpe[0:64, t0:t0 + n], CC[0:64, t0:t0 + n], ALU.mult, [B_ld, B_kpe], [B_pt1[s_]])
                TT("pool", pt2[0:64, 0:n], ksw[0:64, t0:t0 + n], SSn[0:64, t0:t0 + n], ALU.mult, [B_ld], [B_pt2])
                TT("dve", kpe[0:64, t0:t0 + n], pt1[s_][0:64, 0:n], pt2[0:64, 0:n], ALU.add, [B_pt1[s_], B_pt2, B_ld], [B_kpe])

            ndt = len(dtiles)
            dconv_conv_pe(0)
            dconv_conv_evac(0)
            prep_sq(0)
            for di in range(ndt):
                dconv_stats_pe(di)
                if di >= 1:
                    dconv_pw(di - 1)
                prep_mm(di)
                if di + 1 < ndt:
                    dconv_conv_pe(di + 1)
                dconv_ln(di)
                if di >= 1:
                    prep_fin(di - 1)
                if di + 1 < ndt:
                    dconv_conv_evac(di + 1)
                if di + 1 < len(TILES):
                    prep_sq(di + 1)
            dconv_pw(ndt - 1)
            prep_fin(ndt - 1)
            for ti in range(ndt, len(TILES)):
                prep_mm(ti)
                prep_fin(ti)
            P.barrier()

            AR.reset()
            bg = v3(AR.b(4 * NT), NT)
            P.dma("sp", bg, pv[:, 6:10, :], reads=[B_pT[b]], writes=[B_bg])
            kn = v3(AR.b(4 * NT), NT)
            vt = v3(AR.b(18 * 512), 512)
            qn = v3(AR.b(4 * NT), NT)
            qr = v3(AR.b(4 * NT), NT)
            kr, B_kr = kpe, B_kpe
            PTb = [AR.b(512) for _ in range(4)]
            ysm = [AR.b(512) for _ in range(2)]
            B_qr = Buf()
            B_kn, B_vt, B_qn = DBuf(), DBuf(), DBuf()
            B_PT = [Buf() for _ in range(4)]
            B_ysm = [Buf(), Buf()]
            t1 = [AR.f(512) for _ in range(2)]
            t2 = [AR.f(512) for _ in range(2)]
            B_t1 = [Buf(), Buf()]
            B_t2 = [Buf(), Buf()]
            rc = [AR.f(512) for _ in range(2)]
            B_rc = [Buf(), Buf()]
            assert AR.off <= W_PF, (AR.off, W_PF)
            MSET("dve", qr, 0.0, [B_qr])
            tr = 0
            for h in range(4):
                for ti, (t0, n) in enumerate(TILES):
                    bi = next_bank()
                    P.group("pe", [mm(banks[bi][:, 0:n], wukv[:, kc, h * 128:(h + 1) * 128], kva[:, kc, t0:t0 + n], kc == 0, kc == 1) for kc in range(2)],
                            reads=[B_w, B_ckvt[ti]], writes=[BK[bi]])
                    evac_copy(kn[:, h, t0:t0 + n], banks[bi][:, 0:n], [BK[bi]], [B_kn])
                    if last and ti == 4:
                        continue
                    bi = next_bank()
                    P.group("pe", [mm(banks[bi][:, 0:n], wuq[:, kc, h * 128:(h + 1) * 128], qa[:, kc, t0:t0 + n], kc == 0, kc == 2) for kc in range(3)],
                            reads=[B_w, B_cqt[ti]], writes=[BK[bi]])
                    evac_copy(qn[:, h, t0:t0 + n], banks[bi][:, 0:n], [BK[bi]], [B_qn])
                    b1 = next_bank()
                    P.group("pe", [mm(banks[b1][0:64, 0:n], wuq[:, kc, 512 + h * 64:512 + (h + 1) * 64], qa[:, kc, t0:t0 + n], kc == 0, kc == 2) for kc in range(3)],
                            reads=[B_w, B_cqt[ti]], writes=[BK[b1]])
                    b2 = next_bank()
                    P.group("pe", [mm(banks[b2][0:64, 0:n], wuq[:, kc, 768 + h * 64:768 + (h + 1) * 64], qa[:, kc, t0:t0 + n], kc == 0, kc == 2) for kc in range(3)],
                            reads=[B_w, B_cqt[ti]], writes=[BK[b2]])
                    s = tr % 2
                    tr += 1
                    TT("dve", t1[s][0:64, 0:n], banks[b1][0:64, 0:n], CC[0:64, t0:t0 + n], ALU.mult, [BK[b1], B_ld], [B_t1[s]])
                    TT("dve", t2[s][0:64, 0:n], banks[b2][0:64, 0:n], SSn[0:64, t0:t0 + n], ALU.mult, [BK[b2], B_ld], [B_t2[s]])
                    TT("pool", qr[0:64, h, t0:t0 + n], t1[s][0:64, 0:n], t2[s][0:64, 0:n], ALU.add, [B_t1[s], B_t2[s]], [B_qr])
            for tt in range(18):
                bi = next_bank()
                P.group("pe", [mm(banks[bi][:, :], kva[:, kc, tt * 128:(tt + 1) * 128], wukv[:, kc, 512:1024], kc == 0, kc == 1) for kc in range(2)],
                        reads=[B_w, B_ckvt[min(tt // 4, 4)]], writes=[BK[bi]])
                evac_copy(vt[:, tt, :], banks[bi][:, :], [BK[bi]], [B_vt])
            ai = 0
            for h in range(4):
                for qi, (q0, n) in enumerate(TILES):
                    if qi == 4 and last:
                        continue
                    kts = [16, 17] + (list(range(16)) if qi < 4 else [])
                    ob, db = 3 + ai % 2, 5 + ai % 2
                    s = ai % 2
                    ai += 1

                    SBK = (0, 1, 2, 7)

                    def score(j):
                        kt = kts[j]
                        sb = SBK[j % 4]
                        P.group("pe", [mm(banks[sb][:, 0:n], kn[:, h, kt * 128:(kt + 1) * 128], qn[:, h, q0:q0 + n], True, False),
                                       mm(banks[sb][:, 0:n], kr[:, kt * 128:(kt + 1) * 128], qr[:, h, q0:q0 + n], False, True)],
                                reads=[B_kn, B_kr, B_qn, B_qr], writes=[BK[sb]])
                    score(0)
                    if len(kts) > 1:
                        score(1)
                    for j in range(len(kts)):
                        if j + 2 < len(kts):
                            score(j + 2)
                        sb = SBK[j % 4]
                        pb = j % 4
                        kt = kts[j]
                        ACT(PTb[pb][:, 0:n], banks[sb][:, 0:n], AF.Exp, [BK[sb]], [B_PT[pb]], scale=MLA_SCALE)
                        first, lastk = (j == 0), (j == len(kts) - 1)
                        P.group("pe", [mm(banks[ob][:, 0:n], vt[:, kt, h * 128:(h + 1) * 128], PTb[pb][:, 0:n], first, lastk),
                                       mm(banks[db][:, 0:n], ones_b, PTb[pb][:, 0:n], first, lastk)],
                                reads=[B_vt, B_PT[pb], B_const], writes=[BK[ob], BK[db]])
                    RCP(rc[s][:, 0:n], banks[db][:, 0:n], [BK[db]], [B_rc[s]])
                    TT("dve", rc[s][:, 0:n], banks[ob][:, 0:n], rc[s][:, 0:n], ALU.mult, [BK[ob]], [B_rc[s]])
                    TT("pool", ysm[s][:, 0:n], rc[s][:, 0:n], bg[:, h, q0:q0 + n], ALU.mult, [B_rc[s], B_bg], [B_ysm[s]])
                    P.dma("sp", yT[b][512 + h * 128:512 + (h + 1) * 128, q0:q0 + n], ysm[s][:, 0:n], reads=[B_ysm[s]], writes=[B_yT[b]])
            P.barrier()

            AR.reset()
            cq = v3(AR.b(4 * NT), NT)
            cg = v3(AR.b(4 * NT), NT)
            kbd = AR.b(4 * 36 * 128)
            vbd = AR.b(4 * 36 * 128)
            ubb = v3(AR.b(4 * 960), 960)
            PTn = [AR.b(512) for _ in range(4)]
            ysn = [AR.b(512) for _ in range(2)]
            B_nl, B_kbd, B_vbd, B_ub = Buf(), Buf(), Buf(), Buf()
            B_PTn = [Buf() for _ in range(4)]
            B_ysn = [Buf(), Buf()]
            rcn = [AR.f(512) for _ in range(2)]
            B_rcn = [Buf(), Buf()]
            kbd4 = kbd.rearrange("p (h r k) -> p h r k", h=4, r=36)
            vbd4 = vbd.rearrange("p (h r k) -> p h r k", h=4, r=36)
            kbdh = kbd.rearrange("p (h q) -> p h q", h=4)
            vbdh = vbd.rearrange("p (h q) -> p h q", h=4)
            KBDh = KBD[b].rearrange("p (h q) -> p h q", h=4)
            VBDh = VBD[b].rearrange("p (h q) -> p h q", h=4)
            B_kbdh = [Buf() for _ in range(4)]
            B_vbdh = [Buf() for _ in range(4)]
            B_cqh = [Buf() for _ in range(4)]
            B_cgl = Buf()
            ubf = v3(AR.f(4 * 960), 960)
            B_ubf = Buf()
            P.dma("sp", ubf, ub_in[l].rearrange("h p n -> p h n"), writes=[B_ubf])
            ACT(ubb, ubf, AF.Copy, [B_ubf], [B_ub], scale=8.0)
            for hp in range(4):
                P.dma("sp", kbdh[:, hp, :], KBDh[:, hp, :], reads=[B_KBD[b]], writes=[B_kbdh[hp]])
                P.dma("sp", cq[:, hp, :], pv[:, 10 + hp, :], reads=[B_pT[b]], writes=[B_cqh[hp]])
                P.dma("sp", vbdh[:, hp, :], VBDh[:, hp, :], reads=[B_VBD[b]], writes=[B_vbdh[hp]])
            P.dma("sp", cg, pv[:, 18:22, :], reads=[B_pT[b]], writes=[B_cgl])
            ai = 0
            qblocks = [(m * 512, 512, m) for m in range(4)] + ([] if last else [(S, CT, -1)])
            for hp in range(4):
                for (q0, n, m) in qblocks:
                    ob, db = 3 + ai % 2, 5 + ai % 2
                    s = ai % 2
                    ai += 1
                    items = [(32 + r, 0, n, None) for r in range(4)]
                    if m >= 0:
                        for kr_ in range(32):
                            qa_ = max(8 * m, Qr[kr_][0])
                            qb_ = min(8 * m + 8, Qr[kr_][1])
                            if qb_ > qa_:
                                j0 = 7 - kr_ + qa_
                                assert 0 <= j0 and j0 + (qb_ - qa_) <= 15
                                items.append((kr_, (qa_ - 8 * m) * 64, (qb_ - qa_) * 64, j0))

                    SBK = (0, 1, 2, 7)

                    def nscore(j):
                        row, c0, w, j0 = items[j]
                        sb = SBK[j % 4]
                        fns = []
                        if j0 is not None:
                            fns.append(mm(banks[sb][:, 0:w], ident_b, ubb[:, hp, j0 * 64:j0 * 64 + w], True, False))
                        fns.append(mm(banks[sb][:, 0:w], kbd4[:, hp, row, :], cq[:, hp, q0 + c0:q0 + c0 + w], j0 is None, True))
                        P.group("pe", fns, reads=[B_kbdh[hp], B_cqh[hp], B_ub, B_const], writes=[BK[sb]])
                    nscore(0)
                    if len(items) > 1:
                        nscore(1)
                    for j in range(len(items)):
                        if j + 2 < len(items):
                            nscore(j + 2)
                        row, c0, w, j0 = items[j]
                        sb = SBK[j % 4]
                        pb = j % 4
                        ACT(PTn[pb][:, 0:w], banks[sb][:, 0:w], AF.Exp, [BK[sb]], [B_PTn[pb]], scale=NA_SCALE)
                        first, lastk = (j == 0), (j == len(items) - 1)
                        P.group("pe", [mm(banks[ob][:, c0:c0 + w], vbd4[:, hp, row, :], PTn[pb][:, 0:w], first, lastk),
                                       mm(banks[db][:, c0:c0 + w], onesbd_b, PTn[pb][:, 0:w], first, lastk)],
                                reads=[B_vbdh[hp], B_PTn[pb], B_const], writes=[BK[ob], BK[db]])
                    RCP(rcn[s][:, 0:n], banks[db][:, 0:n], [BK[db]], [B_rcn[s]])
                    TT("dve", rcn[s][:, 0:n], banks[ob][:, 0:n], rcn[s][:, 0:n], ALU.mult, [BK[ob]], [B_rcn[s]])
                    TT("pool", ysn[s][:, 0:n], rcn[s][:, 0:n], cg[:, hp, q0:q0 + n], ALU.mult, [B_rcn[s], B_cgl], [B_ysn[s]])
                    P.dma("sp", yT[b][1024 + hp * 128:1024 + (hp + 1) * 128, q0:q0 + n], ysn[s][:, 0:n], reads=[B_ysn[s]], writes=[B_yT[b]])
            P.barrier()

            AR.reset()
            wout = v3(AR.b(16 * 2048), 2048)
            yt = [v3(AR.b(16 * 512), 512) for _ in range(2)]
            sqm = v3(AR.b(16 * 512), 512)
            B_woutg = [Buf() for _ in range(4)]
            B_sqm = DBuf()
            B_yt = [Buf(), Buf()]
            xts = [v3(AR.f(16 * 512), 512) for _ in range(2)]
            B_xts = [Buf(), Buf()]
            rsm = AR.f(512)
            B_rsm = Buf()
            if last:
                osbs = [AR.f(2048) for _ in range(2)]
                B_osbs = [DBuf(), DBuf()]
            else:
                m_ht = v3(AR.b(16 * 512), 512)
                B_mht = DBuf()
                m_tmp = [AR.f(512) for _ in range(2)]
                B_mtmp = [Buf(), Buf()]
            woutv = wout_in[l].rearrange("(c p) n -> p c n", p=128)
            for g in range(4):
                P.dma("pool", wout[:, :, g * 512:(g + 1) * 512], woutv[:, :, g * 512:(g + 1) * 512], writes=[B_woutg[g]])
            mtiles = TILES[:4] if last else TILES
            oi = [0]

            def merge_loads(ti_):
                t0_, n_ = mtiles[ti_]
                P.dma("sp", yt[ti_ % 2][:, :, 0:n_], yTv[:, :, t0_:t0_ + n_], reads=[B_yT[b]], writes=[B_yt[ti_ % 2]])
                P.dma("sp", xts[ti_ % 2][:, :, 0:n_], XTv[:, :, t0_:t0_ + n_], reads=[B_XT[b]], writes=[B_xts[ti_ % 2]])

            def merge_block(ti_, db_):
                t0_, n_ = mtiles[ti_]
                mi = b if ti_ < 4 else 2
                xt, B_xt = xts[ti_ % 2], B_xts[ti_ % 2]
                bi = next_bank()
                P.group("pe", [mm(banks[bi][:, 0:n_], wout[:, kc, db_ * 128:(db_ + 1) * 128], yt[ti_ % 2][:, kc, 0:n_], kc == 0, kc == 15) for kc in range(16)],
                        reads=[B_woutg[db_ // 4], B_yt[ti_ % 2]], writes=[BK[bi]])
                STT(xt[:, db_, 0:n_], banks[bi][:, 0:n_], gtv(db_, mi, l), xt[:, db_, 0:n_], ALU.mult, ALU.add, [BK[bi], B_mod], [B_xt])

            def fin_squares(ti_):
                t0_, n_ = mtiles[ti_]
                xt, B_xt = xts[ti_ % 2], B_xts[ti_ % 2]
                for c in range(16):
                    ACT(sqm[:, c, 0:n_], xt[:, c, 0:n_], AF.Square, [B_xt], [B_sqm])

            def fin_stats(ti_):
                t0_, n_ = mtiles[ti_]
                bi = 7
                P.group("pe", [mm(banks[bi][:, 0:n_], ones_b, sqm[:, c, 0:n_], c == 0, c == 15) for c in range(16)], reads=[B_sqm, B_const], writes=[BK[bi]])
                ACT(rsm[:, 0:n_], banks[bi][:, 0:n_], AF.Sqrt, [BK[bi], B_const], [B_rsm], bias=eps_t, scale=1.0 / D)
                RCP(rsm[:, 0:n_], rsm[:, 0:n_], [B_rsm], [B_rsm])

            def fin_scale(ti_, c):
                t0_, n_ = mtiles[ti_]
                xt, B_xt = xts[ti_ % 2], B_xts[ti_ % 2]
                TT("pool", xt[:, c, 0:n_], xt[:, c, 0:n_], rsm[:, 0:n_], ALU.mult, [B_rsm], [B_xt])
                ACT(xt[:, c, 0:n_], xt[:, c, 0:n_], AF.Identity, [B_const], [B_xt], scale=fng[:, c:c + 1])

            def fin_out(ti_):
                t0_, n_ = mtiles[ti_]
                xt, B_xt = xts[ti_ % 2], B_xts[ti_ % 2]
                for sub in range(n_ // 128):
                    o_ = oi[0] % 2
                    oi[0] += 1
                    for g in range(4):
                        bi = next_bank()
                        fns = [TRN(banks[bi][:, j * 128:(j + 1) * 128], xt[:, g * 4 + j, sub * 128:(sub + 1) * 128]) for j in range(4)]
                        P.group("pe", fns, reads=[B_xt, B_const], writes=[BK[bi]])
                        evac_copy(osbs[o_][:, g * 512:(g + 1) * 512], banks[bi][:, :], [BK[bi]], [B_osbs[o_]])
                    P.dma("sp", out[b, t0_ + sub * 128:t0_ + (sub + 1) * 128, :], osbs[o_], reads=[B_osbs[o_]], writes=[])

            merge_loads(0)
            if not last:
                nt0 = len(mtiles)

                def nargs(ti_):
                    t0_, n_ = mtiles[ti_]
                    return (xts[ti_ % 2], B_xts[ti_ % 2], n_)
                merge_loads(1)
                for db_ in range(16):
                    merge_block(0, db_)
                for ti in range(nt0):
                    t0, n = mtiles[ti]
                    mi_ = b if ti < 4 else 2
                    xs_, B_xs_, _n = nargs(ti)
                    P.dma("sp", XTv[:, :, t0:t0 + n], xs_[:, :, 0:n], reads=[B_xs_], writes=[B_XT[b]])
                    norm_squares(xs_, B_xs_, n, sqm, B_sqm, True)
                    if ti + 1 < nt0:
                        for db_ in range(16):
                            merge_block(ti + 1, db_)
                            if db_ == 3:
                                norm_stats(n, sqm, B_sqm, rsm, B_rsm)
                            if db_ >= 4:
                                norm_chunk(xs_, B_xs_, n, db_ - 4, mi_, l + 1, rsm, B_rsm, m_tmp, B_mtmp, m_ht, B_mht)
                        for c in range(12, 16):
                            norm_chunk(xs_, B_xs_, n, c, mi_, l + 1, rsm, B_rsm, m_tmp, B_mtmp, m_ht, B_mht)
                    else:
                        norm_stats(n, sqm, B_sqm, rsm, B_rsm)
                        for c in range(16):
                            norm_chunk(xs_, B_xs_, n, c, mi_, l + 1, rsm, B_rsm, m_tmp, B_mtmp, m_ht, B_mht)
                    norm_store(n, b, t0, m_ht, B_mht)
                    if ti + 2 < nt0:
                        merge_loads(ti + 2)
            else:
                nt_ = len(mtiles)
                merge_loads(1)
                for db_ in range(16):
                    merge_block(0, db_)
                for ti in range(nt_):
                    nxt = ti + 1 < nt_
                    fin_squares(ti)
                    if nxt:
                        for db_ in range(16):
                            merge_block(ti + 1, db_)
                            if db_ == 3:
                                fin_stats(ti)
                            if db_ >= 4:
                                fin_scale(ti, db_ - 4)
                        for c in range(12, 16):
                            fin_scale(ti, c)
                    else:
                        fin_stats(ti)
                        for c in range(16):
                            fin_scale(ti, c)
                    fin_out(ti)
                    if ti + 2 < nt_:
                        merge_loads(ti + 2)
            P.barrier()

    P.barrier()
    P.emit(nc, None, sems)
    es.close()
    return nc


def _fm(v, nchunk):
    return np.ascontiguousarray(np.asarray(v, np.float32).reshape(nchunk, 128).T)


def _host_layout(inp):
    f32 = np.float32
    w_in = inp["w_in"]
    o = dict(a_x=0, a_b=512, a_c=1024, a_g=1536, b_qa=2048, b_kva=2432, b_kpe=2688, b_g=2752,
             c_q=3264, c_k=3776, c_v=4288, c_g=4800, glu_a=5312, glu_g=5824, d_g=6336)
    cols = []

    def blk(name, j, w=128):
        cols.extend(range(o[name] + j * w, o[name] + (j + 1) * w))
    for j in range(4):
        blk("a_x", j); blk("a_c", j); blk("a_b", j); blk("a_g", j)
    for j in range(4):
        blk("glu_a", j); blk("glu_g", j)
    for j in range(4):
        blk("d_g", j)
    for j in range(3):
        blk("b_qa", j)
    cols.extend(range(o["b_kpe"], o["b_kpe"] + 64))
    cols.extend(list(range(o["b_kpe"] + 32, o["b_kpe"] + 64)) + list(range(o["b_kpe"], o["b_kpe"] + 32)))
    for j in range(2):
        blk("b_kva", j)
    for j in range(4):
        blk("b_g", j)
    for nm in ("c_q", "c_k", "c_g", "c_v"):
        for j in range(4):
            blk(nm, j)
    cols = np.asarray(cols)
    assert cols.size == 6912
    win_r = np.ascontiguousarray(w_in[:, :, cols])
    cq = []
    for h in range(4):
        cq.extend(range(h * 192, h * 192 + 128))
    for h in range(4):
        cq.extend(range(h * 192 + 128, h * 192 + 192))
    for h in range(4):
        cq.extend(list(range(h * 192 + 160, h * 192 + 192)) + list(range(h * 192 + 128, h * 192 + 160)))
    wuq_r = np.ascontiguousarray(inp["mla_w_uq"][:, :, np.asarray(cq)])
    ckv = []
    for h in range(4):
        ckv.extend(range(h * 256, h * 256 + 128))
    for h in range(4):
        ckv.extend(range(h * 256 + 128, h * 256 + 256))
    wukv_r = np.ascontiguousarray(inp["mla_w_ukv"][:, :, np.asarray(ckv)])
    vecs = np.zeros((L, 128, 64), f32)
    dww = np.zeros((L, 128, 124), f32)
    ng3 = np.zeros((L, 128, 48), f32)
    bada3 = np.zeros((L, 128, 144), f32)
    for l in range(L):
        acw = inp["a_conv_w"][l]
        for c in range(4):
            for k in range(3):
                vecs[l, :, c * 3 + k] = acw[k, c * 128:(c + 1) * 128]
        vecs[l, :, 12:16] = _fm(inp["d_dw_b"][l], 4)
        vecs[l, :, 16:20] = _fm(inp["d_ln_g"][l], 4)
        vecs[l, :, 20:24] = _fm(inp["d_ln_b"][l], 4)
        vecs[l, :, 24:28] = _fm(inp["d_pw_b"][l], 4)
        vecs[l, :, 28:31] = _fm(inp["mla_q_norm"][l], 3)
        vecs[l, :, 31:33] = _fm(inp["mla_kv_norm"][l], 2)
        dw = inp["d_dw_w"][l]
        for c in range(4):
            dww[l, :, c * 31:(c + 1) * 31] = dw[:, c * 128:(c + 1) * 128].T
        ng3[l] = np.repeat(_fm(inp["norm_g"][l], 16), 3, axis=1)
        bada3[l] = np.repeat(_fm(inp["b_ada"][l], 48), 3, axis=1)
    fng = _fm(inp["final_norm_g"], 16)
    t = np.arange(S)
    row = (t // GW).astype(f32)
    col = (t % GW).astype(f32)
    freqs = (np.float32(10000.0) ** (-(np.arange(16, dtype=f32) * np.float32(2.0) / np.float32(32)))).astype(f32)
    ang = np.concatenate([row[:, None] * freqs, col[:, None] * freqs], axis=-1).astype(f32)
    cos, sin = np.cos(ang).astype(f32), np.sin(ang).astype(f32)
    rope = np.zeros((2, 64, NT), f32)
    rope[0, :, S:] = 1.0
    rope[0, 0:32, :S] = cos.T
    rope[0, 32:64, :S] = cos.T
    rope[1, 0:32, :S] = -sin.T
    rope[1, 32:64, :S] = sin.T
    rpb = inp["na_rpb"]
    kc = np.arange(64)[:, None]
    qc = np.arange(64)[None, :]
    cstart = np.clip(qc - 8, 0, 48)
    valid = (kc >= cstart) & (kc < cstart + 16)
    dc = np.clip(kc - qc + 15, 0, 30)
    ub = np.full((L, 4, 128, 15, 64), NEG, f32)
    for l in range(L):
        for h in range(8):
            for j in range(15):
                dr = 7 - j
                tab = rpb[l, h, dr + 7][dc]
                ub[l, h // 2, (h % 2) * 64:(h % 2) * 64 + 64, j, :] = np.where(valid, tab, np.float32(NEG))
    ub = ub.reshape(L, 4, 128, 960)
    ident = np.eye(128, dtype=f32)
    onesbd = np.zeros((128, 128), f32)
    onesbd[0:64, 0:64] = 1.0
    onesbd[64:128, 64:128] = 1.0
    shared = dict(ng3=ng3, bada3=bada3, w_ada=np.ascontiguousarray(inp["w_ada"], f32), win_r=win_r,
                  w_out=np.ascontiguousarray(inp["w_out"], f32), wuq_r=wuq_r, wukv_r=wukv_r,
                  d_pw_w=np.ascontiguousarray(inp["d_pw_w"], f32), vecs=vecs, dww=dww, fng=fng, rope=rope, ub=ub,
                  ident=ident, onesbd=onesbd)
    in_maps = []
    for core in range(NCORES):
        bs = [core * NB + i for i in range(NB)]
        cT = np.zeros((128, 16, 3), f32)
        for i, bb in enumerate(bs):
            cT[:, :, i] = _fm(inp["c"][bb], 16)
        cT[:, :, 2] = _fm(inp["c_ctx"], 16)
        m = dict(shared)
        m["x"] = np.ascontiguousarray(inp["x"][bs[0]:bs[0] + NB], f32)
        m["ctx"] = np.ascontiguousarray(inp["ctx"][bs[0]:bs[0] + NB], f32)
        m["cT"] = cT.reshape(128, 48)
        in_maps.append(m)
    return in_maps


_NC_CACHE = {}


def kernel(**inputs):
    inp = {k: np.asarray(v) for k, v in inputs.items()}
    in_maps = _host_layout(inp)
    if "nc" not in _NC_CACHE:
        _NC_CACHE["nc"] = build_nc()
    nc = _NC_CACHE["nc"]
    res = run_bass_kernel_spmd(nc, in_maps, core_ids=list(range(NCORES)))
    outs = [np.asarray(r["out"]) for r in res.results]
    return np.concatenate(outs, axis=0).astype(np.float32)
```

```python
import numpy as np
import concourse.bass as bass
import concourse.mybir as mybir
from concourse.bass_utils import run_bass_kernel_spmd

F32 = mybir.dt.float32
BF16 = mybir.dt.bfloat16
AF = mybir.ActivationFunctionType
ALU = mybir.AluOpType

NCORES = 8
D = 2048
S = 2048
CT = 256
NT = S + CT
L = 2
NB = 2
GW = 64
EPS = 1e-6
MLA_SCALE = float(192 ** -0.5)
NA_SCALE = 0.125
NEG = -30000.0
TILES = [(0, 512), (512, 512), (1024, 512), (1536, 512), (2048, 256)]
UW = 2368
U_LAT = 15
U_CTX = 15 + 2048 + 30

R_QA, R_KPE, R_KSW, R_KVA, R_BG, R_CQ, R_CK, R_CG = 0, 384, 448, 512, 768, 1280, 1792, 2304
PT_ROWS = 2816

DEBUG = False


class Buf:
    __slots__ = ("w", "r", "pr", "strict", "disjoint")

    def __init__(self, strict=False, disjoint=False):
        self.w = {}
        self.r = {}
        self.pr = {}
        self.strict = strict
        self.disjoint = disjoint


def DBuf():
    return Buf(disjoint=True)


COMPUTE = ("pe", "act", "dve", "pool")
STRICT_SAME_ENGINE = True


class Prog:
    def __init__(self, ndma=20):
        self.ops = {e: [] for e in ("pe", "act", "dve", "pool", "sp")}
        self.cnt = {e: 0 for e in COMPUTE}
        self.seen = {e: {} for e in self.ops}
        self.issued = {}
        self.ndma = ndma
        self.dma_rr = {"sp": 0, "pool": 0}
        self.dma_val = {}

    def _waits(self, eng, reads, writes, extra):
        need = {}

        def add(d, strict):
            for k, v in d.items():
                if k == eng and not strict and (eng == "pe" or not STRICT_SAME_ENGINE):
                    continue
                if need.get(k, 0) < v:
                    need[k] = v

        for b in reads:
            add(b.w, b.strict)
        for b in writes:
            if not b.disjoint:
                add(b.w, b.strict)
            else:
                add(b.pr, b.strict)
            add(b.r, b.strict)
        for t in extra:
            if t is not None:
                if need.get(t[0], 0) < t[1]:
                    need[t[0]] = t[1]
        out = []
        sn = self.seen[eng]
        for k, v in need.items():
            if sn.get(k, 0) < v:
                sn[k] = v
                out.append((k, v))
        return out

    def _commit(self, tok, reads, writes):
        k, v = tok
        for b in reads:
            if b.r.get(k, 0) < v:
                b.r[k] = v
        for b in writes:
            if b.r:
                b.pr = b.r
                b.r = {}
                b.w = {}
            if b.w.get(k, 0) < v:
                b.w[k] = v
        if self.issued.get(k, 0) < v:
            self.issued[k] = v

    def op(self, eng, fn, reads=(), writes=(), extra=(), sig=True):
        wl = self._waits(eng, reads, writes, extra)
        tok = None
        if sig:
            self.cnt[eng] += 1
            tok = (eng, self.cnt[eng])
            self._commit(tok, reads, writes)
        self.ops[eng].append((fn, wl, 1 if sig else 0, eng))
        return tok

    def group(self, eng, fns, reads=(), writes=(), extra=()):
        wl = self._waits(eng, reads, writes, extra)
        self.cnt[eng] += 1
        tok = (eng, self.cnt[eng])
        self._commit(tok, reads, writes)
        n = len(fns)
        for i, fn in enumerate(fns):
            self.ops[eng].append((fn, wl if i == 0 else [], 1 if i == n - 1 else 0, eng))
        return tok

    def dma(self, q, out, in_, reads=(), writes=(), extra=()):
        i = self.dma_rr[q]
        self.dma_rr[q] = (i + 1) % self.ndma
        key = ("dma", q, i)
        prev = self.dma_val.get(key, 0)
        ex = list(extra)
        if prev:
            ex.append((key, prev))
        wl = self._waits(q, reads, writes, ex)
        val = prev + 16
        self.dma_val[key] = val
        tok = (key, val)
        self._commit(tok, reads, writes)
        self.ops[q].append((lambda e, o=out, i_=in_: e.dma_start(out=o, in_=i_), wl, 16, key))
        return tok

    def barrier(self):
        snap = dict(self.issued)
        for eng in self.ops:
            wl = []
            sn = self.seen[eng]
            for k, v in snap.items():
                if k == eng:
                    continue
                if sn.get(k, 0) < v:
                    sn[k] = v
                    wl.append((k, v))
            if wl:
                self.ops[eng].append((None, wl, 0, eng))

    def emit(self, nc, engines, sems):
        with nc.Block() as block:
            def run(name):
                def body(e):
                    for fn, wl, inc, key in self.ops[name]:
                        for k, v in wl:
                            e.wait_ge(sems[k], v)
                        if fn is None:
                            continue
                        ins = fn(e)
                        if inc:
                            ins.then_inc(sems[key], inc)
                return body
            block.tensor(run("pe"))
            block.scalar(run("act"))
            block.vector(run("dve"))
            block.gpsimd(run("pool"))
            block.sync(run("sp"))


class Arena:
    def __init__(self, t, nwords):
        self.t = t
        self.nbytes = nwords * 4
        self.off = 0

    def reset(self, off=0):
        self.off = off

    def _take(self, nbytes):
        nb = (nbytes + 63) // 64 * 64
        assert self.off + nb <= self.nbytes, ("arena overflow", self.off, nb, self.nbytes)
        o = self.off
        self.off += nb
        return o

    def f(self, n, parts=128):
        o = self._take(n * 4)
        return self.t[0:parts, o // 4:o // 4 + n]

    def b(self, n, parts=128):
        assert n % 2 == 0
        o = self._take(n * 2)
        return self.t[0:parts, o // 4:o // 4 + n // 2].bitcast(BF16)


def v3(ap, b):
    return ap.rearrange("p (a b) -> p a b", b=b)


def na_tables():
    rows = S // GW
    rstart = [min(max(r - 4, 0), rows - 8) for r in range(rows)]
    Q = {}
    for kr in range(rows):
        qs = [qr for qr in range(rows) if rstart[qr] <= kr <= rstart[qr] + 7]
        assert qs == list(range(qs[0], qs[-1] + 1))
        Q[kr] = (qs[0], qs[-1] + 1)
    return Q


def build_nc():
    nc = bass.Bass("TRN2", target_bir_lowering=False)
    P = Prog()

    def din(name, shape, dt=F32):
        return nc.dram_tensor(name, list(shape), dt, kind="ExternalInput").ap()

    x_in = din("x", [NB, S, D])
    ctx_in = din("ctx", [NB, CT, D])
    cT_in = din("cT", [128, 16 * 3])
    ng3_in = din("ng3", [L, 128, 16 * 3])
    bada3_in = din("bada3", [L, 128, 48 * 3])
    wada_in = din("w_ada", [L, D, 3 * D])
    win_in = din("win_r", [L, D, 6912])
    wout_in = din("w_out", [L, D, D])
    wuq_in = din("wuq_r", [L, 384, 1024])
    wukv_in = din("wukv_r", [L, 256, 1024])
    wpw_in = din("d_pw_w", [L, 512, 512])
    vec_in = din("vecs", [L, 128, 64])
    dww_in = din("dww", [L, 128, 4 * 31])
    fng_in = din("fng", [128, 16])
    rope_in = din("rope", [2, 64, NT])
    ub_in = din("ub", [L, 4, 128, 960])
    ident_in = din("ident", [128, 128])
    onesbd_in = din("onesbd", [128, 128])
    out = nc.dram_tensor("out", [NB, S, D], F32, kind="ExternalOutput").ap()

    skind = "ExternalOutput" if DEBUG else "Internal"
    XT = [nc.dram_tensor(f"XT{b}", [D, NT], F32, kind=skind).ap() for b in range(NB)]
    pT = [nc.dram_tensor(f"pT{b}", [PT_ROWS, NT], BF16, kind=skind).ap() for b in range(NB)]
    vtok = [nc.dram_tensor(f"vtok{b}", [8, 64, 36, 64], BF16, kind=skind).ap() for b in range(NB)]
    yT = [nc.dram_tensor(f"yT{b}", [D, NT], BF16, kind=skind).ap() for b in range(NB)]
    hTd = [nc.dram_tensor(f"hTd{b}", [D, NT], BF16, kind=skind).ap() for b in range(NB)]
    B_hTd = [DBuf() for _ in range(NB)]
    KBD = [nc.dram_tensor(f"KBD{b}", [128, 4 * 36 * 128], BF16, kind=skind).ap() for b in range(NB)]
    VBD = [nc.dram_tensor(f"VBD{b}", [128, 4 * 36 * 128], BF16, kind=skind).ap() for b in range(NB)]
    B_KBD = [DBuf() for _ in range(NB)]
    B_VBD = [DBuf() for _ in range(NB)]
    B_XT = [DBuf() for _ in range(NB)]
    B_pT = [DBuf() for _ in range(NB)]
    B_vtok = [DBuf() for _ in range(NB)]
    B_yT = [DBuf() for _ in range(NB)]

    NW = 51200
    from contextlib import ExitStack
    es = ExitStack()
    ar_t = es.enter_context(nc.sbuf_tensor("arena", [128, NW], F32))
    cst_t = es.enter_context(nc.sbuf_tensor("cst_f", [128, 1400], F32))
    cstb_t = es.enter_context(nc.sbuf_tensor("cst_b", [128, 600], BF16))
    banks = [es.enter_context(nc.psum_tensor(f"ps{i}", [128, 512], F32)) for i in range(8)]
    BK = [Buf() for _ in range(8)]
    AR = Arena(ar_t, NW)

    keys = list(COMPUTE) + [("dma", q, i) for q in ("sp", "pool") for i in range(P.ndma)]
    sems = {}
    for k in keys:
        nm = k if isinstance(k, str) else f"d_{k[1]}_{k[2]}"
        sems[k] = es.enter_context(nc.semaphore("s_" + nm))

    class _CA:
        def __init__(self, t):
            self.t = t
            self.off = 0

        def alloc(self, n):
            ap = self.t[:, self.off:self.off + n]
            self.off += (n + 15) // 16 * 16
            return ap
    CF = _CA(cst_t)
    CB = _CA(cstb_t)
    ident_f = CF.alloc(128)
    eps_t = CF.alloc(1)
    cT = CF.alloc(48)
    modv_l = [CF.alloc(144) for _ in range(L)]
    gsv_l = [CF.alloc(48) for _ in range(L)]
    ng3_l = [CF.alloc(48) for _ in range(L)]
    bada3_l = [CF.alloc(144) for _ in range(L)]
    vecs = CF.alloc(64)
    dww = CF.alloc(124)
    fng = CF.alloc(16)
    onesbd_f = CF.alloc(128)
    ident_b = CB.alloc(128)
    ones_b = CB.alloc(128)
    onesbd_b = CB.alloc(128)
    scT = CB.alloc(48)
    B_const = Buf(strict=True)
    B_mod_l = [Buf(strict=True) for _ in range(L)]
    B_adl = Buf(strict=True)
    B_lay = Buf(strict=True)

    P.dma("sp", ident_f, ident_in, writes=[B_const])
    P.dma("sp", onesbd_f, onesbd_in, writes=[B_const])
    P.dma("sp", cT, cT_in, writes=[B_const])
    P.dma("sp", fng, fng_in, writes=[B_const])
    P.op("dve", lambda e: e.memset(eps_t, EPS), writes=[B_const])
    P.op("dve", lambda e: e.memset(ones_b, 1.0), writes=[B_const])
    P.op("dve", lambda e: e.tensor_copy(out=ident_b, in_=ident_f), reads=[B_const], writes=[B_const])
    P.op("dve", lambda e: e.tensor_copy(out=onesbd_b, in_=onesbd_f), reads=[B_const], writes=[B_const])
    P.op("act", lambda e: e.activation(out=scT, in_=cT, func=AF.Silu), reads=[B_const], writes=[B_const])

    def vcol(i):
        return vecs[:, i:i + 1]
    V_ACW, V_DWB, V_LNG, V_LNB, V_PWB, V_QNG, V_KVG = 0, 12, 16, 20, 24, 28, 31

    cp_rr = [0]
    act_only = [False]

    def evac_copy(out_ap, in_ap, reads, writes, scale=None):
        cp_rr[0] ^= 1
        if cp_rr[0] or act_only[0]:
            if scale is None:
                return P.op("act", lambda e: e.activation(out=out_ap, in_=in_ap, func=AF.Copy), reads=reads, writes=writes)
            return P.op("act", lambda e: e.activation(out=out_ap, in_=in_ap, func=AF.Copy, scale=scale), reads=reads, writes=writes)
        if scale is None:
            return P.op("dve", lambda e: e.tensor_copy(out=out_ap, in_=in_ap), reads=reads, writes=writes)
        return P.op("dve", lambda e: e.tensor_scalar(out=out_ap, in0=in_ap, scalar1=scale, scalar2=None, op0=ALU.mult), reads=reads, writes=writes)

    def mm(out_ap, lhsT, rhs, start, stop):
        return lambda e: e.matmul(out_ap, lhsT=lhsT, rhs=rhs, start=start, stop=stop)

    bk_rr = [0]

    def next_bank(lo=0, hi=6):
        i = lo + bk_rr[0] % (hi - lo)
        bk_rr[0] += 1
        return i


    def ACT(out_, in_, func, reads, writes, bias=None, scale=None):
        kw = {}
        if bias is not None:
            kw["bias"] = bias
        if scale is not None:
            kw["scale"] = scale
        return P.op("act", lambda e: e.activation(out=out_, in_=in_, func=func, **kw), reads=reads, writes=writes)

    def TT(eng, out_, in0, in1, op, reads, writes):
        return P.op(eng, lambda e: e.tensor_tensor(out=out_, in0=in0, in1=in1, op=op), reads=reads, writes=writes)

    def TS(eng, out_, in0, s1, op0, reads, writes):
        return P.op(eng, lambda e: e.tensor_scalar(out=out_, in0=in0, scalar1=s1, scalar2=None, op0=op0), reads=reads, writes=writes)

    def STT(out_, in0, scalar, in1, op0, op1, reads, writes):
        return P.op("dve", lambda e: e.scalar_tensor_tensor(out=out_, in0=in0, scalar=scalar, in1=in1, op0=op0, op1=op1),
                    reads=reads, writes=writes)

    def RCP(out_, in_, reads, writes):
        return P.op("dve", lambda e: e.reciprocal(out=out_, in_=in_), reads=reads, writes=writes)

    def MSET(eng, ap, val, writes):
        return P.op(eng, lambda e: e.memset(ap, val), writes=writes)

    def TRN(out_, in_):
        return lambda e: e.transpose(out_, in_, ident_f)

    AR.reset()
    wb = [v3(AR.b(16 * 512), 512) for _ in range(2)]
    B_wb = [Buf(), Buf()]
    gi_ = 0
    for l in range(L):
        P.dma("sp", ng3_l[l], ng3_in[l], writes=[B_adl])
        P.dma("sp", bada3_l[l], bada3_in[l], writes=[B_adl])
        mbank = 6 + l
        wadav = wada_in[l].rearrange("(c p) n -> p c n", p=128)
        for g in range(12):
            s = gi_ % 2
            gi_ += 1
            P.dma("pool", wb[s], wadav[:, :, g * 512:(g + 1) * 512], writes=[B_wb[s]])
            for j in range(4):
                oc = g * 4 + j
                fns = [mm(banks[mbank][:, oc * 4:oc * 4 + 3], wb[s][:, kc, j * 128:(j + 1) * 128], scT[:, kc * 3:kc * 3 + 3],
                          kc == 0, kc == 15) for kc in range(16)]
                P.group("pe", fns, reads=[B_wb[s], B_const], writes=[BK[mbank]])
        TT("dve", v3(modv_l[l], 3), v3(banks[mbank][:, 0:192], 4)[:, :, 0:3], v3(bada3_l[l], 3), ALU.add, [BK[mbank], B_adl], [B_mod_l[l]])
        TS("dve", gsv_l[l], modv_l[l][:, 48:96], 1.0, ALU.add, [B_mod_l[l]], [B_mod_l[l]])
        TT("dve", gsv_l[l], gsv_l[l], ng3_l[l], ALU.mult, [B_mod_l[l], B_adl], [B_mod_l[l]])
    P.barrier()

    def shv(c, mi, l_):
        return modv_l[l_][:, c * 3 + mi:c * 3 + mi + 1]

    def gsc(c, mi, l_):
        return gsv_l[l_][:, c * 3 + mi:c * 3 + mi + 1]

    def gtv(c, mi, l_):
        return modv_l[l_][:, 96 + c * 3 + mi:96 + c * 3 + mi + 1]

    ntmp = [0]

    def norm_squares(xs, B_xs, n, sq, B_sq, sq_on_act):
        for c in range(16):
            if sq_on_act or c % 4 == 3:
                ACT(sq[:, c, 0:n], xs[:, c, 0:n], AF.Square, [B_xs], [B_sq])
            else:
                TT("dve", sq[:, c, 0:n], xs[:, c, 0:n], xs[:, c, 0:n], ALU.mult, [B_xs], [B_sq])

    def norm_stats(n, sq, B_sq, rs, B_rs):
        bi = next_bank()
        P.group("pe", [mm(banks[bi][:, 0:n], ones_b, sq[:, c, 0:n], c == 0, c == 15) for c in range(16)],
                reads=[B_sq, B_const], writes=[BK[bi]])
        ACT(rs[:, 0:n], banks[bi][:, 0:n], AF.Sqrt, [BK[bi], B_const], [B_rs], bias=eps_t, scale=1.0 / D)
        RCP(rs[:, 0:n], rs[:, 0:n], [B_rs], [B_rs])

    def norm_chunk(xs, B_xs, n, c, mi, l_, rs, B_rs, tmps, B_tmps, ht, B_ht):
        s_ = ntmp[0] % len(tmps)
        ntmp[0] += 1
        TT("dve", tmps[s_][:, 0:n], xs[:, c, 0:n], rs[:, 0:n], ALU.mult, [B_xs, B_rs], [B_tmps[s_]])
        ACT(ht[:, c, 0:n], tmps[s_][:, 0:n], AF.Identity, [B_tmps[s_], B_mod_l[l_]], [B_ht], bias=shv(c, mi, l_), scale=gsc(c, mi, l_))

    def norm_store(n, b_, t0, ht, B_ht):
        P.dma("sp", hTd[b_].rearrange("(c p) t -> p c t", p=128)[:, :, t0:t0 + n], ht[:, :, 0:n], reads=[B_ht], writes=[B_hTd[b_]])

    def emit_norm_tile(xs, B_xs, n, b_, t0, mi, l_, sq, B_sq, rs, B_rs, tmps, B_tmps, ht, B_ht, sq_on_act):
        norm_squares(xs, B_xs, n, sq, B_sq, sq_on_act)
        norm_stats(n, sq, B_sq, rs, B_rs)
        for c in range(16):
            norm_chunk(xs, B_xs, n, c, mi, l_, rs, B_rs, tmps, B_tmps, ht, B_ht)
        norm_store(n, b_, t0, ht, B_ht)

    AR.reset()
    zt = AR.b(4 * 36 * 128)
    B_zt = Buf()
    MSET("dve", zt, 0.0, [B_zt])
    for b in range(NB):
        P.dma("sp", KBD[b], zt, reads=[B_zt], writes=[B_KBD[b]])
        P.dma("sp", VBD[b], zt, reads=[B_zt], writes=[B_VBD[b]])
    xin = [AR.f(2048) for _ in range(4)]
    xst = [v3(AR.f(16 * 512), 512) for _ in range(2)]
    B_xin = [Buf() for _ in range(4)]
    B_xst = [DBuf(), DBuf()]
    p_sq = v3(AR.b(16 * 512), 512)
    p_ht = [v3(AR.b(16 * 512), 512) for _ in range(2)]
    p_rs = [AR.f(512) for _ in range(2)]
    p_tmp = [AR.f(512) for _ in range(4)]
    B_psq_ = DBuf()
    B_prs_ = [Buf(), Buf()]
    pend = [None]

    def pro_back():
        if pend[0] is None:
            return
        (xs_, B_xs_, n_, b_, t0_, mi_, rs_, B_rs_, ht_, B_ht_) = pend[0]
        for c in range(16):
            norm_chunk(xs_, B_xs_, n_, c, mi_, 0, rs_, B_rs_, p_tmp, B_ptmp, ht_, B_ht_)
        norm_store(n_, b_, t0_, ht_, B_ht_)
        pend[0] = None
    B_pht = [DBuf(), DBuf()]
    B_ptmp = [Buf() for _ in range(4)]
    groups = [(b_, g0, ntt) for b_ in range(NB) for (g0, ntt) in ((0, 4), (4, 4), (8, 4), (12, 4), (16, 2))]

    def pro_loads(gidx):
        b_, g0, ntt = groups[gidx]
        for q in range(ntt):
            tt = g0 + q
            src = x_in[b_, tt * 128:(tt + 1) * 128, :] if tt < 16 else ctx_in[b_, (tt - 16) * 128:(tt - 15) * 128, :]
            P.dma("sp", xin[q], src, writes=[B_xin[q]])

    pro_loads(0)
    for gidx, (b, g0, ntt) in enumerate(groups):
        XTv0 = XT[b].rearrange("(c p) t -> p c t", p=128)
        sg = gidx % 2
        for q in range(ntt):
            for g in range(4):
                bi = next_bank(0, 8)
                fns = [TRN(banks[bi][:, j * 128:(j + 1) * 128], xin[q][:, (g * 4 + j) * 128:(g * 4 + j + 1) * 128]) for j in range(4)]
                P.group("pe", fns, reads=[B_xin[q], B_const], writes=[BK[bi]])
                evac_copy(xst[sg][:, g * 4:(g + 1) * 4, q * 128:(q + 1) * 128], v3(banks[bi][:, :], 128), [BK[bi]], [B_xst[sg]])
        if gidx + 1 < len(groups):
            pro_loads(gidx + 1)
        P.dma("sp", XTv0[:, :, g0 * 128:(g0 + ntt) * 128], xst[sg][:, :, 0:ntt * 128], reads=[B_xst[sg]], writes=[B_XT[b]])
        norm_squares(xst[sg], B_xst[sg], ntt * 128, p_sq, B_psq_, False)
        norm_stats(ntt * 128, p_sq, B_psq_, p_rs[sg], B_prs_[sg])
        pro_back()
        pend[0] = (xst[sg], B_xst[sg], ntt * 128, b, g0 * 128, (b if g0 < 16 else 2), p_rs[sg], B_prs_[sg], p_ht[sg], B_pht[sg])
    pro_back()
    P.barrier()

    Qr = na_tables()

    for l in range(L):
        last = (l == L - 1)
        AR.reset()
        P.dma("sp", vecs, vec_in[l], writes=[B_lay])
        P.dma("sp", dww, dww_in[l], writes=[B_lay])
        B_mod = B_mod_l[l]
        P.barrier()

        for b in range(NB):
            XTv = XT[b].rearrange("(c p) t -> p c t", p=128)
            yTv = yT[b].rearrange("(c p) t -> p c t", p=128)
            pv = pT[b].rearrange("(c p) t -> p c t", p=128)
            AR.reset()
            u_pad = v3(AR.b(4 * UW), UW)
            dg = v3(AR.b(4 * NT), NT)
            HT_OFF = AR.off
            hT = v3(AR.b(16 * NT), NT)
            BF_BASE = AR.off
            B_u = Buf()
            B_dg = Buf()
            B_hT = [DBuf() for _ in TILES]
            hTdv = hTd[b].rearrange("(c p) t -> p c t", p=128)
            for ti, (t0, n) in enumerate(TILES):
                P.dma("sp", hT[:, :, t0:t0 + n], hTdv[:, :, t0:t0 + n], reads=[B_hTd[b]], writes=[B_hT[ti]])

            AR.reset(BF_BASE)
            wbuf = [v3(AR.b(16 * 256), 256) for _ in range(2)]
            B_wbuf = [Buf(), Buf()]
            stage = [AR.b(NT) for _ in range(3)]
            B_stage = [DBuf() for _ in range(3)]
            vst = [v3(AR.b(18 * 256), 256) for _ in range(2)]
            B_vst = [DBuf(), DBuf()]
            T = [AR.f(NT) for _ in range(4)]
            B_T = [Buf() for _ in range(4)]
            st_rr = [0]
            act_only[0] = True
            MSET("pool", u_pad, 0.0, [B_u])

            winv = win_in[l].rearrange("(c p) n -> p c n", p=128)
            grp = [0]

            def load_group(g):
                s_ = grp[0] % 2
                P.dma("pool", wbuf[s_], winv[:, :, g * 256:(g + 1) * 256], writes=[B_wbuf[s_]])
                grp[0] += 1
                return s_

            ntl = [len(TILES)]

            def proj_block(s_, o, m, handler, kv=False):
                tiles = TILES if (kv or not last) else TILES[:4]
                ntl[0] = len(tiles)
                for ti, (t0, n) in enumerate(tiles):
                    bi = next_bank()
                    P.group("pe", [mm(banks[bi][0:m, 0:n], wbuf[s_][:, kc, o:o + m], hT[:, kc, t0:t0 + n], kc == 0, kc == 15)
                                   for kc in range(16)], reads=[B_wbuf[s_], B_hT[ti]], writes=[BK[bi]])
                    handler(banks[bi][0:m, 0:n], ti, t0, n, BK[bi])

            def h_copy_T(k):
                def h(ps, ti, t0, n, bk):
                    evac_copy(T[k][:, t0:t0 + n], ps, [bk], [B_T[k]])
                return h

            def h_act_T(k, func):
                def h(ps, ti, t0, n, bk):
                    ACT(T[k][:, t0:t0 + n], ps, func, [bk], [B_T[k]])
                return h

            def h_act_ap(dst3, j, func, bufw):
                def h(ps, ti, t0, n, bk):
                    ACT(dst3[:, j, t0:t0 + n], ps, func, [bk], [bufw])
                return h

            def h_dram(row0, m, func=None):
                s_ = st_rr[0] % 3
                st_rr[0] += 1

                def h(ps, ti, t0, n, bk):
                    if func is None:
                        evac_copy(stage[s_][0:m, t0:t0 + n], ps, [bk], [B_stage[s_]])
                    else:
                        ACT(stage[s_][0:m, t0:t0 + n], ps, func, [bk], [B_stage[s_]])
                    if ti == ntl[0] - 1:
                        P.dma("sp", pT[b][row0:row0 + m, :], stage[s_][0:m, :], reads=[B_stage[s_]], writes=[B_pT[b]])
                return h

            KBDv = KBD[b].rearrange("p (h r k) -> p h r k", h=4, r=36)
            VBDv = VBD[b].rearrange("p (h r k) -> p h r k", h=4, r=36)

            def h_kbd(hp):
                s_ = st_rr[0] % 3
                st_rr[0] += 1

                def h(ps, ti, t0, n, bk):
                    evac_copy(stage[s_][:, t0:t0 + n], ps, [bk], [B_stage[s_]])
                    if ti == ntl[0] - 1:
                        for lo in (0, 64):
                            P.dma("sp", KBDv[lo:lo + 64, hp, :, lo:lo + 64], stage[s_][lo:lo + 64, :].rearrange("p (r k) -> p r k", k=64),
                                  reads=[B_stage[s_]], writes=[B_KBD[b]])
                return h

            def unit_ck(g, hp0):
                s = load_group(g)
                proj_block(s, 0, 128, h_kbd(hp0), kv=True)
                proj_block(s, 128, 128, h_kbd(hp0 + 1), kv=True)

            def unit_A(j):
                s = load_group(2 * j)
                proj_block(s, 0, 128, h_copy_T(0))
                proj_block(s, 128, 128, h_copy_T(1))
                s = load_group(2 * j + 1)
                proj_block(s, 0, 128, h_copy_T(2))
                proj_block(s, 128, 128, h_act_T(3, AF.Silu))
                TT("dve", T[0], T[0], T[1], ALU.mult, [B_T[1]], [B_T[0]])
                w0, w1, w2 = (vcol(V_ACW + j * 3 + k) for k in range(3))
                TS("dve", T[1], T[0], w1, ALU.mult, [B_T[0], B_lay], [B_T[1]])
                for (a, n_) in ((0, S), (S, CT)):
                    STT(T[1][:, a + 1:a + n_], T[0][:, a:a + n_ - 1], w0, T[1][:, a + 1:a + n_], ALU.mult, ALU.add, [B_T[0], B_lay], [B_T[1]])
                    STT(T[1][:, a:a + n_ - 1], T[0][:, a + 1:a + n_], w2, T[1][:, a:a + n_ - 1], ALU.mult, ALU.add, [B_T[0], B_lay], [B_T[1]])
                TT("dve", T[2], T[2], T[3], ALU.mult, [B_T[3]], [B_T[2]])
                ss = st_rr[0] % 3
                st_rr[0] += 1
                TT("dve", stage[ss], T[1], T[2], ALU.mult, [B_T[1], B_T[2]], [B_stage[ss]])
                P.dma("sp", yT[b][j * 128:(j + 1) * 128, :], stage[ss], reads=[B_stage[ss]], writes=[B_yT[b]])

            def unit_D(j):
                s = load_group(8 + j)
                proj_block(s, 0, 128, h_copy_T(0))
                proj_block(s, 128, 128, h_act_T(1, AF.Sigmoid))
                TT("dve", u_pad[:, j, U_LAT:U_LAT + S], T[0][:, 0:S], T[1][:, 0:S], ALU.mult, [B_T[0], B_T[1]], [B_u])
                TT("dve", u_pad[:, j, U_CTX:U_CTX + CT], T[0][:, S:NT], T[1][:, S:NT], ALU.mult, [B_T[0], B_T[1]], [B_u])

            def unit_dg(jj):
                s = load_group(12 + jj)
                for q in range(2):
                    proj_block(s, q * 128, 128, h_act_ap(dg, jj * 2 + q, AF.Silu, B_dg))

            def unit_pair(g, row0, func=None):
                s = load_group(g)
                proj_block(s, 0, 128, h_dram(row0, 128, func), kv=(row0 == R_KVA))
                proj_block(s, 128, 128, h_dram(row0 + 128, 128, func), kv=(row0 == R_KVA))

            def unit_qa2():
                s = load_group(15)
                proj_block(s, 0, 128, h_dram(R_QA + 256, 128))
                proj_block(s, 128, 64, h_dram(R_KPE, 64), kv=True)
                proj_block(s, 192, 64, h_dram(R_KSW, 64), kv=True)

            def unit_v(half):
                s = load_group(25 + half)
                for tt in range(18):
                    bi = next_bank()
                    P.group("pe", [mm(banks[bi][:, 0:256], hT[:, kc, tt * 128:(tt + 1) * 128], wbuf[s][:, kc, :], kc == 0, kc == 15)
                                   for kc in range(16)], reads=[B_wbuf[s], B_hT[min(tt // 4, 4)]], writes=[BK[bi]])
                    evac_copy(vst[half][:, tt, :], banks[bi][:, 0:256], [BK[bi]], [B_vst[half]])
                for rl in range(2):
                    for hl in range(4):
                        hh = 4 * half + hl
                        hp_, hf = hh // 2, hh % 2
                        dst = VBDv[hf * 64:(hf + 1) * 64, hp_, :, hf * 64:(hf + 1) * 64].rearrange("k (t r) d -> r k t d", r=2)[rl]
                        src = vst[half][rl * 64:(rl + 1) * 64, :, hl * 64:(hl + 1) * 64]
                        P.dma("sp", dst, src, reads=[B_vst[half]], writes=[B_VBD[b]])

            unit_A(0)
            unit_pair(14, R_QA)
            unit_qa2()
            unit_A(1)
            unit_pair(16, R_KVA)
            unit_pair(17, R_BG, AF.Silu)
            unit_A(2)
            unit_pair(18, R_BG + 256, AF.Silu)
            unit_pair(19, R_CQ)
            unit_A(3)
            unit_pair(20, R_CQ + 256)
            unit_ck(21, 0)
            unit_D(0)
            unit_ck(22, 2)
            unit_D(1)
            unit_pair(23, R_CG, AF.Silu)
            unit_D(2)
            unit_pair(24, R_CG + 256, AF.Silu)
            unit_D(3)
            unit_dg(0)
            unit_dg(1)
            unit_v(0)
            unit_v(1)
            assert grp[0] == 27
            act_only[0] = False
            P.barrier()

            MLA_PF = 147456
            AR.reset(MLA_PF)
            qa = v3(AR.b(3 * NT), NT)
            kva = v3(AR.b(2 * NT), NT)
            kpe = AR.b(NT)
            ksw = AR.b(NT)
            CC = AR.f(NT)
            SSn = AR.f(NT)
            B_ld, B_w, B_bg = Buf(), Buf(), Buf()
            B_cqt = [Buf() for _ in TILES]
            B_ckvt = [Buf() for _ in TILES]
            P.dma("sp", qa, pv[:, 0:3, :], reads=[B_pT[b]], writes=B_cqt)
            P.dma("sp", kva, pv[:, 4:6, :], reads=[B_pT[b]], writes=B_ckvt)
            P.dma("sp", kpe[0:64, :], pT[b][R_KPE:R_KPE + 64, :], reads=[B_pT[b]], writes=[B_ld])
            P.dma("sp", ksw[0:64, :], pT[b][R_KSW:R_KSW + 64, :], reads=[B_pT[b]], writes=[B_ld])
            P.dma("sp", CC[0:64, :], rope_in[0], writes=[B_ld])
            P.dma("sp", SSn[0:64, :], rope_in[1], writes=[B_ld])
            pt1 = [AR.f(512) for _ in range(2)]
            pt2 = AR.f(512)
            B_pt1 = [Buf(), Buf()]
            B_pt2 = Buf()
            assert AR.off <= NW * 4, AR.off
            B_kpe = Buf()
            MSET("pool", kpe[64:128, :], 0.0, [B_kpe])
            AR.reset(HT_OFF)
            diag = AR.b(4 * 31 * 128)
            wpw = v3(AR.b(4 * 512), 512)
            vb2 = [v3(AR.b(4 * 512), 512)] * 2
            sqb2 = [v3(AR.b(4 * 512), 512)] * 2
            zb2 = [v3(AR.b(4 * 512), 512) for _ in range(2)]
            ys = [v3(AR.b(4 * 512), 512) for _ in range(2)]
            B_diagc = [Buf() for _ in range(4)]
            B_wpw = Buf()
            B_vb2, B_sqb2, B_zb2 = [Buf()] * 2, [Buf()] * 2, [Buf(), Buf()]
            B_ys = [Buf(), Buf()]
            vf2 = [v3(AR.f(4 * 512), 512)] * 2
            B_vf2 = [Buf()] * 2
            st2 = [[AR.f(512) for _ in range(3)] + [None] for _ in range(2)]
            psq = [v3(AR.b(3 * 512), 512) for _ in range(2)]
            B_psq = [DBuf(), DBuf()]
            prqs = [AR.f(512) for _ in range(4)]
            B_prqs = [Buf() for _ in range(4)]
            B_mean2, B_var2 = [Buf(), Buf()], [Buf(), Buf()]
            dt2 = [AR.f(512) for _ in range(2)]
            B_dt2 = [Buf(), Buf()]
            P.dma("pool", wpw, wpw_in[l].rearrange("(c p) n -> p c n", p=128), writes=[B_wpw])
            for c in range(4):
                for k in range(31):
                    TS("dve", diag[:, (c * 31 + k) * 128:(c * 31 + k + 1) * 128], ident_f, dww[:, c * 31 + k:c * 31 + k + 1], ALU.mult,
                       [B_const, B_lay], [B_diagc[c]])
            W_PF = AR.off
            wuq = v3(AR.b(3 * 1024), 1024)
            wukv = v3(AR.b(2 * 1024), 1024)
            P.dma("pool", wuq, wuq_in[l].rearrange("(c p) n -> p c n", p=128), writes=[B_w])
            P.dma("pool", wukv, wukv_in[l].rearrange("(c p) n -> p c n", p=128), writes=[B_w])
            assert AR.off <= MLA_PF, AR.off
            dtiles = [(t0, n, t0) for (t0, n) in TILES[:4]] + ([] if last else [(S, CT, U_CTX - 15)])
            dti = [0]

            CB_ = [0, 1, 2, 3]

            def dconv_conv_pe(di):
                t0, n, uo = dtiles[di]
                for c in range(4):
                    bi = CB_[c]
                    P.group("pe", [mm(banks[bi][:, 0:n], diag[:, (c * 31 + k) * 128:(c * 31 + k + 1) * 128], u_pad[:, c, uo + k:uo + k + n], k == 0, k == 30)
                                   for k in range(31)], reads=[B_diagc[c], B_u], writes=[BK[bi]])

            def dconv_conv_evac(di):
                t0, n, uo = dtiles[di]
                p = di % 2
                vf, vb, sqb = vf2[p], vb2[p], sqb2[p]
                for c in range(4):
                    bi = CB_[c]
                    ACT(vf[:, c, 0:n], banks[bi][:, 0:n], AF.Identity, [BK[bi], B_lay], [B_vf2[p]], bias=vcol(V_DWB + c))
                    ACT(sqb[:, c, 0:n], banks[bi][:, 0:n], AF.Square, [BK[bi], B_lay], [B_sqb2[p]], bias=vcol(V_DWB + c))
                    P.op("dve", (lambda o, i_: (lambda e: e.tensor_copy(out=o, in_=i_)))(vb[:, c, 0:n], vf[:, c, 0:n]), reads=[B_vf2[p]], writes=[B_vb2[p]])

            def dconv_stats_pe(di):
                t0, n, uo = dtiles[di]
                p = di % 2
                P.group("pe", [mm(banks[4][:, 0:n], ones_b, vb2[p][:, c, 0:n], c == 0, c == 3) for c in range(4)], reads=[B_vb2[p], B_const], writes=[BK[4]])
                P.group("pe", [mm(banks[5][:, 0:n], ones_b, sqb2[p][:, c, 0:n], c == 0, c == 3) for c in range(4)], reads=[B_sqb2[p], B_const], writes=[BK[5]])

            def dconv_ln(di):
                t0, n, uo = dtiles[di]
                p = di % 2
                vf, zb = vf2[p], zb2[p]
                B_vf, B_zb = B_vf2[p], B_zb2[p]
                mean, msq, var, _unused = st2[p]
                B_mean, B_var = B_mean2[p], B_var2[p]
                TS("dve", mean[:, 0:n], banks[4][:, 0:n], 1.0 / 512, ALU.mult, [BK[4]], [B_mean])
                TT("dve", msq[:, 0:n], mean[:, 0:n], mean[:, 0:n], ALU.mult, [B_mean], [B_var])
                STT(var[:, 0:n], banks[5][:, 0:n], 1.0 / 512, msq[:, 0:n], ALU.mult, ALU.subtract, [BK[5], B_var], [B_var])
                ACT(var[:, 0:n], var[:, 0:n], AF.Sqrt, [B_var, B_const], [B_var], bias=eps_t)
                RCP(var[:, 0:n], var[:, 0:n], [B_var], [B_var])
                for c in range(4):
                    q = dti[0] % 2
                    dti[0] += 1
                    TT("dve", dt2[q][:, 0:n], vf[:, c, 0:n], mean[:, 0:n], ALU.subtract, [B_vf, B_mean], [B_dt2[q]])
                    TT("pool", dt2[q][:, 0:n], dt2[q][:, 0:n], var[:, 0:n], ALU.mult, [B_var], [B_dt2[q]])
                    ACT(zb[:, c, 0:n], dt2[q][:, 0:n], AF.Silu, [B_dt2[q], B_lay], [B_zb], bias=vcol(V_LNB + c), scale=vcol(V_LNG + c))

            def dconv_pw(di):
                t0, n, uo = dtiles[di]
                p = di % 2
                ysb, zb = ys[p], zb2[p]
                for co in range(4):
                    bi = 6 + co % 2
                    P.group("pe", [mm(banks[bi][:, 0:n], wpw[:, ci, co * 128:(co + 1) * 128], zb[:, ci, 0:n], ci == 0, ci == 3) for ci in range(4)],
                            reads=[B_wpw, B_zb2[p]], writes=[BK[bi]])
                    STT(ysb[:, co, 0:n], banks[bi][:, 0:n], vcol(V_PWB + co), dg[:, co, t0:t0 + n], ALU.add, ALU.mult,
                        [BK[bi], B_dg, B_lay], [B_ys[p]])
                P.dma("sp", yTv[:, 12:16, t0:t0 + n], ysb[:, :, 0:n], reads=[B_ys[p]], writes=[B_yT[b]])

            ptr = [0]

            def prep_srcs(ti):
                lst = []
                for k, (src, nchunk, Bs, gidx, dim) in enumerate(((qa, 3, B_cqt[ti], V_QNG, 384.0), (kva, 2, B_ckvt[ti], V_KVG, 256.0))):
                    if last and ti == 4 and k == 0:
                        continue
                    lst.append((k, src, nchunk, Bs, gidx, dim))
                return lst

            def prep_sq(ti):
                t0, n = TILES[ti]
                for (k, src, nchunk, Bs, gidx, dim) in prep_srcs(ti):
                    for c in range(nchunk):
                        ACT(psq[k][:, c, 0:n], src[:, c, t0:t0 + n], AF.Square, [Bs], [B_psq[k]])

            def prep_mm(ti):
                t0, n = TILES[ti]
                for (k, src, nchunk, Bs, gidx, dim) in prep_srcs(ti):
                    bi = 6 + k
                    rq, B_rq = prqs[(ti % 2) * 2 + k], B_prqs[(ti % 2) * 2 + k]
                    P.group("pe", [mm(banks[bi][:, 0:n], ones_b, psq[k][:, c, 0:n], c == 0, c == nchunk - 1) for c in range(nchunk)],
                            reads=[B_psq[k], B_const], writes=[BK[bi]])
                    ACT(rq[:, 0:n], banks[bi][:, 0:n], AF.Sqrt, [BK[bi], B_const], [B_rq], bias=eps_t, scale=1.0 / dim)

            def prep_fin(ti):
                t0, n = TILES[ti]
                for (k, src, nchunk, Bs, gidx, dim) in prep_srcs(ti):
                    rq, B_rq = prqs[(ti % 2) * 2 + k], B_prqs[(ti % 2) * 2 + k]
                    RCP(rq[:, 0:n], rq[:, 0:n], [B_rq], [B_rq])
                    for c in range(nchunk):
                        s_ = ptr[0] % 2
                        ptr[0] += 1
                        TT("dve", pt1[s_][:, 0:n], src[:, c, t0:t0 + n], rq[:, 0:n], ALU.mult, [Bs, B_rq], [B_pt1[s_]])
                        ACT(src[:, c, t0:t0 + n], pt1[s_][:, 0:n], AF.Identity, [B_pt1[s_], B_lay], [Bs], scale=vcol(gidx + c))
                s_ = ptr[0] % 2
                ptr[0] += 1
                TT("dve", pt1[s_][0:64, 0:n], kpe[0:64, t0:t0 + n], CC[0:64, t0:t0 + n], ALU.mult, [B_ld, B_kpe], [B_pt1[s_]])
                TT("pool", pt2[0:64, 0:n], ksw[0:64, t0:t0 + n], SSn[0:64, t0:t0 + n], ALU.mult, [B_ld], [B_pt2])
                TT("dve", kpe[0:64, t0:t0 + n], pt1[s_][0:64, 0:n], pt2[0:64, 0:n], ALU.add, [B_pt1[s_], B_pt2, B_ld], [B_kpe])

            ndt = len(dtiles)
            dconv_conv_pe(0)
            dconv_conv_evac(0)
            prep_sq(0)
            for di in range(ndt):
                dconv_stats_pe(di)
                if di >= 1:
                    dconv_pw(di - 1)
                prep_mm(di)
                if di + 1 < ndt:
                    dconv_conv_pe(di + 1)
                dconv_ln(di)
                if di >= 1:
                    prep_fin(di - 1)
                if di + 1 < ndt:
                    dconv_conv_evac(di + 1)
                if di + 1 < len(TILES):
                    prep_sq(di + 1)
            dconv_pw(ndt - 1)
            prep_fin(ndt - 1)
            for ti in range(ndt, len(TILES)):
                prep_mm(ti)
                prep_fin(ti)
            P.barrier()

            AR.reset()
            bg = v3(AR.b(4 * NT), NT)
            P.dma("sp", bg, pv[:, 6:10, :], reads=[B_pT[b]], writes=[B_bg])
            kn = v3(AR.b(4 * NT), NT)
            vt = v3(AR.b(18 * 512), 512)
            qn = v3(AR.b(4 * NT), NT)
            qr = v3(AR.b(4 * NT), NT)
            kr, B_kr = kpe, B_kpe
            PTb = [AR.b(512) for _ in range(4)]
            ysm = [AR.b(512) for _ in range(2)]
            B_qr = Buf()
            B_kn, B_vt, B_qn = DBuf(), DBuf(), DBuf()
            B_PT = [Buf() for _ in range(4)]
            B_ysm = [Buf(), Buf()]
            t1 = [AR.f(512) for _ in range(2)]
            t2 = [AR.f(512) for _ in range(2)]
            B_t1 = [Buf(), Buf()]
            B_t2 = [Buf(), Buf()]
            rc = [AR.f(512) for _ in range(2)]
            B_rc = [Buf(), Buf()]
            assert AR.off <= W_PF, (AR.off, W_PF)
            MSET("dve", qr, 0.0, [B_qr])
            tr = 0
            for h in range(4):
                for ti, (t0, n) in enumerate(TILES):
                    bi = next_bank()
                    P.group("pe", [mm(banks[bi][:, 0:n], wukv[:, kc, h * 128:(h + 1) * 128], kva[:, kc, t0:t0 + n], kc == 0, kc == 1) for kc in range(2)],
                            reads=[B_w, B_ckvt[ti]], writes=[BK[bi]])
                    evac_copy(kn[:, h, t0:t0 + n], banks[bi][:, 0:n], [BK[bi]], [B_kn])
                    if last and ti == 4:
                        continue
                    bi = next_bank()
                    P.group("pe", [mm(banks[bi][:, 0:n], wuq[:, kc, h * 128:(h + 1) * 128], qa[:, kc, t0:t0 + n], kc == 0, kc == 2) for kc in range(3)],
                            reads=[B_w, B_cqt[ti]], writes=[BK[bi]])
                    evac_copy(qn[:, h, t0:t0 + n], banks[bi][:, 0:n], [BK[bi]], [B_qn])
                    b1 = next_bank()
                    P.group("pe", [mm(banks[b1][0:64, 0:n], wuq[:, kc, 512 + h * 64:512 + (h + 1) * 64], qa[:, kc, t0:t0 + n], kc == 0, kc == 2) for kc in range(3)],
                            reads=[B_w, B_cqt[ti]], writes=[BK[b1]])
                    b2 = next_bank()
                    P.group("pe", [mm(banks[b2][0:64, 0:n], wuq[:, kc, 768 + h * 64:768 + (h + 1) * 64], qa[:, kc, t0:t0 + n], kc == 0, kc == 2) for kc in range(3)],
                            reads=[B_w, B_cqt[ti]], writes=[BK[b2]])
                    s = tr % 2
                    tr += 1
                    TT("dve", t1[s][0:64, 0:n], banks[b1][0:64, 0:n], CC[0:64, t0:t0 + n], ALU.mult, [BK[b1], B_ld], [B_t1[s]])
                    TT("dve", t2[s][0:64, 0:n], banks[b2][0:64, 0:n], SSn[0:64, t0:t0 + n], ALU.mult, [BK[b2], B_ld], [B_t2[s]])
                    TT("pool", qr[0:64, h, t0:t0 + n], t1[s][0:64, 0:n], t2[s][0:64, 0:n], ALU.add, [B_t1[s], B_t2[s]], [B_qr])
            for tt in range(18):
                bi = next_bank()
                P.group("pe", [mm(banks[bi][:, :], kva[:, kc, tt * 128:(tt + 1) * 128], wukv[:, kc, 512:1024], kc == 0, kc == 1) for kc in range(2)],
                        reads=[B_w, B_ckvt[min(tt // 4, 4)]], writes=[BK[bi]])
                evac_copy(vt[:, tt, :], banks[bi][:, :], [BK[bi]], [B_vt])
            ai = 0
            for h in range(4):
                for qi, (q0, n) in enumerate(TILES):
                    if qi == 4 and last:
                        continue
                    kts = [16, 17] + (list(range(16)) if qi < 4 else [])
                    ob, db = 3 + ai % 2, 5 + ai % 2
                    s = ai % 2
                    ai += 1

                    SBK = (0, 1, 2, 7)

                    def score(j):
                        kt = kts[j]
                        sb = SBK[j % 4]
                        P.group("pe", [mm(banks[sb][:, 0:n], kn[:, h, kt * 128:(kt + 1) * 128], qn[:, h, q0:q0 + n], True, False),
                                       mm(banks[sb][:, 0:n], kr[:, kt * 128:(kt + 1) * 128], qr[:, h, q0:q0 + n], False, True)],
                                reads=[B_kn, B_kr, B_qn, B_qr], writes=[BK[sb]])
                    score(0)
                    if len(kts) > 1:
                        score(1)
                    for j in range(len(kts)):
                        if j + 2 < len(kts):
                            score(j + 2)
                        sb = SBK[j % 4]
                        pb = j % 4
                        kt = kts[j]
                        ACT(PTb[pb][:, 0:n], banks[sb][:, 0:n], AF.Exp, [BK[sb]], [B_PT[pb]], scale=MLA_SCALE)
                        first, lastk = (j == 0), (j == len(kts) - 1)
                        P.group("pe", [mm(banks[ob][:, 0:n], vt[:, kt, h * 128:(h + 1) * 128], PTb[pb][:, 0:n], first, lastk),
                                       mm(banks[db][:, 0:n], ones_b, PTb[pb][:, 0:n], first, lastk)],
                                reads=[B_vt, B_PT[pb], B_const], writes=[BK[ob], BK[db]])
                    RCP(rc[s][:, 0:n], banks[db][:, 0:n], [BK[db]], [B_rc[s]])
                    TT("dve", rc[s][:, 0:n], banks[ob][:, 0:n], rc[s][:, 0:n], ALU.mult, [BK[ob]], [B_rc[s]])
                    TT("pool", ysm[s][:, 0:n], rc[s][:, 0:n], bg[:, h, q0:q0 + n], ALU.mult, [B_rc[s], B_bg], [B_ysm[s]])
                    P.dma("sp", yT[b][512 + h * 128:512 + (h + 1) * 128, q0:q0 + n], ysm[s][:, 0:n], reads=[B_ysm[s]], writes=[B_yT[b]])
            P.barrier()

            AR.reset()
            cq = v3(AR.b(4 * NT), NT)
            cg = v3(AR.b(4 * NT), NT)
            kbd = AR.b(4 * 36 * 128)
            vbd = AR.b(4 * 36 * 128)
            ubb = v3(AR.b(4 * 960), 960)
            PTn = [AR.b(512) for _ in range(4)]
            ysn = [AR.b(512) for _ in range(2)]
            B_nl, B_kbd, B_vbd, B_ub = Buf(), Buf(), Buf(), Buf()
            B_PTn = [Buf() for _ in range(4)]
            B_ysn = [Buf(), Buf()]
            rcn = [AR.f(512) for _ in range(2)]
            B_rcn = [Buf(), Buf()]
            kbd4 = kbd.rearrange("p (h r k) -> p h r k", h=4, r=36)
            vbd4 = vbd.rearrange("p (h r k) -> p h r k", h=4, r=36)
            kbdh = kbd.rearrange("p (h q) -> p h q", h=4)
            vbdh = vbd.rearrange("p (h q) -> p h q", h=4)
            KBDh = KBD[b].rearrange("p (h q) -> p h q", h=4)
            VBDh = VBD[b].rearrange("p (h q) -> p h q", h=4)
            B_kbdh = [Buf() for _ in range(4)]
            B_vbdh = [Buf() for _ in range(4)]
            B_cqh = [Buf() for _ in range(4)]
            B_cgl = Buf()
            ubf = v3(AR.f(4 * 960), 960)
            B_ubf = Buf()
            P.dma("sp", ubf, ub_in[l].rearrange("h p n -> p h n"), writes=[B_ubf])
            ACT(ubb, ubf, AF.Copy, [B_ubf], [B_ub], scale=8.0)
            for hp in range(4):
                P.dma("sp", kbdh[:, hp, :], KBDh[:, hp, :], reads=[B_KBD[b]], writes=[B_kbdh[hp]])
                P.dma("sp", cq[:, hp, :], pv[:, 10 + hp, :], reads=[B_pT[b]], writes=[B_cqh[hp]])
                P.dma("sp", vbdh[:, hp, :], VBDh[:, hp, :], reads=[B_VBD[b]], writes=[B_vbdh[hp]])
            P.dma("sp", cg, pv[:, 18:22, :], reads=[B_pT[b]], writes=[B_cgl])
            ai = 0
            qblocks = [(m * 512, 512, m) for m in range(4)] + ([] if last else [(S, CT, -1)])
            for hp in range(4):
                for (q0, n, m) in qblocks:
                    ob, db = 3 + ai % 2, 5 + ai % 2
                    s = ai % 2
                    ai += 1
                    items = [(32 + r, 0, n, None) for r in range(4)]
                    if m >= 0:
                        for kr_ in range(32):
                            qa_ = max(8 * m, Qr[kr_][0])
                            qb_ = min(8 * m + 8, Qr[kr_][1])
                            if qb_ > qa_:
                                j0 = 7 - kr_ + qa_
                                assert 0 <= j0 and j0 + (qb_ - qa_) <= 15
                                items.append((kr_, (qa_ - 8 * m) * 64, (qb_ - qa_) * 64, j0))

                    SBK = (0, 1, 2, 7)

                    def nscore(j):
                        row, c0, w, j0 = items[j]
                        sb = SBK[j % 4]
                        fns = []
                        if j0 is not None:
                            fns.append(mm(banks[sb][:, 0:w], ident_b, ubb[:, hp, j0 * 64:j0 * 64 + w], True, False))
                        fns.append(mm(banks[sb][:, 0:w], kbd4[:, hp, row, :], cq[:, hp, q0 + c0:q0 + c0 + w], j0 is None, True))
                        P.group("pe", fns, reads=[B_kbdh[hp], B_cqh[hp], B_ub, B_const], writes=[BK[sb]])
                    nscore(0)
                    if len(items) > 1:
                        nscore(1)
                    for j in range(len(items)):
                        if j + 2 < len(items):
                            nscore(j + 2)
                        row, c0, w, j0 = items[j]
                        sb = SBK[j % 4]
                        pb = j % 4
                        ACT(PTn[pb][:, 0:w], banks[sb][:, 0:w], AF.Exp, [BK[sb]], [B_PTn[pb]], scale=NA_SCALE)
                        first, lastk = (j == 0), (j == len(items) - 1)
                        P.group("pe", [mm(banks[ob][:, c0:c0 + w], vbd4[:, hp, row, :], PTn[pb][:, 0:w], first, lastk),
                                       mm(banks[db][:, c0:c0 + w], onesbd_b, PTn[pb][:, 0:w], first, lastk)],
                                reads=[B_vbdh[hp], B_PTn[pb], B_const], writes=[BK[ob], BK[db]])
                    RCP(rcn[s][:, 0:n], banks[db][:, 0:n], [BK[db]], [B_rcn[s]])
                    TT("dve", rcn[s][:, 0:n], banks[ob][:, 0:n], rcn[s][:, 0:n], ALU.mult, [BK[ob]], [B_rcn[s]])
                    TT("pool", ysn[s][:, 0:n], rcn[s][:, 0:n], cg[:, hp, q0:q0 + n], ALU.mult, [B_rcn[s], B_cgl], [B_ysn[s]])
                    P.dma("sp", yT[b][1024 + hp * 128:1024 + (hp + 1) * 128, q0:q0 + n], ysn[s][:, 0:n], reads=[B_ysn[s]], writes=[B_yT[b]])
            P.barrier()

            AR.reset()
            wout = v3(AR.b(16 * 2048), 2048)
            yt = [v3(AR.b(16 * 512), 512) for _ in range(2)]
            sqm = v3(AR.b(16 * 512), 512)
            B_woutg = [Buf() for _ in range(4)]
            B_sqm = DBuf()
            B_yt = [Buf(), Buf()]
            xts = [v3(AR.f(16 * 512), 512) for _ in range(2)]
            B_xts = [Buf(), Buf()]
            rsm = AR.f(512)
            B_rsm = Buf()
            if last:
                osbs = [AR.f(2048) for _ in range(2)]
                B_osbs = [DBuf(), DBuf()]
            else:
                m_ht = v3(AR.b(16 * 512), 512)
                B_mht = DBuf()
                m_tmp = [AR.f(512) for _ in range(2)]
                B_mtmp = [Buf(), Buf()]
            woutv = wout_in[l].rearrange("(c p) n -> p c n", p=128)
            for g in range(4):
                P.dma("pool", wout[:, :, g * 512:(g + 1) * 512], woutv[:, :, g * 512:(g + 1) * 512], writes=[B_woutg[g]])
            mtiles = TILES[:4] if last else TILES
            oi = [0]

            def merge_loads(ti_):
                t0_, n_ = mtiles[ti_]
                P.dma("sp", yt[ti_ % 2][:, :, 0:n_], yTv[:, :, t0_:t0_ + n_], reads=[B_yT[b]], writes=[B_yt[ti_ % 2]])
                P.dma("sp", xts[ti_ % 2][:, :, 0:n_], XTv[:, :, t0_:t0_ + n_], reads=[B_XT[b]], writes=[B_xts[ti_ % 2]])

            def merge_block(ti_, db_):
                t0_, n_ = mtiles[ti_]
                mi = b if ti_ < 4 else 2
                xt, B_xt = xts[ti_ % 2], B_xts[ti_ % 2]
                bi = next_bank()
                P.group("pe", [mm(banks[bi][:, 0:n_], wout[:, kc, db_ * 128:(db_ + 1) * 128], yt[ti_ % 2][:, kc, 0:n_], kc == 0, kc == 15) for kc in range(16)],
                        reads=[B_woutg[db_ // 4], B_yt[ti_ % 2]], writes=[BK[bi]])
                STT(xt[:, db_, 0:n_], banks[bi][:, 0:n_], gtv(db_, mi, l), xt[:, db_, 0:n_], ALU.mult, ALU.add, [BK[bi], B_mod], [B_xt])

            def fin_squares(ti_):
                t0_, n_ = mtiles[ti_]
                xt, B_xt = xts[ti_ % 2], B_xts[ti_ % 2]
                for c in range(16):
                    ACT(sqm[:, c, 0:n_], xt[:, c, 0:n_], AF.Square, [B_xt], [B_sqm])

            def fin_stats(ti_):
                t0_, n_ = mtiles[ti_]
                bi = 7
                P.group("pe", [mm(banks[bi][:, 0:n_], ones_b, sqm[:, c, 0:n_], c == 0, c == 15) for c in range(16)], reads=[B_sqm, B_const], writes=[BK[bi]])
                ACT(rsm[:, 0:n_], banks[bi][:, 0:n_], AF.Sqrt, [BK[bi], B_const], [B_rsm], bias=eps_t, scale=1.0 / D)
                RCP(rsm[:, 0:n_], rsm[:, 0:n_], [B_rsm], [B_rsm])

            def fin_scale(ti_, c):
                t0_, n_ = mtiles[ti_]
                xt, B_xt = xts[ti_ % 2], B_xts[ti_ % 2]
                TT("pool", xt[:, c, 0:n_], xt[:, c, 0:n_], rsm[:, 0:n_], ALU.mult, [B_rsm], [B_xt])
                ACT(xt[:, c, 0:n_], xt[:, c, 0:n_], AF.Identity, [B_const], [B_xt], scale=fng[:, c:c + 1])

            def fin_out(ti_):
                t0_, n_ = mtiles[ti_]
                xt, B_xt = xts[ti_ % 2], B_xts[ti_ % 2]
                for sub in range(n_ // 128):
                    o_ = oi[0] % 2
                    oi[0] += 1
                    for g in range(4):
                        bi = next_bank()
                        fns = [TRN(banks[bi][:, j * 128:(j + 1) * 128], xt[:, g * 4 + j, sub * 128:(sub + 1) * 128]) for j in range(4)]
                        P.group("pe", fns, reads=[B_xt, B_const], writes=[BK[bi]])
                        evac_copy(osbs[o_][:, g * 512:(g + 1) * 512], banks[bi][:, :], [BK[bi]], [B_osbs[o_]])
                    P.dma("sp", out[b, t0_ + sub * 128:t0_ + (sub + 1) * 128, :], osbs[o_], reads=[B_osbs[o_]], writes=[])

            merge_loads(0)
            if not last:
                nt0 = len(mtiles)

                def nargs(ti_):
                    t0_, n_ = mtiles[ti_]
                    return (xts[ti_ % 2], B_xts[ti_ % 2], n_)
                merge_loads(1)
                for db_ in range(16):
                    merge_block(0, db_)
                for ti in range(nt0):
                    t0, n = mtiles[ti]
                    mi_ = b if ti < 4 else 2
                    xs_, B_xs_, _n = nargs(ti)
                    P.dma("sp", XTv[:, :, t0:t0 + n], xs_[:, :, 0:n], reads=[B_xs_], writes=[B_XT[b]])
                    norm_squares(xs_, B_xs_, n, sqm, B_sqm, True)
                    if ti + 1 < nt0:
                        for db_ in range(16):
                            merge_block(ti + 1, db_)
                            if db_ == 1:
                                norm_stats(n, sqm, B_sqm, rsm, B_rsm)
                            if 2 <= db_ <= 9:
                                for c in (2 * (db_ - 2), 2 * (db_ - 2) + 1):
                                    norm_chunk(xs_, B_xs_, n, c, mi_, l + 1, rsm, B_rsm, m_tmp, B_mtmp, m_ht, B_mht)
                            if db_ == 9 and ti + 2 < nt0:
                                merge_loads(ti + 2)
                    else:
                        norm_stats(n, sqm, B_sqm, rsm, B_rsm)
                        for c in range(16):
                            norm_chunk(xs_, B_xs_, n, c, mi_, l + 1, rsm, B_rsm, m_tmp, B_mtmp, m_ht, B_mht)
                    norm_store(n, b, t0, m_ht, B_mht)
            else:
                nt_ = len(mtiles)
                merge_loads(1)
                for db_ in range(16):
                    merge_block(0, db_)
                for ti in range(nt_):
                    nxt = ti + 1 < nt_
                    fin_squares(ti)
                    if nxt:
                        for db_ in range(16):
                            merge_block(ti + 1, db_)
                            if db_ == 3:
                                fin_stats(ti)
                            if db_ >= 4:
                                fin_scale(ti, db_ - 4)
                        for c in range(12, 16):
                            fin_scale(ti, c)
                    else:
                        fin_stats(ti)
                        for c in range(16):
                            fin_scale(ti, c)
                    fin_out(ti)
                    if ti + 2 < nt_:
                        merge_loads(ti + 2)
            P.barrier()

    P.barrier()
    P.emit(nc, None, sems)
    es.close()
    return nc


def _fm(v, nchunk):
    return np.ascontiguousarray(np.asarray(v, np.float32).reshape(nchunk, 128).T)


def _host_layout(inp):
    f32 = np.float32
    w_in = inp["w_in"]
    o = dict(a_x=0, a_b=512, a_c=1024, a_g=1536, b_qa=2048, b_kva=2432, b_kpe=2688, b_g=2752,
             c_q=3264, c_k=3776, c_v=4288, c_g=4800, glu_a=5312, glu_g=5824, d_g=6336)
    cols = []

    def blk(name, j, w=128):
        cols.extend(range(o[name] + j * w, o[name] + (j + 1) * w))
    for j in range(4):
        blk("a_x", j); blk("a_c", j); blk("a_b", j); blk("a_g", j)
    for j in range(4):
        blk("glu_a", j); blk("glu_g", j)
    for j in range(4):
        blk("d_g", j)
    for j in range(3):
        blk("b_qa", j)
    cols.extend(range(o["b_kpe"], o["b_kpe"] + 64))
    cols.extend(list(range(o["b_kpe"] + 32, o["b_kpe"] + 64)) + list(range(o["b_kpe"], o["b_kpe"] + 32)))
    for j in range(2):
        blk("b_kva", j)
    for j in range(4):
        blk("b_g", j)
    for nm in ("c_q", "c_k", "c_g", "c_v"):
        for j in range(4):
            blk(nm, j)
    cols = np.asarray(cols)
    assert cols.size == 6912
    win_r = np.ascontiguousarray(w_in[:, :, cols])
    cq = []
    for h in range(4):
        cq.extend(range(h * 192, h * 192 + 128))
    for h in range(4):
        cq.extend(range(h * 192 + 128, h * 192 + 192))
    for h in range(4):
        cq.extend(list(range(h * 192 + 160, h * 192 + 192)) + list(range(h * 192 + 128, h * 192 + 160)))
    wuq_r = np.ascontiguousarray(inp["mla_w_uq"][:, :, np.asarray(cq)])
    ckv = []
    for h in range(4):
        ckv.extend(range(h * 256, h * 256 + 128))
    for h in range(4):
        ckv.extend(range(h * 256 + 128, h * 256 + 256))
    wukv_r = np.ascontiguousarray(inp["mla_w_ukv"][:, :, np.asarray(ckv)])
    vecs = np.zeros((L, 128, 64), f32)
    dww = np.zeros((L, 128, 124), f32)
    ng3 = np.zeros((L, 128, 48), f32)
    bada3 = np.zeros((L, 128, 144), f32)
    for l in range(L):
        acw = inp["a_conv_w"][l]
        for c in range(4):
            for k in range(3):
                vecs[l, :, c * 3 + k] = acw[k, c * 128:(c + 1) * 128]
        vecs[l, :, 12:16] = _fm(inp["d_dw_b"][l], 4)
        vecs[l, :, 16:20] = _fm(inp["d_ln_g"][l], 4)
        vecs[l, :, 20:24] = _fm(inp["d_ln_b"][l], 4)
        vecs[l, :, 24:28] = _fm(inp["d_pw_b"][l], 4)
        vecs[l, :, 28:31] = _fm(inp["mla_q_norm"][l], 3)
        vecs[l, :, 31:33] = _fm(inp["mla_kv_norm"][l], 2)
        dw = inp["d_dw_w"][l]
        for c in range(4):
            dww[l, :, c * 31:(c + 1) * 31] = dw[:, c * 128:(c + 1) * 128].T
        ng3[l] = np.repeat(_fm(inp["norm_g"][l], 16), 3, axis=1)
        bada3[l] = np.repeat(_fm(inp["b_ada"][l], 48), 3, axis=1)
    fng = _fm(inp["final_norm_g"], 16)
    t = np.arange(S)
    row = (t // GW).astype(f32)
    col = (t % GW).astype(f32)
    freqs = (np.float32(10000.0) ** (-(np.arange(16, dtype=f32) * np.float32(2.0) / np.float32(32)))).astype(f32)
    ang = np.concatenate([row[:, None] * freqs, col[:, None] * freqs], axis=-1).astype(f32)
    cos, sin = np.cos(ang).astype(f32), np.sin(ang).astype(f32)
    rope = np.zeros((2, 64, NT), f32)
    rope[0, :, S:] = 1.0
    rope[0, 0:32, :S] = cos.T
    rope[0, 32:64, :S] = cos.T
    rope[1, 0:32, :S] = -sin.T
    rope[1, 32:64, :S] = sin.T
    rpb = inp["na_rpb"]
    kc = np.arange(64)[:, None]
    qc = np.arange(64)[None, :]
    cstart = np.clip(qc - 8, 0, 48)
    valid = (kc >= cstart) & (kc < cstart + 16)
    dc = np.clip(kc - qc + 15, 0, 30)
    ub = np.full((L, 4, 128, 15, 64), NEG, f32)
    for l in range(L):
        for h in range(8):
            for j in range(15):
                dr = 7 - j
                tab = rpb[l, h, dr + 7][dc]
                ub[l, h // 2, (h % 2) * 64:(h % 2) * 64 + 64, j, :] = np.where(valid, tab, np.float32(NEG))
    ub = ub.reshape(L, 4, 128, 960)
    ident = np.eye(128, dtype=f32)
    onesbd = np.zeros((128, 128), f32)
    onesbd[0:64, 0:64] = 1.0
    onesbd[64:128, 64:128] = 1.0
    shared = dict(ng3=ng3, bada3=bada3, w_ada=np.ascontiguousarray(inp["w_ada"], f32), win_r=win_r,
                  w_out=np.ascontiguousarray(inp["w_out"], f32), wuq_r=wuq_r, wukv_r=wukv_r,
                  d_pw_w=np.ascontiguousarray(inp["d_pw_w"], f32), vecs=vecs, dww=dww, fng=fng, rope=rope, ub=ub,
                  ident=ident, onesbd=onesbd)
    in_maps = []
    for core in range(NCORES):
        bs = [core * NB + i for i in range(NB)]
        cT = np.zeros((128, 16, 3), f32)
        for i, bb in enumerate(bs):
            cT[:, :, i] = _fm(inp["c"][bb], 16)
        cT[:, :, 2] = _fm(inp["c_ctx"], 16)
        m = dict(shared)
        m["x"] = np.ascontiguousarray(inp["x"][bs[0]:bs[0] + NB], f32)
        m["ctx"] = np.ascontiguousarray(inp["ctx"][bs[0]:bs[0] + NB], f32)
        m["cT"] = cT.reshape(128, 48)
        in_maps.append(m)
    return in_maps


_NC_CACHE = {}


def kernel(**inputs):
    inp = {k: np.asarray(v) for k, v in inputs.items()}
    in_maps = _host_layout(inp)
    if "nc" not in _NC_CACHE:
        _NC_CACHE["nc"] = build_nc()
    nc = _NC_CACHE["nc"]
    res = run_bass_kernel_spmd(nc, in_maps, core_ids=list(range(NCORES)))
    outs = [np.asarray(r["out"]) for r in res.results]
    return np.concatenate(outs, axis=0).astype(np.float32)
```

```python
import numpy as np
import concourse.bass as bass
import concourse.mybir as mybir
from concourse.bass_utils import run_bass_kernel_spmd

F32 = mybir.dt.float32
BF16 = mybir.dt.bfloat16
AF = mybir.ActivationFunctionType
ALU = mybir.AluOpType

NCORES = 8
D = 2048
S = 2048
CT = 256
NT = S + CT
L = 2
NB = 2
GW = 64
EPS = 1e-6
MLA_SCALE = float(192 ** -0.5)
NA_SCALE = 0.125
NEG = -30000.0
TILES = [(0, 512), (512, 512), (1024, 512), (1536, 512), (2048, 256)]
UW = 2368
U_LAT = 15
U_CTX = 15 + 2048 + 30

R_QA, R_KPE, R_KSW, R_KVA, R_BG, R_CQ, R_CK, R_CG = 0, 384, 448, 512, 768, 1280, 1792, 2304
PT_ROWS = 2816

DEBUG = False


class Buf:
    __slots__ = ("w", "r", "pr", "strict", "disjoint")

    def __init__(self, strict=False, disjoint=False):
        self.w = {}
        self.r = {}
        self.pr = {}
        self.strict = strict
        self.disjoint = disjoint


def DBuf():
    return Buf(disjoint=True)


COMPUTE = ("pe", "act", "dve", "pool")
STRICT_SAME_ENGINE = False


class Prog:
    def __init__(self, ndma=20):
        self.ops = {e: [] for e in ("pe", "act", "dve", "pool", "sp")}
        self.cnt = {e: 0 for e in COMPUTE}
        self.seen = {e: {} for e in self.ops}
        self.issued = {}
        self.ndma = ndma
        self.dma_rr = {"sp": 0, "pool": 0}
        self.dma_val = {}

    def _waits(self, eng, reads, writes, extra):
        need = {}

        def add(d, strict):
            for k, v in d.items():
                if k == eng and not strict and (eng == "pe" or not STRICT_SAME_ENGINE):
                    continue
                if need.get(k, 0) < v:
                    need[k] = v

        for b in reads:
            add(b.w, b.strict)
        for b in writes:
            if not b.disjoint:
                add(b.w, b.strict)
            else:
                add(b.pr, b.strict)
            add(b.r, b.strict)
        for t in extra:
            if t is not None:
                if need.get(t[0], 0) < t[1]:
                    need[t[0]] = t[1]
        out = []
        sn = self.seen[eng]
        for k, v in need.items():
            if sn.get(k, 0) < v:
                sn[k] = v
                out.append((k, v))
        return out

    def _commit(self, tok, reads, writes):
        k, v = tok
        for b in reads:
            if b.r.get(k, 0) < v:
                b.r[k] = v
        for b in writes:
            if b.r:
                b.pr = b.r
                b.r = {}
                b.w = {}
            if b.w.get(k, 0) < v:
                b.w[k] = v
        if self.issued.get(k, 0) < v:
            self.issued[k] = v

    def op(self, eng, fn, reads=(), writes=(), extra=(), sig=True):
        wl = self._waits(eng, reads, writes, extra)
        tok = None
        if sig:
            self.cnt[eng] += 1
            tok = (eng, self.cnt[eng])
            self._commit(tok, reads, writes)
        self.ops[eng].append((fn, wl, 1 if sig else 0, eng))
        return tok

    def group(self, eng, fns, reads=(), writes=(), extra=()):
        wl = self._waits(eng, reads, writes, extra)
        self.cnt[eng] += 1
        tok = (eng, self.cnt[eng])
        self._commit(tok, reads, writes)
        n = len(fns)
        for i, fn in enumerate(fns):
            self.ops[eng].append((fn, wl if i == 0 else [], 1 if i == n - 1 else 0, eng))
        return tok

    def dma(self, q, out, in_, reads=(), writes=(), extra=()):
        i = self.dma_rr[q]
        self.dma_rr[q] = (i + 1) % self.ndma
        key = ("dma", q, i)
        prev = self.dma_val.get(key, 0)
        ex = list(extra)
        if prev:
            ex.append((key, prev))
        wl = self._waits(q, reads, writes, ex)
        val = prev + 16
        self.dma_val[key] = val
        tok = (key, val)
        self._commit(tok, reads, writes)
        self.ops[q].append((lambda e, o=out, i_=in_: e.dma_start(out=o, in_=i_), wl, 16, key))
        return tok

    def barrier(self):
        snap = dict(self.issued)
        for eng in self.ops:
            wl = []
            sn = self.seen[eng]
            for k, v in snap.items():
                if k == eng:
                    continue
                if sn.get(k, 0) < v:
                    sn[k] = v
                    wl.append((k, v))
            if wl:
                self.ops[eng].append((None, wl, 0, eng))

    def emit(self, nc, engines, sems):
        with nc.Block() as block:
            def run(name):
                def body(e):
                    for fn, wl, inc, key in self.ops[name]:
                        for k, v in wl:
                            e.wait_ge(sems[k], v)
                        if fn is None:
                            continue
                        ins = fn(e)
                        if inc:
                            ins.then_inc(sems[key], inc)
                return body
            block.tensor(run("pe"))
            block.scalar(run("act"))
            block.vector(run("dve"))
            block.gpsimd(run("pool"))
            block.sync(run("sp"))


class Arena:
    def __init__(self, t, nwords):
        self.t = t
        self.nbytes = nwords * 4
        self.off = 0

    def reset(self, off=0):
        self.off = off

    def _take(self, nbytes):
        nb = (nbytes + 63) // 64 * 64
        assert self.off + nb <= self.nbytes, ("arena overflow", self.off, nb, self.nbytes)
        o = self.off
        self.off += nb
        return o

    def f(self, n, parts=128):
        o = self._take(n * 4)
        return self.t[0:parts, o // 4:o // 4 + n]

    def b(self, n, parts=128):
        assert n % 2 == 0
        o = self._take(n * 2)
        return self.t[0:parts, o // 4:o // 4 + n // 2].bitcast(BF16)


def v3(ap, b):
    return ap.rearrange("p (a b) -> p a b", b=b)


def na_tables():
    rows = S // GW
    rstart = [min(max(r - 4, 0), rows - 8) for r in range(rows)]
    Q = {}
    for kr in range(rows):
        qs = [qr for qr in range(rows) if rstart[qr] <= kr <= rstart[qr] + 7]
        assert qs == list(range(qs[0], qs[-1] + 1))
        Q[kr] = (qs[0], qs[-1] + 1)
    return Q


def build_nc():
    nc = bass.Bass("TRN2", target_bir_lowering=False)
    P = Prog()

    def din(name, shape, dt=F32):
        return nc.dram_tensor(name, list(shape), dt, kind="ExternalInput").ap()

    x_in = din("x", [NB, S, D])
    ctx_in = din("ctx", [NB, CT, D])
    cT_in = din("cT", [128, 16 * 3])
    ng3_in = din("ng3", [L, 128, 16 * 3])
    bada3_in = din("bada3", [L, 128, 48 * 3])
    wada_in = din("w_ada", [L, D, 3 * D])
    win_in = din("win_r", [L, D, 6912])
    wout_in = din("w_out", [L, D, D])
    wuq_in = din("wuq_r", [L, 384, 1024])
    wukv_in = din("wukv_r", [L, 256, 1024])
    wpw_in = din("d_pw_w", [L, 512, 512])
    vec_in = din("vecs", [L, 128, 64])
    dww_in = din("dww", [L, 128, 4 * 31])
    fng_in = din("fng", [128, 16])
    rope_in = din("rope", [2, 64, NT])
    ub_in = din("ub", [L, 4, 128, 960])
    ident_in = din("ident", [128, 128])
    onesbd_in = din("onesbd", [128, 128])
    out = nc.dram_tensor("out", [NB, S, D], F32, kind="ExternalOutput").ap()

    skind = "ExternalOutput" if DEBUG else "Internal"
    XT = [nc.dram_tensor(f"XT{b}", [D, NT], F32, kind=skind).ap() for b in range(NB)]
    pT = [nc.dram_tensor(f"pT{b}", [PT_ROWS, NT], BF16, kind=skind).ap() for b in range(NB)]
    vtok = [nc.dram_tensor(f"vtok{b}", [8, 64, 36, 64], BF16, kind=skind).ap() for b in range(NB)]
    yT = [nc.dram_tensor(f"yT{b}", [D, NT], BF16, kind=skind).ap() for b in range(NB)]
    hTd = [nc.dram_tensor(f"hTd{b}", [D, NT], BF16, kind=skind).ap() for b in range(NB)]
    B_hTd = [DBuf() for _ in range(NB)]
    KBD = [nc.dram_tensor(f"KBD{b}", [128, 4 * 36 * 128], BF16, kind=skind).ap() for b in range(NB)]
    VBD = [nc.dram_tensor(f"VBD{b}", [128, 4 * 36 * 128], BF16, kind=skind).ap() for b in range(NB)]
    B_KBD = [DBuf() for _ in range(NB)]
    B_VBD = [DBuf() for _ in range(NB)]
    B_XT = [DBuf() for _ in range(NB)]
    B_pT = [DBuf() for _ in range(NB)]
    B_vtok = [DBuf() for _ in range(NB)]
    B_yT = [DBuf() for _ in range(NB)]

    NW = 51200
    from contextlib import ExitStack
    es = ExitStack()
    ar_t = es.enter_context(nc.sbuf_tensor("arena", [128, NW], F32))
    cst_t = es.enter_context(nc.sbuf_tensor("cst_f", [128, 1400], F32))
    cstb_t = es.enter_context(nc.sbuf_tensor("cst_b", [128, 600], BF16))
    banks = [es.enter_context(nc.psum_tensor(f"ps{i}", [128, 512], F32)) for i in range(8)]
    BK = [Buf() for _ in range(8)]
    AR = Arena(ar_t, NW)

    keys = list(COMPUTE) + [("dma", q, i) for q in ("sp", "pool") for i in range(P.ndma)]
    sems = {}
    for k in keys:
        nm = k if isinstance(k, str) else f"d_{k[1]}_{k[2]}"
        sems[k] = es.enter_context(nc.semaphore("s_" + nm))

    class _CA:
        def __init__(self, t):
            self.t = t
            self.off = 0

        def alloc(self, n):
            ap = self.t[:, self.off:self.off + n]
            self.off += (n + 15) // 16 * 16
            return ap
    CF = _CA(cst_t)
    CB = _CA(cstb_t)
    ident_f = CF.alloc(128)
    eps_t = CF.alloc(1)
    cT = CF.alloc(48)
    modv_l = [CF.alloc(144) for _ in range(L)]
    gsv_l = [CF.alloc(48) for _ in range(L)]
    ng3_l = [CF.alloc(48) for _ in range(L)]
    bada3_l = [CF.alloc(144) for _ in range(L)]
    vecs = CF.alloc(64)
    dww = CF.alloc(124)
    fng = CF.alloc(16)
    onesbd_f = CF.alloc(128)
    ident_b = CB.alloc(128)
    ones_b = CB.alloc(128)
    onesbd_b = CB.alloc(128)
    scT = CB.alloc(48)
    B_const = Buf(strict=True)
    B_mod_l = [Buf(strict=True) for _ in range(L)]
    B_adl = Buf(strict=True)
    B_lay = Buf(strict=True)

    P.dma("sp", ident_f, ident_in, writes=[B_const])
    P.dma("sp", onesbd_f, onesbd_in, writes=[B_const])
    P.dma("sp", cT, cT_in, writes=[B_const])
    P.dma("sp", fng, fng_in, writes=[B_const])
    P.op("dve", lambda e: e.memset(eps_t, EPS), writes=[B_const])
    P.op("dve", lambda e: e.memset(ones_b, 1.0), writes=[B_const])
    P.op("dve", lambda e: e.tensor_copy(out=ident_b, in_=ident_f), reads=[B_const], writes=[B_const])
    P.op("dve", lambda e: e.tensor_copy(out=onesbd_b, in_=onesbd_f), reads=[B_const], writes=[B_const])
    P.op("act", lambda e: e.activation(out=scT, in_=cT, func=AF.Silu), reads=[B_const], writes=[B_const])

    def vcol(i):
        return vecs[:, i:i + 1]
    V_ACW, V_DWB, V_LNG, V_LNB, V_PWB, V_QNG, V_KVG = 0, 12, 16, 20, 24, 28, 31

    cp_rr = [0]
    act_only = [False]

    def evac_copy(out_ap, in_ap, reads, writes, scale=None):
        cp_rr[0] ^= 1
        if cp_rr[0] or act_only[0]:
            if scale is None:
                return P.op("act", lambda e: e.activation(out=out_ap, in_=in_ap, func=AF.Copy), reads=reads, writes=writes)
            return P.op("act", lambda e: e.activation(out=out_ap, in_=in_ap, func=AF.Copy, scale=scale), reads=reads, writes=writes)
        if scale is None:
            return P.op("dve", lambda e: e.tensor_copy(out=out_ap, in_=in_ap), reads=reads, writes=writes)
        return P.op("dve", lambda e: e.tensor_scalar(out=out_ap, in0=in_ap, scalar1=scale, scalar2=None, op0=ALU.mult), reads=reads, writes=writes)

    def mm(out_ap, lhsT, rhs, start, stop):
        return lambda e: e.matmul(out_ap, lhsT=lhsT, rhs=rhs, start=start, stop=stop)

    bk_rr = [0]

    def next_bank(lo=0, hi=6):
        i = lo + bk_rr[0] % (hi - lo)
        bk_rr[0] += 1
        return i


    def ACT(out_, in_, func, reads, writes, bias=None, scale=None):
        kw = {}
        if bias is not None:
            kw["bias"] = bias
        if scale is not None:
            kw["scale"] = scale
        return P.op("act", lambda e: e.activation(out=out_, in_=in_, func=func, **kw), reads=reads, writes=writes)

    def TT(eng, out_, in0, in1, op, reads, writes):
        return P.op(eng, lambda e: e.tensor_tensor(out=out_, in0=in0, in1=in1, op=op), reads=reads, writes=writes)

    def TS(eng, out_, in0, s1, op0, reads, writes):
        return P.op(eng, lambda e: e.tensor_scalar(out=out_, in0=in0, scalar1=s1, scalar2=None, op0=op0), reads=reads, writes=writes)

    def STT(out_, in0, scalar, in1, op0, op1, reads, writes):
        return P.op("dve", lambda e: e.scalar_tensor_tensor(out=out_, in0=in0, scalar=scalar, in1=in1, op0=op0, op1=op1),
                    reads=reads, writes=writes)

    def RCP(out_, in_, reads, writes):
        return P.op("dve", lambda e: e.reciprocal(out=out_, in_=in_), reads=reads, writes=writes)

    def MSET(eng, ap, val, writes):
        return P.op(eng, lambda e: e.memset(ap, val), writes=writes)

    def TRN(out_, in_):
        return lambda e: e.transpose(out_, in_, ident_f)

    AR.reset()
    wb = [v3(AR.b(16 * 512), 512) for _ in range(2)]
    B_wb = [Buf(), Buf()]
    gi_ = 0
    for l in range(L):
        P.dma("sp", ng3_l[l], ng3_in[l], writes=[B_adl])
        P.dma("sp", bada3_l[l], bada3_in[l], writes=[B_adl])
        mbank = 6 + l
        wadav = wada_in[l].rearrange("(c p) n -> p c n", p=128)
        for g in range(12):
            s = gi_ % 2
            gi_ += 1
            P.dma("pool", wb[s], wadav[:, :, g * 512:(g + 1) * 512], writes=[B_wb[s]])
            for j in range(4):
                oc = g * 4 + j
                fns = [mm(banks[mbank][:, oc * 4:oc * 4 + 3], wb[s][:, kc, j * 128:(j + 1) * 128], scT[:, kc * 3:kc * 3 + 3],
                          kc == 0, kc == 15) for kc in range(16)]
                P.group("pe", fns, reads=[B_wb[s], B_const], writes=[BK[mbank]])
        TT("dve", v3(modv_l[l], 3), v3(banks[mbank][:, 0:192], 4)[:, :, 0:3], v3(bada3_l[l], 3), ALU.add, [BK[mbank], B_adl], [B_mod_l[l]])
        TS("dve", gsv_l[l], modv_l[l][:, 48:96], 1.0, ALU.add, [B_mod_l[l]], [B_mod_l[l]])
        TT("dve", gsv_l[l], gsv_l[l], ng3_l[l], ALU.mult, [B_mod_l[l], B_adl], [B_mod_l[l]])
    P.barrier()

    def shv(c, mi, l_):
        return modv_l[l_][:, c * 3 + mi:c * 3 + mi + 1]

    def gsc(c, mi, l_):
        return gsv_l[l_][:, c * 3 + mi:c * 3 + mi + 1]

    def gtv(c, mi, l_):
        return modv_l[l_][:, 96 + c * 3 + mi:96 + c * 3 + mi + 1]

    ntmp = [0]

    def norm_squares(xs, B_xs, n, sq, B_sq, sq_on_act):
        for c in range(16):
            if sq_on_act or c % 4 == 3:
                ACT(sq[:, c, 0:n], xs[:, c, 0:n], AF.Square, [B_xs], [B_sq])
            else:
                TT("dve", sq[:, c, 0:n], xs[:, c, 0:n], xs[:, c, 0:n], ALU.mult, [B_xs], [B_sq])

    def norm_stats(n, sq, B_sq, rs, B_rs):
        bi = next_bank()
        P.group("pe", [mm(banks[bi][:, 0:n], ones_b, sq[:, c, 0:n], c == 0, c == 15) for c in range(16)],
                reads=[B_sq, B_const], writes=[BK[bi]])
        ACT(rs[:, 0:n], banks[bi][:, 0:n], AF.Sqrt, [BK[bi], B_const], [B_rs], bias=eps_t, scale=1.0 / D)
        RCP(rs[:, 0:n], rs[:, 0:n], [B_rs], [B_rs])

    def norm_chunk(xs, B_xs, n, c, mi, l_, rs, B_rs, tmps, B_tmps, ht, B_ht):
        s_ = ntmp[0] % len(tmps)
        ntmp[0] += 1
        TT("dve", tmps[s_][:, 0:n], xs[:, c, 0:n], rs[:, 0:n], ALU.mult, [B_xs, B_rs], [B_tmps[s_]])
        ACT(ht[:, c, 0:n], tmps[s_][:, 0:n], AF.Identity, [B_tmps[s_], B_mod_l[l_]], [B_ht], bias=shv(c, mi, l_), scale=gsc(c, mi, l_))

    def norm_store(n, b_, t0, ht, B_ht):
        P.dma("sp", hTd[b_].rearrange("(c p) t -> p c t", p=128)[:, :, t0:t0 + n], ht[:, :, 0:n], reads=[B_ht], writes=[B_hTd[b_]])

    def emit_norm_tile(xs, B_xs, n, b_, t0, mi, l_, sq, B_sq, rs, B_rs, tmps, B_tmps, ht, B_ht, sq_on_act):
        norm_squares(xs, B_xs, n, sq, B_sq, sq_on_act)
        norm_stats(n, sq, B_sq, rs, B_rs)
        for c in range(16):
            norm_chunk(xs, B_xs, n, c, mi, l_, rs, B_rs, tmps, B_tmps, ht, B_ht)
        norm_store(n, b_, t0, ht, B_ht)

    AR.reset()
    zt = AR.b(4 * 36 * 128)
    B_zt = Buf()
    MSET("dve", zt, 0.0, [B_zt])
    for b in range(NB):
        P.dma("sp", KBD[b], zt, reads=[B_zt], writes=[B_KBD[b]])
        P.dma("sp", VBD[b], zt, reads=[B_zt], writes=[B_VBD[b]])
    xin = [AR.f(2048) for _ in range(4)]
    xst = [v3(AR.f(16 * 512), 512) for _ in range(2)]
    B_xin = [Buf() for _ in range(4)]
    B_xst = [DBuf(), DBuf()]
    p_sq = v3(AR.b(16 * 512), 512)
    p_ht = [v3(AR.b(16 * 512), 512) for _ in range(2)]
    p_rs = [AR.f(512) for _ in range(2)]
    p_tmp = [AR.f(512) for _ in range(4)]
    B_psq_ = DBuf()
    B_prs_ = [Buf(), Buf()]
    pend = [None]

    def pro_back():
        if pend[0] is None:
            return
        (xs_, B_xs_, n_, b_, t0_, mi_, rs_, B_rs_, ht_, B_ht_) = pend[0]
        for c in range(16):
            norm_chunk(xs_, B_xs_, n_, c, mi_, 0, rs_, B_rs_, p_tmp, B_ptmp, ht_, B_ht_)
        norm_store(n_, b_, t0_, ht_, B_ht_)
        pend[0] = None
    B_pht = [DBuf(), DBuf()]
    B_ptmp = [Buf() for _ in range(4)]
    groups = [(b_, g0, ntt) for b_ in range(NB) for (g0, ntt) in ((0, 4), (4, 4), (8, 4), (12, 4), (16, 2))]

    def pro_loads(gidx):
        b_, g0, ntt = groups[gidx]
        for q in range(ntt):
            tt = g0 + q
            src = x_in[b_, tt * 128:(tt + 1) * 128, :] if tt < 16 else ctx_in[b_, (tt - 16) * 128:(tt - 15) * 128, :]
            P.dma("sp", xin[q], src, writes=[B_xin[q]])

    pro_loads(0)
    for gidx, (b, g0, ntt) in enumerate(groups):
        XTv0 = XT[b].rearrange("(c p) t -> p c t", p=128)
        sg = gidx % 2
        for q in range(ntt):
            for g in range(4):
                bi = next_bank(0, 8)
                fns = [TRN(banks[bi][:, j * 128:(j + 1) * 128], xin[q][:, (g * 4 + j) * 128:(g * 4 + j + 1) * 128]) for j in range(4)]
                P.group("pe", fns, reads=[B_xin[q], B_const], writes=[BK[bi]])
                evac_copy(xst[sg][:, g * 4:(g + 1) * 4, q * 128:(q + 1) * 128], v3(banks[bi][:, :], 128), [BK[bi]], [B_xst[sg]])
        if gidx + 1 < len(groups):
            pro_loads(gidx + 1)
        P.dma("sp", XTv0[:, :, g0 * 128:(g0 + ntt) * 128], xst[sg][:, :, 0:ntt * 128], reads=[B_xst[sg]], writes=[B_XT[b]])
        norm_squares(xst[sg], B_xst[sg], ntt * 128, p_sq, B_psq_, False)
        norm_stats(ntt * 128, p_sq, B_psq_, p_rs[sg], B_prs_[sg])
        pro_back()
        pend[0] = (xst[sg], B_xst[sg], ntt * 128, b, g0 * 128, (b if g0 < 16 else 2), p_rs[sg], B_prs_[sg], p_ht[sg], B_pht[sg])
    pro_back()
    P.barrier()

    Qr = na_tables()

    for l in range(L):
        last = (l == L - 1)
        AR.reset()
        P.dma("sp", vecs, vec_in[l], writes=[B_lay])
        P.dma("sp", dww, dww_in[l], writes=[B_lay])
        B_mod = B_mod_l[l]
        P.barrier()

        for b in range(NB):
            XTv = XT[b].rearrange("(c p) t -> p c t", p=128)
            yTv = yT[b].rearrange("(c p) t -> p c t", p=128)
            pv = pT[b].rearrange("(c p) t -> p c t", p=128)
            AR.reset()
            u_pad = v3(AR.b(4 * UW), UW)
            dg = v3(AR.b(4 * NT), NT)
            HT_OFF = AR.off
            hT = v3(AR.b(16 * NT), NT)
            BF_BASE = AR.off
            B_u = Buf()
            B_dg = Buf()
            B_hT = [DBuf() for _ in TILES]
            hTdv = hTd[b].rearrange("(c p) t -> p c t", p=128)
            for ti, (t0, n) in enumerate(TILES):
                P.dma("sp", hT[:, :, t0:t0 + n], hTdv[:, :, t0:t0 + n], reads=[B_hTd[b]], writes=[B_hT[ti]])

            AR.reset(BF_BASE)
            wbuf = [v3(AR.b(16 * 256), 256) for _ in range(2)]
            B_wbuf = [Buf(), Buf()]
            stage = [AR.b(NT) for _ in range(3)]
            B_stage = [DBuf() for _ in range(3)]
            vst = [v3(AR.b(18 * 256), 256) for _ in range(2)]
            B_vst = [DBuf(), DBuf()]
            T = [AR.f(NT) for _ in range(4)]
            B_T = [Buf() for _ in range(4)]
            st_rr = [0]
            act_only[0] = True
            MSET("pool", u_pad, 0.0, [B_u])

            winv = win_in[l].rearrange("(c p) n -> p c n", p=128)
            grp = [0]

            def load_group(g):
                s_ = grp[0] % 2
                P.dma("pool", wbuf[s_], winv[:, :, g * 256:(g + 1) * 256], writes=[B_wbuf[s_]])
                grp[0] += 1
                return s_

            ntl = [len(TILES)]

            def proj_block(s_, o, m, handler, kv=False):
                tiles = TILES if (kv or not last) else TILES[:4]
                ntl[0] = len(tiles)
                for ti, (t0, n) in enumerate(tiles):
                    bi = next_bank()
                    P.group("pe", [mm(banks[bi][0:m, 0:n], wbuf[s_][:, kc, o:o + m], hT[:, kc, t0:t0 + n], kc == 0, kc == 15)
                                   for kc in range(16)], reads=[B_wbuf[s_], B_hT[ti]], writes=[BK[bi]])
                    handler(banks[bi][0:m, 0:n], ti, t0, n, BK[bi])

            def h_copy_T(k):
                def h(ps, ti, t0, n, bk):
                    evac_copy(T[k][:, t0:t0 + n], ps, [bk], [B_T[k]])
                return h

            def h_act_T(k, func):
                def h(ps, ti, t0, n, bk):
                    ACT(T[k][:, t0:t0 + n], ps, func, [bk], [B_T[k]])
                return h

            def h_act_ap(dst3, j, func, bufw):
                def h(ps, ti, t0, n, bk):
                    ACT(dst3[:, j, t0:t0 + n], ps, func, [bk], [bufw])
                return h

            def h_dram(row0, m, func=None):
                s_ = st_rr[0] % 3
                st_rr[0] += 1

                def h(ps, ti, t0, n, bk):
                    if func is None:
                        evac_copy(stage[s_][0:m, t0:t0 + n], ps, [bk], [B_stage[s_]])
                    else:
                        ACT(stage[s_][0:m, t0:t0 + n], ps, func, [bk], [B_stage[s_]])
                    if ti == ntl[0] - 1:
                        P.dma("sp", pT[b][row0:row0 + m, :], stage[s_][0:m, :], reads=[B_stage[s_]], writes=[B_pT[b]])
                return h

            KBDv = KBD[b].rearrange("p (h r k) -> p h r k", h=4, r=36)
            VBDv = VBD[b].rearrange("p (h r k) -> p h r k", h=4, r=36)

            def h_kbd(hp):
                s_ = st_rr[0] % 3
                st_rr[0] += 1

                def h(ps, ti, t0, n, bk):
                    evac_copy(stage[s_][:, t0:t0 + n], ps, [bk], [B_stage[s_]])
                    if ti == ntl[0] - 1:
                        for lo in (0, 64):
                            P.dma("sp", KBDv[lo:lo + 64, hp, :, lo:lo + 64], stage[s_][lo:lo + 64, :].rearrange("p (r k) -> p r k", k=64),
                                  reads=[B_stage[s_]], writes=[B_KBD[b]])
                return h

            def unit_ck(g, hp0):
                s = load_group(g)
                proj_block(s, 0, 128, h_kbd(hp0), kv=True)
                proj_block(s, 128, 128, h_kbd(hp0 + 1), kv=True)

            def unit_A(j):
                s = load_group(2 * j)
                proj_block(s, 0, 128, h_copy_T(0))
                proj_block(s, 128, 128, h_copy_T(1))
                s = load_group(2 * j + 1)
                proj_block(s, 0, 128, h_copy_T(2))
                proj_block(s, 128, 128, h_act_T(3, AF.Silu))
                TT("dve", T[0], T[0], T[1], ALU.mult, [B_T[1]], [B_T[0]])
                w0, w1, w2 = (vcol(V_ACW + j * 3 + k) for k in range(3))
                TS("dve", T[1], T[0], w1, ALU.mult, [B_T[0], B_lay], [B_T[1]])
                for (a, n_) in ((0, S), (S, CT)):
                    STT(T[1][:, a + 1:a + n_], T[0][:, a:a + n_ - 1], w0, T[1][:, a + 1:a + n_], ALU.mult, ALU.add, [B_T[0], B_lay], [B_T[1]])
                    STT(T[1][:, a:a + n_ - 1], T[0][:, a + 1:a + n_], w2, T[1][:, a:a + n_ - 1], ALU.mult, ALU.add, [B_T[0], B_lay], [B_T[1]])
                TT("dve", T[2], T[2], T[3], ALU.mult, [B_T[3]], [B_T[2]])
                ss = st_rr[0] % 3
                st_rr[0] += 1
                TT("dve", stage[ss], T[1], T[2], ALU.mult, [B_T[1], B_T[2]], [B_stage[ss]])
                P.dma("sp", yT[b][j * 128:(j + 1) * 128, :], stage[ss], reads=[B_stage[ss]], writes=[B_yT[b]])

            def unit_D(j):
                s = load_group(8 + j)
                proj_block(s, 0, 128, h_copy_T(0))
                proj_block(s, 128, 128, h_act_T(1, AF.Sigmoid))
                TT("dve", u_pad[:, j, U_LAT:U_LAT + S], T[0][:, 0:S], T[1][:, 0:S], ALU.mult, [B_T[0], B_T[1]], [B_u])
                TT("dve", u_pad[:, j, U_CTX:U_CTX + CT], T[0][:, S:NT], T[1][:, S:NT], ALU.mult, [B_T[0], B_T[1]], [B_u])

            def unit_dg(jj):
                s = load_group(12 + jj)
                for q in range(2):
                    proj_block(s, q * 128, 128, h_act_ap(dg, jj * 2 + q, AF.Silu, B_dg))

            def unit_pair(g, row0, func=None):
                s = load_group(g)
                proj_block(s, 0, 128, h_dram(row0, 128, func), kv=(row0 == R_KVA))
                proj_block(s, 128, 128, h_dram(row0 + 128, 128, func), kv=(row0 == R_KVA))

            def unit_qa2():
                s = load_group(15)
                proj_block(s, 0, 128, h_dram(R_QA + 256, 128))
                proj_block(s, 128, 64, h_dram(R_KPE, 64), kv=True)
                proj_block(s, 192, 64, h_dram(R_KSW, 64), kv=True)

            def unit_v(half):
                s = load_group(25 + half)
                for tt in range(18):
                    bi = next_bank()
                    P.group("pe", [mm(banks[bi][:, 0:256], hT[:, kc, tt * 128:(tt + 1) * 128], wbuf[s][:, kc, :], kc == 0, kc == 15)
                                   for kc in range(16)], reads=[B_wbuf[s], B_hT[min(tt // 4, 4)]], writes=[BK[bi]])
                    evac_copy(vst[half][:, tt, :], banks[bi][:, 0:256], [BK[bi]], [B_vst[half]])
                for rl in range(2):
                    for hl in range(4):
                        hh = 4 * half + hl
                        hp_, hf = hh // 2, hh % 2
                        dst = VBDv[hf * 64:(hf + 1) * 64, hp_, :, hf * 64:(hf + 1) * 64].rearrange("k (t r) d -> r k t d", r=2)[rl]
                        src = vst[half][rl * 64:(rl + 1) * 64, :, hl * 64:(hl + 1) * 64]
                        P.dma("sp", dst, src, reads=[B_vst[half]], writes=[B_VBD[b]])

            unit_A(0)
            unit_pair(14, R_QA)
            unit_qa2()
            unit_A(1)
            unit_pair(16, R_KVA)
            unit_pair(17, R_BG, AF.Silu)
            unit_A(2)
            unit_pair(18, R_BG + 256, AF.Silu)
            unit_pair(19, R_CQ)
            unit_A(3)
            unit_pair(20, R_CQ + 256)
            unit_ck(21, 0)
            unit_D(0)
            unit_ck(22, 2)
            unit_D(1)
            unit_pair(23, R_CG, AF.Silu)
            unit_D(2)
            unit_pair(24, R_CG + 256, AF.Silu)
            unit_D(3)
            unit_dg(0)
            unit_dg(1)
            unit_v(0)
            unit_v(1)
            assert grp[0] == 27
            act_only[0] = False
            P.barrier()

            MLA_PF = 147456
            AR.reset(MLA_PF)
            qa = v3(AR.b(3 * NT), NT)
            kva = v3(AR.b(2 * NT), NT)
            kpe = AR.b(NT)
            ksw = AR.b(NT)
            CC = AR.f(NT)
            SSn = AR.f(NT)
            B_ld, B_w, B_bg = Buf(), Buf(), Buf()
            B_cqt = [Buf() for _ in TILES]
            B_ckvt = [Buf() for _ in TILES]
            P.dma("sp", qa, pv[:, 0:3, :], reads=[B_pT[b]], writes=B_cqt)
            P.dma("sp", kva, pv[:, 4:6, :], reads=[B_pT[b]], writes=B_ckvt)
            P.dma("sp", kpe[0:64, :], pT[b][R_KPE:R_KPE + 64, :], reads=[B_pT[b]], writes=[B_ld])
            P.dma("sp", ksw[0:64, :], pT[b][R_KSW:R_KSW + 64, :], reads=[B_pT[b]], writes=[B_ld])
            P.dma("sp", CC[0:64, :], rope_in[0], writes=[B_ld])
            P.dma("sp", SSn[0:64, :], rope_in[1], writes=[B_ld])
            pt1 = [AR.f(512) for _ in range(2)]
            pt2 = AR.f(512)
            B_pt1 = [Buf(), Buf()]
            B_pt2 = Buf()
            assert AR.off <= NW * 4, AR.off
            B_kpe = Buf()
            MSET("pool", kpe[64:128, :], 0.0, [B_kpe])
            AR.reset(HT_OFF)
            diag = AR.b(4 * 31 * 128)
            wpw = v3(AR.b(4 * 512), 512)
            vb2 = [v3(AR.b(4 * 512), 512)] * 2
            sqb2 = [v3(AR.b(4 * 512), 512)] * 2
            zb2 = [v3(AR.b(4 * 512), 512) for _ in range(2)]
            ys = [v3(AR.b(4 * 512), 512) for _ in range(2)]
            B_diagc = [Buf() for _ in range(4)]
            B_wpw = Buf()
            B_vb2, B_sqb2, B_zb2 = [Buf()] * 2, [Buf()] * 2, [Buf(), Buf()]
            B_ys = [Buf(), Buf()]
            vf2 = [v3(AR.f(4 * 512), 512)] * 2
            B_vf2 = [Buf()] * 2
            st2 = [[AR.f(512) for _ in range(3)] + [None] for _ in range(2)]
            psq = [v3(AR.b(3 * 512), 512) for _ in range(2)]
            B_psq = [DBuf(), DBuf()]
            prqs = [AR.f(512) for _ in range(4)]
            B_prqs = [Buf() for _ in range(4)]
            B_mean2, B_var2 = [Buf(), Buf()], [Buf(), Buf()]
            dt2 = [AR.f(512) for _ in range(2)]
            B_dt2 = [Buf(), Buf()]
            P.dma("pool", wpw, wpw_in[l].rearrange("(c p) n -> p c n", p=128), writes=[B_wpw])
            for c in range(4):
                for k in range(31):
                    TS("dve", diag[:, (c * 31 + k) * 128:(c * 31 + k + 1) * 128], ident_f, dww[:, c * 31 + k:c * 31 + k + 1], ALU.mult,
                       [B_const, B_lay], [B_diagc[c]])
            W_PF = AR.off
            wuq = v3(AR.b(3 * 1024), 1024)
            wukv = v3(AR.b(2 * 1024), 1024)
            P.dma("pool", wuq, wuq_in[l].rearrange("(c p) n -> p c n", p=128), writes=[B_w])
            P.dma("pool", wukv, wukv_in[l].rearrange("(c p) n -> p c n", p=128), writes=[B_w])
            assert AR.off <= MLA_PF, AR.off
            dtiles = [(t0, n, t0) for (t0, n) in TILES[:4]] + ([] if last else [(S, CT, U_CTX - 15)])
            dti = [0]

            CB_ = [0, 1, 2, 3]

            def dconv_conv_pe(di):
                t0, n, uo = dtiles[di]
                for c in range(4):
                    bi = CB_[c]
                    P.group("pe", [mm(banks[bi][:, 0:n], diag[:, (c * 31 + k) * 128:(c * 31 + k + 1) * 128], u_pad[:, c, uo + k:uo + k + n], k == 0, k == 30)
                                   for k in range(31)], reads=[B_diagc[c], B_u], writes=[BK[bi]])

            def dconv_conv_evac(di):
                t0, n, uo = dtiles[di]
                p = di % 2
                vf, vb, sqb = vf2[p], vb2[p], sqb2[p]
                for c in range(4):
                    bi = CB_[c]
                    ACT(vf[:, c, 0:n], banks[bi][:, 0:n], AF.Identity, [BK[bi], B_lay], [B_vf2[p]], bias=vcol(V_DWB + c))
                    ACT(sqb[:, c, 0:n], banks[bi][:, 0:n], AF.Square, [BK[bi], B_lay], [B_sqb2[p]], bias=vcol(V_DWB + c))
                    P.op("dve", (lambda o, i_: (lambda e: e.tensor_copy(out=o, in_=i_)))(vb[:, c, 0:n], vf[:, c, 0:n]), reads=[B_vf2[p]], writes=[B_vb2[p]])

            def dconv_stats_pe(di):
                t0, n, uo = dtiles[di]
                p = di % 2
                P.group("pe", [mm(banks[4][:, 0:n], ones_b, vb2[p][:, c, 0:n], c == 0, c == 3) for c in range(4)], reads=[B_vb2[p], B_const], writes=[BK[4]])
                P.group("pe", [mm(banks[5][:, 0:n], ones_b, sqb2[p][:, c, 0:n], c == 0, c == 3) for c in range(4)], reads=[B_sqb2[p], B_const], writes=[BK[5]])

            def dconv_ln(di):
                t0, n, uo = dtiles[di]
                p = di % 2
                vf, zb = vf2[p], zb2[p]
                B_vf, B_zb = B_vf2[p], B_zb2[p]
                mean, msq, var, _unused = st2[p]
                B_mean, B_var = B_mean2[p], B_var2[p]
                TS("dve", mean[:, 0:n], banks[4][:, 0:n], 1.0 / 512, ALU.mult, [BK[4]], [B_mean])
                TT("dve", msq[:, 0:n], mean[:, 0:n], mean[:, 0:n], ALU.mult, [B_mean], [B_var])
                STT(var[:, 0:n], banks[5][:, 0:n], 1.0 / 512, msq[:, 0:n], ALU.mult, ALU.subtract, [BK[5], B_var], [B_var])
                ACT(var[:, 0:n], var[:, 0:n], AF.Sqrt, [B_var, B_const], [B_var], bias=eps_t)
                RCP(var[:, 0:n], var[:, 0:n], [B_var], [B_var])
                for c in range(4):
                    q = dti[0] % 2
                    dti[0] += 1
                    TT("dve", dt2[q][:, 0:n], vf[:, c, 0:n], mean[:, 0:n], ALU.subtract, [B_vf, B_mean], [B_dt2[q]])
                    TT("pool", dt2[q][:, 0:n], dt2[q][:, 0:n], var[:, 0:n], ALU.mult, [B_var], [B_dt2[q]])
                    ACT(zb[:, c, 0:n], dt2[q][:, 0:n], AF.Silu, [B_dt2[q], B_lay], [B_zb], bias=vcol(V_LNB + c), scale=vcol(V_LNG + c))

            def dconv_pw(di):
                t0, n, uo = dtiles[di]
                p = di % 2
                ysb, zb = ys[p], zb2[p]
                for co in range(4):
                    bi = 6 + co % 2
                    P.group("pe", [mm(banks[bi][:, 0:n], wpw[:, ci, co * 128:(co + 1) * 128], zb[:, ci, 0:n], ci == 0, ci == 3) for ci in range(4)],
                            reads=[B_wpw, B_zb2[p]], writes=[BK[bi]])
                    STT(ysb[:, co, 0:n], banks[bi][:, 0:n], vcol(V_PWB + co), dg[:, co, t0:t0 + n], ALU.add, ALU.mult,
                        [BK[bi], B_dg, B_lay], [B_ys[p]])
                P.dma("sp", yTv[:, 12:16, t0:t0 + n], ysb[:, :, 0:n], reads=[B_ys[p]], writes=[B_yT[b]])

            ptr = [0]

            def prep_srcs(ti):
                lst = []
                for k, (src, nchunk, Bs, gidx, dim) in enumerate(((qa, 3, B_cqt[ti], V_QNG, 384.0), (kva, 2, B_ckvt[ti], V_KVG, 256.0))):
                    if last and ti == 4 and k == 0:
                        continue
                    lst.append((k, src, nchunk, Bs, gidx, dim))
                return lst

            def prep_sq(ti):
                t0, n = TILES[ti]
                for (k, src, nchunk, Bs, gidx, dim) in prep_srcs(ti):
                    for c in range(nchunk):
                        ACT(psq[k][:, c, 0:n], src[:, c, t0:t0 + n], AF.Square, [Bs], [B_psq[k]])

            def prep_mm(ti):
                t0, n = TILES[ti]
                for (k, src, nchunk, Bs, gidx, dim) in prep_srcs(ti):
                    bi = 6 + k
                    rq, B_rq = prqs[(ti % 2) * 2 + k], B_prqs[(ti % 2) * 2 + k]
                    P.group("pe", [mm(banks[bi][:, 0:n], ones_b, psq[k][:, c, 0:n], c == 0, c == nchunk - 1) for c in range(nchunk)],
                            reads=[B_psq[k], B_const], writes=[BK[bi]])
                    ACT(rq[:, 0:n], banks[bi][:, 0:n], AF.Sqrt, [BK[bi], B_const], [B_rq], bias=eps_t, scale=1.0 / dim)

            def prep_fin(ti):
                t0, n = TILES[ti]
                for (k, src, nchunk, Bs, gidx, dim) in prep_srcs(ti):
                    rq, B_rq = prqs[(ti % 2) * 2 + k], B_prqs[(ti % 2) * 2 + k]
                    RCP(rq[:, 0:n], rq[:, 0:n], [B_rq], [B_rq])
                    for c in range(nchunk):
                        s_ = ptr[0] % 2
                        ptr[0] += 1
                        TT("dve", pt1[s_][:, 0:n], src[:, c, t0:t0 + n], rq[:, 0:n], ALU.mult, [Bs, B_rq], [B_pt1[s_]])
                        ACT(src[:, c, t0:t0 + n], pt1[s_][:, 0:n], AF.Identity, [B_pt1[s_], B_lay], [Bs], scale=vcol(gidx + c))
                s_ = ptr[0] % 2
                ptr[0] += 1
                TT("dve", pt1[s_][0:64, 0:n], kpe[0:64, t0:t0 + n], CC[0:64, t0:t0 + n], ALU.mult, [B_ld, B_kpe], [B_pt1[s_]])
                TT("pool", pt2[0:64, 0:n], ksw[0:64, t0:t0 + n], SSn[0:64, t0:t0 + n], ALU.mult, [B_ld], [B_pt2])
                TT("dve", kpe[0:64, t0:t0 + n], pt1[s_][0:64, 0:n], pt2[0:64, 0:n], ALU.add, [B_pt1[s_], B_pt2, B_ld], [B_kpe])

            ndt = len(dtiles)
            dconv_conv_pe(0)
            dconv_conv_evac(0)
            prep_sq(0)
            for di in range(ndt):
                dconv_stats_pe(di)
                if di >= 1:
                    dconv_pw(di - 1)
                prep_mm(di)
                if di + 1 < ndt:
                    dconv_conv_pe(di + 1)
                dconv_ln(di)
                if di >= 1:
                    prep_fin(di - 1)
                if di + 1 < ndt:
                    dconv_conv_evac(di + 1)
                if di + 1 < len(TILES):
                    prep_sq(di + 1)
            dconv_pw(ndt - 1)
            prep_fin(ndt - 1)
            for ti in range(ndt, len(TILES)):
                prep_mm(ti)
                prep_fin(ti)
            P.barrier()

            AR.reset()
            bg = v3(AR.b(4 * NT), NT)
            P.dma("sp", bg, pv[:, 6:10, :], reads=[B_pT[b]], writes=[B_bg])
            kn = v3(AR.b(4 * NT), NT)
            vt = v3(AR.b(18 * 512), 512)
            qn = v3(AR.b(4 * NT), NT)
            qr = v3(AR.b(4 * NT), NT)
            kr, B_kr = kpe, B_kpe
            PTb = [AR.b(512) for _ in range(4)]
            ysm = [AR.b(512) for _ in range(2)]
            B_qr = Buf()
            B_kn, B_vt, B_qn = DBuf(), DBuf(), DBuf()
            B_PT = [Buf() for _ in range(4)]
            B_ysm = [Buf(), Buf()]
            t1 = [AR.f(512) for _ in range(2)]
            t2 = [AR.f(512) for _ in range(2)]
            B_t1 = [Buf(), Buf()]
            B_t2 = [Buf(), Buf()]
            rc = [AR.f(512) for _ in range(2)]
            B_rc = [Buf(), Buf()]
            assert AR.off <= W_PF, (AR.off, W_PF)
            MSET("dve", qr, 0.0, [B_qr])
            tr = 0
            for h in range(4):
                for ti, (t0, n) in enumerate(TILES):
                    bi = next_bank()
                    P.group("pe", [mm(banks[bi][:, 0:n], wukv[:, kc, h * 128:(h + 1) * 128], kva[:, kc, t0:t0 + n], kc == 0, kc == 1) for kc in range(2)],
                            reads=[B_w, B_ckvt[ti]], writes=[BK[bi]])
                    evac_copy(kn[:, h, t0:t0 + n], banks[bi][:, 0:n], [BK[bi]], [B_kn])
                    if last and ti == 4:
                        continue
                    bi = next_bank()
                    P.group("pe", [mm(banks[bi][:, 0:n], wuq[:, kc, h * 128:(h + 1) * 128], qa[:, kc, t0:t0 + n], kc == 0, kc == 2) for kc in range(3)],
                            reads=[B_w, B_cqt[ti]], writes=[BK[bi]])
                    evac_copy(qn[:, h, t0:t0 + n], banks[bi][:, 0:n], [BK[bi]], [B_qn])
                    b1 = next_bank()
                    P.group("pe", [mm(banks[b1][0:64, 0:n], wuq[:, kc, 512 + h * 64:512 + (h + 1) * 64], qa[:, kc, t0:t0 + n], kc == 0, kc == 2) for kc in range(3)],
                            reads=[B_w, B_cqt[ti]], writes=[BK[b1]])
                    b2 = next_bank()
                    P.group("pe", [mm(banks[b2][0:64, 0:n], wuq[:, kc, 768 + h * 64:768 + (h + 1) * 64], qa[:, kc, t0:t0 + n], kc == 0, kc == 2) for kc in range(3)],
                            reads=[B_w, B_cqt[ti]], writes=[BK[b2]])
                    s = tr % 2
                    tr += 1
                    TT("dve", t1[s][0:64, 0:n], banks[b1][0:64, 0:n], CC[0:64, t0:t0 + n], ALU.mult, [BK[b1], B_ld], [B_t1[s]])
                    TT("dve", t2[s][0:64, 0:n], banks[b2][0:64, 0:n], SSn[0:64, t0:t0 + n], ALU.mult, [BK[b2], B_ld], [B_t2[s]])
                    TT("pool", qr[0:64, h, t0:t0 + n], t1[s][0:64, 0:n], t2[s][0:64, 0:n], ALU.add, [B_t1[s], B_t2[s]], [B_qr])
            for tt in range(18):
                bi = next_bank()
                P.group("pe", [mm(banks[bi][:, :], kva[:, kc, tt * 128:(tt + 1) * 128], wukv[:, kc, 512:1024], kc == 0, kc == 1) for kc in range(2)],
                        reads=[B_w, B_ckvt[min(tt // 4, 4)]], writes=[BK[bi]])
                evac_copy(vt[:, tt, :], banks[bi][:, :], [BK[bi]], [B_vt])
            ai = 0
            for h in range(4):
                for qi, (q0, n) in enumerate(TILES):
                    if qi == 4 and last:
                        continue
                    kts = [16, 17] + (list(range(16)) if qi < 4 else [])
                    ob, db = 3 + ai % 2, 5 + ai % 2
                    s = ai % 2
                    ai += 1

                    SBK = (0, 1, 2, 7)

                    def score(j):
                        kt = kts[j]
                        sb = SBK[j % 4]
                        P.group("pe", [mm(banks[sb][:, 0:n], kn[:, h, kt * 128:(kt + 1) * 128], qn[:, h, q0:q0 + n], True, False),
                                       mm(banks[sb][:, 0:n], kr[:, kt * 128:(kt + 1) * 128], qr[:, h, q0:q0 + n], False, True)],
                                reads=[B_kn, B_kr, B_qn, B_qr], writes=[BK[sb]])
                    score(0)
                    if len(kts) > 1:
                        score(1)
                    for j in range(len(kts)):
                        if j + 2 < len(kts):
                            score(j + 2)
                        sb = SBK[j % 4]
                        pb = j % 4
                        kt = kts[j]
                        ACT(PTb[pb][:, 0:n], banks[sb][:, 0:n], AF.Exp, [BK[sb]], [B_PT[pb]], scale=MLA_SCALE)
                        first, lastk = (j == 0), (j == len(kts) - 1)
                        P.group("pe", [mm(banks[ob][:, 0:n], vt[:, kt, h * 128:(h + 1) * 128], PTb[pb][:, 0:n], first, lastk),
                                       mm(banks[db][:, 0:n], ones_b, PTb[pb][:, 0:n], first, lastk)],
                                reads=[B_vt, B_PT[pb], B_const], writes=[BK[ob], BK[db]])
                    RCP(rc[s][:, 0:n], banks[db][:, 0:n], [BK[db]], [B_rc[s]])
                    TT("dve", rc[s][:, 0:n], banks[ob][:, 0:n], rc[s][:, 0:n], ALU.mult, [BK[ob]], [B_rc[s]])
                    TT("pool", ysm[s][:, 0:n], rc[s][:, 0:n], bg[:, h, q0:q0 + n], ALU.mult, [B_rc[s], B_bg], [B_ysm[s]])
                    P.dma("sp", yT[b][512 + h * 128:512 + (h + 1) * 128, q0:q0 + n], ysm[s][:, 0:n], reads=[B_ysm[s]], writes=[B_yT[b]])
            P.barrier()

            AR.reset()
            cq = v3(AR.b(4 * NT), NT)
            cg = v3(AR.b(4 * NT), NT)
            kbd = AR.b(4 * 36 * 128)
            vbd = AR.b(4 * 36 * 128)
            ubb = v3(AR.b(4 * 960), 960)
            PTn = [AR.b(512) for _ in range(4)]
            ysn = [AR.b(512) for _ in range(2)]
            B_nl, B_kbd, B_vbd, B_ub = Buf(), Buf(), Buf(), Buf()
            B_PTn = [Buf() for _ in range(4)]
            B_ysn = [Buf(), Buf()]
            rcn = [AR.f(512) for _ in range(2)]
            B_rcn = [Buf(), Buf()]
            kbd4 = kbd.rearrange("p (h r k) -> p h r k", h=4, r=36)
            vbd4 = vbd.rearrange("p (h r k) -> p h r k", h=4, r=36)
            kbdh = kbd.rearrange("p (h q) -> p h q", h=4)
            vbdh = vbd.rearrange("p (h q) -> p h q", h=4)
            KBDh = KBD[b].rearrange("p (h q) -> p h q", h=4)
            VBDh = VBD[b].rearrange("p (h q) -> p h q", h=4)
            B_kbdh = [Buf() for _ in range(4)]
            B_vbdh = [Buf() for _ in range(4)]
            B_cqh = [Buf() for _ in range(4)]
            B_cgl = Buf()
            ubf = v3(AR.f(4 * 960), 960)
            B_ubf = Buf()
            P.dma("sp", ubf, ub_in[l].rearrange("h p n -> p h n"), writes=[B_ubf])
            ACT(ubb, ubf, AF.Copy, [B_ubf], [B_ub], scale=8.0)
            for hp in range(4):
                P.dma("sp", kbdh[:, hp, :], KBDh[:, hp, :], reads=[B_KBD[b]], writes=[B_kbdh[hp]])
                P.dma("sp", cq[:, hp, :], pv[:, 10 + hp, :], reads=[B_pT[b]], writes=[B_cqh[hp]])
                P.dma("sp", vbdh[:, hp, :], VBDh[:, hp, :], reads=[B_VBD[b]], writes=[B_vbdh[hp]])
            P.dma("sp", cg, pv[:, 18:22, :], reads=[B_pT[b]], writes=[B_cgl])
            ai = 0
            qblocks = [(m * 512, 512, m) for m in range(4)] + ([] if last else [(S, CT, -1)])
            for hp in range(4):
                for (q0, n, m) in qblocks:
                    ob, db = 3 + ai % 2, 5 + ai % 2
                    s = ai % 2
                    ai += 1
                    items = [(32 + r, 0, n, None) for r in range(4)]
                    if m >= 0:
                        for kr_ in range(32):
                            qa_ = max(8 * m, Qr[kr_][0])
                            qb_ = min(8 * m + 8, Qr[kr_][1])
                            if qb_ > qa_:
                                j0 = 7 - kr_ + qa_
                                assert 0 <= j0 and j0 + (qb_ - qa_) <= 15
                                items.append((kr_, (qa_ - 8 * m) * 64, (qb_ - qa_) * 64, j0))

                    SBK = (0, 1, 2, 7)

                    def nscore(j):
                        row, c0, w, j0 = items[j]
                        sb = SBK[j % 4]
                        fns = []
                        if j0 is not None:
                            fns.append(mm(banks[sb][:, 0:w], ident_b, ubb[:, hp, j0 * 64:j0 * 64 + w], True, False))
                        fns.append(mm(banks[sb][:, 0:w], kbd4[:, hp, row, :], cq[:, hp, q0 + c0:q0 + c0 + w], j0 is None, True))
                        P.group("pe", fns, reads=[B_kbdh[hp], B_cqh[hp], B_ub, B_const], writes=[BK[sb]])
                    nscore(0)
                    if len(items) > 1:
                        nscore(1)
                    for j in range(len(items)):
                        if j + 2 < len(items):
                            nscore(j + 2)
                        row, c0, w, j0 = items[j]
                        sb = SBK[j % 4]
                        pb = j % 4
                        ACT(PTn[pb][:, 0:w], banks[sb][:, 0:w], AF.Exp, [BK[sb]], [B_PTn[pb]], scale=NA_SCALE)
                        first, lastk = (j == 0), (j == len(items) - 1)
                        P.group("pe", [mm(banks[ob][:, c0:c0 + w], vbd4[:, hp, row, :], PTn[pb][:, 0:w], first, lastk),
                                       mm(banks[db][:, c0:c0 + w], onesbd_b, PTn[pb][:, 0:w], first, lastk)],
                                reads=[B_vbdh[hp], B_PTn[pb], B_const], writes=[BK[ob], BK[db]])
                    RCP(rcn[s][:, 0:n], banks[db][:, 0:n], [BK[db]], [B_rcn[s]])
                    TT("dve", rcn[s][:, 0:n], banks[ob][:, 0:n], rcn[s][:, 0:n], ALU.mult, [BK[ob]], [B_rcn[s]])
                    TT("pool", ysn[s][:, 0:n], rcn[s][:, 0:n], cg[:, hp, q0:q0 + n], ALU.mult, [B_rcn[s], B_cgl], [B_ysn[s]])
                    P.dma("sp", yT[b][1024 + hp * 128:1024 + (hp + 1) * 128, q0:q0 + n], ysn[s][:, 0:n], reads=[B_ysn[s]], writes=[B_yT[b]])
            P.barrier()

            AR.reset()
            wout = v3(AR.b(16 * 2048), 2048)
            yt = [v3(AR.b(16 * 512), 512) for _ in range(2)]
            sqm = v3(AR.b(16 * 512), 512)
            B_woutg = [Buf() for _ in range(4)]
            B_sqm = DBuf()
            B_yt = [Buf(), Buf()]
            xts = [v3(AR.f(16 * 512), 512) for _ in range(2)]
            B_xts = [Buf(), Buf()]
            rsm = AR.f(512)
            B_rsm = Buf()
            if last:
                osbs = [AR.f(2048) for _ in range(2)]
                B_osbs = [DBuf(), DBuf()]
            else:
                m_ht = v3(AR.b(16 * 512), 512)
                B_mht = DBuf()
                m_tmp = [AR.f(512) for _ in range(2)]
                B_mtmp = [Buf(), Buf()]
            woutv = wout_in[l].rearrange("(c p) n -> p c n", p=128)
            for g in range(4):
                P.dma("pool", wout[:, :, g * 512:(g + 1) * 512], woutv[:, :, g * 512:(g + 1) * 512], writes=[B_woutg[g]])
            mtiles = TILES[:4] if last else TILES
            oi = [0]

            def merge_loads(ti_):
                t0_, n_ = mtiles[ti_]
                P.dma("sp", yt[ti_ % 2][:, :, 0:n_], yTv[:, :, t0_:t0_ + n_], reads=[B_yT[b]], writes=[B_yt[ti_ % 2]])
                P.dma("sp", xts[ti_ % 2][:, :, 0:n_], XTv[:, :, t0_:t0_ + n_], reads=[B_XT[b]], writes=[B_xts[ti_ % 2]])

            def merge_block(ti_, db_):
                t0_, n_ = mtiles[ti_]
                mi = b if ti_ < 4 else 2
                xt, B_xt = xts[ti_ % 2], B_xts[ti_ % 2]
                bi = next_bank()
                P.group("pe", [mm(banks[bi][:, 0:n_], wout[:, kc, db_ * 128:(db_ + 1) * 128], yt[ti_ % 2][:, kc, 0:n_], kc == 0, kc == 15) for kc in range(16)],
                        reads=[B_woutg[db_ // 4], B_yt[ti_ % 2]], writes=[BK[bi]])
                STT(xt[:, db_, 0:n_], banks[bi][:, 0:n_], gtv(db_, mi, l), xt[:, db_, 0:n_], ALU.mult, ALU.add, [BK[bi], B_mod], [B_xt])

            def fin_squares(ti_):
                t0_, n_ = mtiles[ti_]
                xt, B_xt = xts[ti_ % 2], B_xts[ti_ % 2]
                for c in range(16):
                    ACT(sqm[:, c, 0:n_], xt[:, c, 0:n_], AF.Square, [B_xt], [B_sqm])

            def fin_stats(ti_):
                t0_, n_ = mtiles[ti_]
                bi = 7
                P.group("pe", [mm(banks[bi][:, 0:n_], ones_b, sqm[:, c, 0:n_], c == 0, c == 15) for c in range(16)], reads=[B_sqm, B_const], writes=[BK[bi]])
                ACT(rsm[:, 0:n_], banks[bi][:, 0:n_], AF.Sqrt, [BK[bi], B_const], [B_rsm], bias=eps_t, scale=1.0 / D)
                RCP(rsm[:, 0:n_], rsm[:, 0:n_], [B_rsm], [B_rsm])

            def fin_scale(ti_, c):
                t0_, n_ = mtiles[ti_]
                xt, B_xt = xts[ti_ % 2], B_xts[ti_ % 2]
                TT("pool", xt[:, c, 0:n_], xt[:, c, 0:n_], rsm[:, 0:n_], ALU.mult, [B_rsm], [B_xt])
                ACT(xt[:, c, 0:n_], xt[:, c, 0:n_], AF.Identity, [B_const], [B_xt], scale=fng[:, c:c + 1])

            def fin_out(ti_):
                t0_, n_ = mtiles[ti_]
                xt, B_xt = xts[ti_ % 2], B_xts[ti_ % 2]
                for sub in range(n_ // 128):
                    o_ = oi[0] % 2
                    oi[0] += 1
                    for g in range(4):
                        bi = next_bank()
                        fns = [TRN(banks[bi][:, j * 128:(j + 1) * 128], xt[:, g * 4 + j, sub * 128:(sub + 1) * 128]) for j in range(4)]
                        P.group("pe", fns, reads=[B_xt, B_const], writes=[BK[bi]])
                        evac_copy(osbs[o_][:, g * 512:(g + 1) * 512], banks[bi][:, :], [BK[bi]], [B_osbs[o_]])
                    P.dma("sp", out[b, t0_ + sub * 128:t0_ + (sub + 1) * 128, :], osbs[o_], reads=[B_osbs[o_]], writes=[])

            merge_loads(0)
            if not last:
                nt0 = len(mtiles)

                def nargs(ti_):
                    t0_, n_ = mtiles[ti_]
                    return (xts[ti_ % 2], B_xts[ti_ % 2], n_)
                merge_loads(1)
                for db_ in range(16):
                    merge_block(0, db_)
                for ti in range(nt0):
                    t0, n = mtiles[ti]
                    mi_ = b if ti < 4 else 2
                    xs_, B_xs_, _n = nargs(ti)
                    P.dma("sp", XTv[:, :, t0:t0 + n], xs_[:, :, 0:n], reads=[B_xs_], writes=[B_XT[b]])
                    norm_squares(xs_, B_xs_, n, sqm, B_sqm, True)
                    if ti + 1 < nt0:
                        for db_ in range(16):
                            merge_block(ti + 1, db_)
                            if db_ == 1:
                                norm_stats(n, sqm, B_sqm, rsm, B_rsm)
                            if 2 <= db_ <= 9:
                                for c in (2 * (db_ - 2), 2 * (db_ - 2) + 1):
                                    norm_chunk(xs_, B_xs_, n, c, mi_, l + 1, rsm, B_rsm, m_tmp, B_mtmp, m_ht, B_mht)
                            if db_ == 9 and ti + 2 < nt0:
                                merge_loads(ti + 2)
                    else:
                        norm_stats(n, sqm, B_sqm, rsm, B_rsm)
                        for c in range(16):
                            norm_chunk(xs_, B_xs_, n, c, mi_, l + 1, rsm, B_rsm, m_tmp, B_mtmp, m_ht, B_mht)
                    norm_store(n, b, t0, m_ht, B_mht)
            else:
                nt_ = len(mtiles)
                merge_loads(1)
                for db_ in range(16):
                    merge_block(0, db_)
                for ti in range(nt_):
                    nxt = ti + 1 < nt_
                    fin_squares(ti)
                    if nxt:
                        for db_ in range(16):
                            merge_block(ti + 1, db_)
                            if db_ == 3:
                                fin_stats(ti)
                            if db_ >= 4:
                                fin_scale(ti, db_ - 4)
                        for c in range(12, 16):
                            fin_scale(ti, c)
                    else:
                        fin_stats(ti)
                        for c in range(16):
                            fin_scale(ti, c)
                    fin_out(ti)
                    if ti + 2 < nt_:
                        merge_loads(ti + 2)
            P.barrier()

    P.barrier()
    P.emit(nc, None, sems)
    es.close()
    return nc


def _fm(v, nchunk):
    return np.ascontiguousarray(np.asarray(v, np.float32).reshape(nchunk, 128).T)


def _host_layout(inp):
    f32 = np.float32
    w_in = inp["w_in"]
    o = dict(a_x=0, a_b=512, a_c=1024, a_g=1536, b_qa=2048, b_kva=2432, b_kpe=2688, b_g=2752,
             c_q=3264, c_k=3776, c_v=4288, c_g=4800, glu_a=5312, glu_g=5824, d_g=6336)
    cols = []

    def blk(name, j, w=128):
        cols.extend(range(o[name] + j * w, o[name] + (j + 1) * w))
    for j in range(4):
        blk("a_x", j); blk("a_c", j); blk("a_b", j); blk("a_g", j)
    for j in range(4):
        blk("glu_a", j); blk("glu_g", j)
    for j in range(4):
        blk("d_g", j)
    for j in range(3):
        blk("b_qa", j)
    cols.extend(range(o["b_kpe"], o["b_kpe"] + 64))
    cols.extend(list(range(o["b_kpe"] + 32, o["b_kpe"] + 64)) + list(range(o["b_kpe"], o["b_kpe"] + 32)))
    for j in range(2):
        blk("b_kva", j)
    for j in range(4):
        blk("b_g", j)
    for nm in ("c_q", "c_k", "c_g", "c_v"):
        for j in range(4):
            blk(nm, j)
    cols = np.asarray(cols)
    assert cols.size == 6912
    win_r = np.ascontiguousarray(w_in[:, :, cols])
    cq = []
    for h in range(4):
        cq.extend(range(h * 192, h * 192 + 128))
    for h in range(4):
        cq.extend(range(h * 192 + 128, h * 192 + 192))
    for h in range(4):
        cq.extend(list(range(h * 192 + 160, h * 192 + 192)) + list(range(h * 192 + 128, h * 192 + 160)))
    wuq_r = np.ascontiguousarray(inp["mla_w_uq"][:, :, np.asarray(cq)])
    ckv = []
    for h in range(4):
        ckv.extend(range(h * 256, h * 256 + 128))
    for h in range(4):
        ckv.extend(range(h * 256 + 128, h * 256 + 256))
    wukv_r = np.ascontiguousarray(inp["mla_w_ukv"][:, :, np.asarray(ckv)])
    vecs = np.zeros((L, 128, 64), f32)
    dww = np.zeros((L, 128, 124), f32)
    ng3 = np.zeros((L, 128, 48), f32)
    bada3 = np.zeros((L, 128, 144), f32)
    for l in range(L):
        acw = inp["a_conv_w"][l]
        for c in range(4):
            for k in range(3):
                vecs[l, :, c * 3 + k] = acw[k, c * 128:(c + 1) * 128]
        vecs[l, :, 12:16] = _fm(inp["d_dw_b"][l], 4)
        vecs[l, :, 16:20] = _fm(inp["d_ln_g"][l], 4)
        vecs[l, :, 20:24] = _fm(inp["d_ln_b"][l], 4)
        vecs[l, :, 24:28] = _fm(inp["d_pw_b"][l], 4)
        vecs[l, :, 28:31] = _fm(inp["mla_q_norm"][l], 3)
        vecs[l, :, 31:33] = _fm(inp["mla_kv_norm"][l], 2)
        dw = inp["d_dw_w"][l]
        for c in range(4):
            dww[l, :, c * 31:(c + 1) * 31] = dw[:, c * 128:(c + 1) * 128].T
        ng3[l] = np.repeat(_fm(inp["norm_g"][l], 16), 3, axis=1)
        bada3[l] = np.repeat(_fm(inp["b_ada"][l], 48), 3, axis=1)
    fng = _fm(inp["final_norm_g"], 16)
    t = np.arange(S)
    row = (t // GW).astype(f32)
    col = (t % GW).astype(f32)
    freqs = (np.float32(10000.0) ** (-(np.arange(16, dtype=f32) * np.float32(2.0) / np.float32(32)))).astype(f32)
    ang = np.concatenate([row[:, None] * freqs, col[:, None] * freqs], axis=-1).astype(f32)
    cos, sin = np.cos(ang).astype(f32), np.sin(ang).astype(f32)
    rope = np.zeros((2, 64, NT), f32)
    rope[0, :, S:] = 1.0
    rope[0, 0:32, :S] = cos.T
    rope[0, 32:64, :S] = cos.T
    rope[1, 0:32, :S] = -sin.T
    rope[1, 32:64, :S] = sin.T
    rpb = inp["na_rpb"]
    kc = np.arange(64)[:, None]
    qc = np.arange(64)[None, :]
    cstart = np.clip(qc - 8, 0, 48)
    valid = (kc >= cstart) & (kc < cstart + 16)
    dc = np.clip(kc - qc + 15, 0, 30)
    ub = np.full((L, 4, 128, 15, 64), NEG, f32)
    for l in range(L):
        for h in range(8):
            for j in range(15):
                dr = 7 - j
                tab = rpb[l, h, dr + 7][dc]
                ub[l, h // 2, (h % 2) * 64:(h % 2) * 64 + 64, j, :] = np.where(valid, tab, np.float32(NEG))
    ub = ub.reshape(L, 4, 128, 960)
    ident = np.eye(128, dtype=f32)
    onesbd = np.zeros((128, 128), f32)
    onesbd[0:64, 0:64] = 1.0
    onesbd[64:128, 64:128] = 1.0
    shared = dict(ng3=ng3, bada3=bada3, w_ada=np.ascontiguousarray(inp["w_ada"], f32), win_r=win_r,
                  w_out=np.ascontiguousarray(inp["w_out"], f32), wuq_r=wuq_r, wukv_r=wukv_r,
                  d_pw_w=np.ascontiguousarray(inp["d_pw_w"], f32), vecs=vecs, dww=dww, fng=fng, rope=rope, ub=ub,
                  ident=ident, onesbd=onesbd)
    in_maps = []
    for core in range(NCORES):
        bs = [core * NB + i for i in range(NB)]
        cT = np.zeros((128, 16, 3), f32)
        for i, bb in enumerate(bs):
            cT[:, :, i] = _fm(inp["c"][bb], 16)
        cT[:, :, 2] = _fm(inp["c_ctx"], 16)
        m = dict(shared)
        m["x"] = np.ascontiguousarray(inp["x"][bs[0]:bs[0] + NB], f32)
        m["ctx"] = np.ascontiguousarray(inp["ctx"][bs[0]:bs[0] + NB], f32)
        m["cT"] = cT.reshape(128, 48)
        in_maps.append(m)
    return in_maps


_NC_CACHE = {}


def kernel(**inputs):
    inp = {k: np.asarray(v) for k, v in inputs.items()}
    in_maps = _host_layout(inp)
    if "nc" not in _NC_CACHE:
        _NC_CACHE["nc"] = build_nc()
    nc = _NC_CACHE["nc"]
    res = run_bass_kernel_spmd(nc, in_maps, core_ids=list(range(NCORES)))
    outs = [np.asarray(r["out"]) for r in res.results]
    return np.concatenate(outs, axis=0).astype(np.float32)
```

```python
import numpy as np
import concourse.bass as bass
import concourse.mybir as mybir
from concourse.bass_utils import run_bass_kernel_spmd

F32 = mybir.dt.float32
BF16 = mybir.dt.bfloat16
AF = mybir.ActivationFunctionType
ALU = mybir.AluOpType

NCORES = 8
D = 2048
S = 2048
CT = 256
NT = S + CT
L = 2
NB = 2
GW = 64
EPS = 1e-6
MLA_SCALE = float(192 ** -0.5)
NA_SCALE = 0.125
NEG = -30000.0
TILES = [(0, 512), (512, 512), (1024, 512), (1536, 512), (2048, 256)]
UW = 2368
U_LAT = 15
U_CTX = 15 + 2048 + 30

R_QA, R_KPE, R_KSW, R_KVA, R_BG, R_CQ, R_CK, R_CG = 0, 384, 448, 512, 768, 1280, 1792, 2304
PT_ROWS = 2816

DEBUG = False


class Buf:
    __slots__ = ("w", "r", "pr", "strict", "disjoint")

    def __init__(self, strict=False, disjoint=False):
        self.w = {}
        self.r = {}
        self.pr = {}
        self.strict = strict
        self.disjoint = disjoint


def DBuf():
    return Buf(disjoint=True)


COMPUTE = ("pe", "act", "dve", "pool")
STRICT_SAME_ENGINE = True


class Prog:
    def __init__(self, ndma=20):
        self.ops = {e: [] for e in ("pe", "act", "dve", "pool", "sp")}
        self.cnt = {e: 0 for e in COMPUTE}
        self.seen = {e: {} for e in self.ops}
        self.issued = {}
        self.ndma = ndma
        self.dma_rr = {"sp": 0, "pool": 0}
        self.dma_val = {}

    def _waits(self, eng, reads, writes, extra):
        need = {}

        def add(d, strict):
            for k, v in d.items():
                if k == eng and not strict and (eng == "pe" or not STRICT_SAME_ENGINE):
                    continue
                if need.get(k, 0) < v:
                    need[k] = v

        for b in reads:
            add(b.w, b.strict)
        for b in writes:
            if not b.disjoint:
                add(b.w, b.strict)
            else:
                add(b.pr, b.strict)
            add(b.r, b.strict)
        for t in extra:
            if t is not None:
                if need.get(t[0], 0) < t[1]:
                    need[t[0]] = t[1]
        out = []
        sn = self.seen[eng]
        for k, v in need.items():
            if sn.get(k, 0) < v:
                sn[k] = v
                out.append((k, v))
        return out

    def _commit(self, tok, reads, writes):
        k, v = tok
        for b in reads:
            if b.r.get(k, 0) < v:
                b.r[k] = v
        for b in writes:
            if b.r:
                b.pr = b.r
                b.r = {}
                b.w = {}
            if b.w.get(k, 0) < v:
                b.w[k] = v
        if self.issued.get(k, 0) < v:
            self.issued[k] = v

    def op(self, eng, fn, reads=(), writes=(), extra=(), sig=True):
        wl = self._waits(eng, reads, writes, extra)
        tok = None
        if sig:
            self.cnt[eng] += 1
            tok = (eng, self.cnt[eng])
            self._commit(tok, reads, writes)
        self.ops[eng].append((fn, wl, 1 if sig else 0, eng))
        return tok

    def group(self, eng, fns, reads=(), writes=(), extra=()):
        wl = self._waits(eng, reads, writes, extra)
        self.cnt[eng] += 1
        tok = (eng, self.cnt[eng])
        self._commit(tok, reads, writes)
        n = len(fns)
        for i, fn in enumerate(fns):
            self.ops[eng].append((fn, wl if i == 0 else [], 1 if i == n - 1 else 0, eng))
        return tok

    def dma(self, q, out, in_, reads=(), writes=(), extra=()):
        i = self.dma_rr[q]
        self.dma_rr[q] = (i + 1) % self.ndma
        key = ("dma", q, i)
        prev = self.dma_val.get(key, 0)
        ex = list(extra)
        if prev:
            ex.append((key, prev))
        wl = self._waits(q, reads, writes, ex)
        val = prev + 16
        self.dma_val[key] = val
        tok = (key, val)
        self._commit(tok, reads, writes)
        self.ops[q].append((lambda e, o=out, i_=in_: e.dma_start(out=o, in_=i_), wl, 16, key))
        return tok

    def barrier(self):
        snap = dict(self.issued)
        for eng in self.ops:
            if eng == "pe":
                continue
            wl = []
            sn = self.seen[eng]
            for k, v in snap.items():
                if k == eng:
                    continue
                if sn.get(k, 0) < v:
                    sn[k] = v
                    wl.append((k, v))
            if wl:
                self.ops[eng].append((None, wl, 0, eng))

    def emit(self, nc, engines, sems):
        with nc.Block() as block:
            def run(name):
                def body(e):
                    for fn, wl, inc, key in self.ops[name]:
                        for k, v in wl:
                            e.wait_ge(sems[k], v)
                        if fn is None:
                            continue
                        ins = fn(e)
                        if inc:
                            ins.then_inc(sems[key], inc)
                return body
            block.tensor(run("pe"))
            block.scalar(run("act"))
            block.vector(run("dve"))
            block.gpsimd(run("pool"))
            block.sync(run("sp"))


class Arena:
    def __init__(self, t, nwords):
        self.t = t
        self.nbytes = nwords * 4
        self.off = 0

    def reset(self, off=0):
        self.off = off

    def _take(self, nbytes):
        nb = (nbytes + 63) // 64 * 64
        assert self.off + nb <= self.nbytes, ("arena overflow", self.off, nb, self.nbytes)
        o = self.off
        self.off += nb
        return o

    def f(self, n, parts=128):
        o = self._take(n * 4)
        return self.t[0:parts, o // 4:o // 4 + n]

    def b(self, n, parts=128):
        assert n % 2 == 0
        o = self._take(n * 2)
        return self.t[0:parts, o // 4:o // 4 + n // 2].bitcast(BF16)


def v3(ap, b):
    return ap.rearrange("p (a b) -> p a b", b=b)


def na_tables():
    rows = S // GW
    rstart = [min(max(r - 4, 0), rows - 8) for r in range(rows)]
    Q = {}
    for kr in range(rows):
        qs = [qr for qr in range(rows) if rstart[qr] <= kr <= rstart[qr] + 7]
        assert qs == list(range(qs[0], qs[-1] + 1))
        Q[kr] = (qs[0], qs[-1] + 1)
    return Q


def build_nc():
    nc = bass.Bass("TRN2", target_bir_lowering=False)
    P = Prog()

    def din(name, shape, dt=F32):
        return nc.dram_tensor(name, list(shape), dt, kind="ExternalInput").ap()

    x_in = din("x", [NB, S, D])
    ctx_in = din("ctx", [NB, CT, D])
    cT_in = din("cT", [128, 16 * 3])
    ng3_in = din("ng3", [L, 128, 16 * 3])
    bada3_in = din("bada3", [L, 128, 48 * 3])
    wada_in = din("w_ada", [L, D, 3 * D])
    win_in = din("win_r", [L, D, 6912])
    wout_in = din("w_out", [L, D, D])
    wuq_in = din("wuq_r", [L, 384, 1024])
    wukv_in = din("wukv_r", [L, 256, 1024])
    wpw_in = din("d_pw_w", [L, 512, 512])
    vec_in = din("vecs", [L, 128, 64])
    dww_in = din("dww", [L, 128, 4 * 31])
    fng_in = din("fng", [128, 16])
    rope_in = din("rope", [2, 64, NT])
    ub_in = din("ub", [L, 4, 128, 960])
    ident_in = din("ident", [128, 128])
    onesbd_in = din("onesbd", [128, 128])
    out = nc.dram_tensor("out", [NB, S, D], F32, kind="ExternalOutput").ap()

    skind = "ExternalOutput" if DEBUG else "Internal"
    XT = [nc.dram_tensor(f"XT{b}", [D, NT], F32, kind=skind).ap() for b in range(NB)]
    pT = [nc.dram_tensor(f"pT{b}", [PT_ROWS, NT], BF16, kind=skind).ap() for b in range(NB)]
    vtok = [nc.dram_tensor(f"vtok{b}", [8, 64, 36, 64], BF16, kind=skind).ap() for b in range(NB)]
    yT = [nc.dram_tensor(f"yT{b}", [D, NT], BF16, kind=skind).ap() for b in range(NB)]
    hTd = [nc.dram_tensor(f"hTd{b}", [D, NT], BF16, kind=skind).ap() for b in range(NB)]
    B_hTd = [DBuf() for _ in range(NB)]
    KBD = [nc.dram_tensor(f"KBD{b}", [128, 4 * 36 * 128], BF16, kind=skind).ap() for b in range(NB)]
    VBD = [nc.dram_tensor(f"VBD{b}", [128, 4 * 36 * 128], BF16, kind=skind).ap() for b in range(NB)]
    B_KBD = [DBuf() for _ in range(NB)]
    B_VBD = [DBuf() for _ in range(NB)]
    B_XT = [DBuf() for _ in range(NB)]
    B_pT = [DBuf() for _ in range(NB)]
    B_vtok = [DBuf() for _ in range(NB)]
    B_yT = [DBuf() for _ in range(NB)]

    NW = 51200
    from contextlib import ExitStack
    es = ExitStack()
    ar_t = es.enter_context(nc.sbuf_tensor("arena", [128, NW], F32))
    cst_t = es.enter_context(nc.sbuf_tensor("cst_f", [128, 1400], F32))
    cstb_t = es.enter_context(nc.sbuf_tensor("cst_b", [128, 600], BF16))
    banks = [es.enter_context(nc.psum_tensor(f"ps{i}", [128, 512], F32)) for i in range(8)]
    BK = [Buf() for _ in range(8)]
    AR = Arena(ar_t, NW)

    keys = list(COMPUTE) + [("dma", q, i) for q in ("sp", "pool") for i in range(P.ndma)]
    sems = {}
    for k in keys:
        nm = k if isinstance(k, str) else f"d_{k[1]}_{k[2]}"
        sems[k] = es.enter_context(nc.semaphore("s_" + nm))

    class _CA:
        def __init__(self, t):
            self.t = t
            self.off = 0

        def alloc(self, n):
            ap = self.t[:, self.off:self.off + n]
            self.off += (n + 15) // 16 * 16
            return ap
    CF = _CA(cst_t)
    CB = _CA(cstb_t)
    ident_f = CF.alloc(128)
    eps_t = CF.alloc(1)
    cT = CF.alloc(48)
    modv_l = [CF.alloc(144) for _ in range(L)]
    gsv_l = [CF.alloc(48) for _ in range(L)]
    ng3_l = [CF.alloc(48) for _ in range(L)]
    bada3_l = [CF.alloc(144) for _ in range(L)]
    vecs = CF.alloc(64)
    dww = CF.alloc(124)
    fng = CF.alloc(16)
    onesbd_f = CF.alloc(128)
    ident_b = CB.alloc(128)
    ones_b = CB.alloc(128)
    onesbd_b = CB.alloc(128)
    scT = CB.alloc(48)
    B_const = Buf(strict=True)
    B_mod_l = [Buf(strict=True) for _ in range(L)]
    B_adl = Buf(strict=True)
    B_lay = Buf(strict=True)

    P.dma("sp", ident_f, ident_in, writes=[B_const])
    P.dma("sp", onesbd_f, onesbd_in, writes=[B_const])
    P.dma("sp", cT, cT_in, writes=[B_const])
    P.dma("sp", fng, fng_in, writes=[B_const])
    P.op("dve", lambda e: e.memset(eps_t, EPS), writes=[B_const])
    P.op("dve", lambda e: e.memset(ones_b, 1.0), writes=[B_const])
    P.op("dve", lambda e: e.tensor_copy(out=ident_b, in_=ident_f), reads=[B_const], writes=[B_const])
    P.op("dve", lambda e: e.tensor_copy(out=onesbd_b, in_=onesbd_f), reads=[B_const], writes=[B_const])
    P.op("act", lambda e: e.activation(out=scT, in_=cT, func=AF.Silu), reads=[B_const], writes=[B_const])

    def vcol(i):
        return vecs[:, i:i + 1]
    V_ACW, V_DWB, V_LNG, V_LNB, V_PWB, V_QNG, V_KVG = 0, 12, 16, 20, 24, 28, 31

    cp_rr = [0]
    act_only = [False]

    def evac_copy(out_ap, in_ap, reads, writes, scale=None):
        cp_rr[0] ^= 1
        if cp_rr[0] or act_only[0]:
            if scale is None:
                return P.op("act", lambda e: e.activation(out=out_ap, in_=in_ap, func=AF.Copy), reads=reads, writes=writes)
            return P.op("act", lambda e: e.activation(out=out_ap, in_=in_ap, func=AF.Copy, scale=scale), reads=reads, writes=writes)
        if scale is None:
            return P.op("dve", lambda e: e.tensor_copy(out=out_ap, in_=in_ap), reads=reads, writes=writes)
        return P.op("dve", lambda e: e.tensor_scalar(out=out_ap, in0=in_ap, scalar1=scale, scalar2=None, op0=ALU.mult), reads=reads, writes=writes)

    def mm(out_ap, lhsT, rhs, start, stop):
        return lambda e: e.matmul(out_ap, lhsT=lhsT, rhs=rhs, start=start, stop=stop)

    bk_rr = [0]

    def next_bank(lo=0, hi=6):
        i = lo + bk_rr[0] % (hi - lo)
        bk_rr[0] += 1
        return i


    def ACT(out_, in_, func, reads, writes, bias=None, scale=None):
        kw = {}
        if bias is not None:
            kw["bias"] = bias
        if scale is not None:
            kw["scale"] = scale
        return P.op("act", lambda e: e.activation(out=out_, in_=in_, func=func, **kw), reads=reads, writes=writes)

    def TT(eng, out_, in0, in1, op, reads, writes):
        return P.op(eng, lambda e: e.tensor_tensor(out=out_, in0=in0, in1=in1, op=op), reads=reads, writes=writes)

    def TS(eng, out_, in0, s1, op0, reads, writes):
        return P.op(eng, lambda e: e.tensor_scalar(out=out_, in0=in0, scalar1=s1, scalar2=None, op0=op0), reads=reads, writes=writes)

    def STT(out_, in0, scalar, in1, op0, op1, reads, writes):
        return P.op("dve", lambda e: e.scalar_tensor_tensor(out=out_, in0=in0, scalar=scalar, in1=in1, op0=op0, op1=op1),
                    reads=reads, writes=writes)

    def RCP(out_, in_, reads, writes):
        return P.op("dve", lambda e: e.reciprocal(out=out_, in_=in_), reads=reads, writes=writes)

    def MSET(eng, ap, val, writes):
        return P.op(eng, lambda e: e.memset(ap, val), writes=writes)

    def TRN(out_, in_):
        return lambda e: e.transpose(out_, in_, ident_f)

    AR.reset()
    wb = [v3(AR.b(16 * 512), 512) for _ in range(2)]
    B_wb = [Buf(), Buf()]
    gi_ = 0
    for l in range(L):
        P.dma("sp", ng3_l[l], ng3_in[l], writes=[B_adl])
        P.dma("sp", bada3_l[l], bada3_in[l], writes=[B_adl])
        mbank = 6 + l
        wadav = wada_in[l].rearrange("(c p) n -> p c n", p=128)
        for g in range(12):
            s = gi_ % 2
            gi_ += 1
            P.dma("pool", wb[s], wadav[:, :, g * 512:(g + 1) * 512], writes=[B_wb[s]])
            for j in range(4):
                oc = g * 4 + j
                fns = [mm(banks[mbank][:, oc * 4:oc * 4 + 3], wb[s][:, kc, j * 128:(j + 1) * 128], scT[:, kc * 3:kc * 3 + 3],
                          kc == 0, kc == 15) for kc in range(16)]
                P.group("pe", fns, reads=[B_wb[s], B_const], writes=[BK[mbank]])
        TT("dve", v3(modv_l[l], 3), v3(banks[mbank][:, 0:192], 4)[:, :, 0:3], v3(bada3_l[l], 3), ALU.add, [BK[mbank], B_adl], [B_mod_l[l]])
        TS("dve", gsv_l[l], modv_l[l][:, 48:96], 1.0, ALU.add, [B_mod_l[l]], [B_mod_l[l]])
        TT("dve", gsv_l[l], gsv_l[l], ng3_l[l], ALU.mult, [B_mod_l[l], B_adl], [B_mod_l[l]])
    P.barrier()

    def shv(c, mi, l_):
        return modv_l[l_][:, c * 3 + mi:c * 3 + mi + 1]

    def gsc(c, mi, l_):
        return gsv_l[l_][:, c * 3 + mi:c * 3 + mi + 1]

    def gtv(c, mi, l_):
        return modv_l[l_][:, 96 + c * 3 + mi:96 + c * 3 + mi + 1]

    ntmp = [0]

    def norm_squares(xs, B_xs, n, sq, B_sq, sq_on_act):
        for c in range(16):
            if sq_on_act or c % 4 == 3:
                ACT(sq[:, c, 0:n], xs[:, c, 0:n], AF.Square, [B_xs], [B_sq])
            else:
                TT("dve", sq[:, c, 0:n], xs[:, c, 0:n], xs[:, c, 0:n], ALU.mult, [B_xs], [B_sq])

    def norm_stats(n, sq, B_sq, rs, B_rs):
        bi = next_bank()
        P.group("pe", [mm(banks[bi][:, 0:n], ones_b, sq[:, c, 0:n], c == 0, c == 15) for c in range(16)],
                reads=[B_sq, B_const], writes=[BK[bi]])
        ACT(rs[:, 0:n], banks[bi][:, 0:n], AF.Sqrt, [BK[bi], B_const], [B_rs], bias=eps_t, scale=1.0 / D)
        RCP(rs[:, 0:n], rs[:, 0:n], [B_rs], [B_rs])

    def norm_chunk(xs, B_xs, n, c, mi, l_, rs, B_rs, tmps, B_tmps, ht, B_ht):
        s_ = ntmp[0] % len(tmps)
        ntmp[0] += 1
        TT("dve", tmps[s_][:, 0:n], xs[:, c, 0:n], rs[:, 0:n], ALU.mult, [B_xs, B_rs], [B_tmps[s_]])
        ACT(ht[:, c, 0:n], tmps[s_][:, 0:n], AF.Identity, [B_tmps[s_], B_mod_l[l_]], [B_ht], bias=shv(c, mi, l_), scale=gsc(c, mi, l_))

    def norm_store(n, b_, t0, ht, B_ht):
        P.dma("sp", hTd[b_].rearrange("(c p) t -> p c t", p=128)[:, :, t0:t0 + n], ht[:, :, 0:n], reads=[B_ht], writes=[B_hTd[b_]])

    def emit_norm_tile(xs, B_xs, n, b_, t0, mi, l_, sq, B_sq, rs, B_rs, tmps, B_tmps, ht, B_ht, sq_on_act):
        norm_squares(xs, B_xs, n, sq, B_sq, sq_on_act)
        norm_stats(n, sq, B_sq, rs, B_rs)
        for c in range(16):
            norm_chunk(xs, B_xs, n, c, mi, l_, rs, B_rs, tmps, B_tmps, ht, B_ht)
        norm_store(n, b_, t0, ht, B_ht)

    AR.reset()
    zt = AR.b(4 * 36 * 128)
    B_zt = Buf()
    MSET("dve", zt, 0.0, [B_zt])
    for b in range(NB):
        P.dma("sp", KBD[b], zt, reads=[B_zt], writes=[B_KBD[b]])
        P.dma("sp", VBD[b], zt, reads=[B_zt], writes=[B_VBD[b]])
    xin = [AR.f(2048) for _ in range(4)]
    xst = [v3(AR.f(16 * 512), 512) for _ in range(2)]
    B_xin = [Buf() for _ in range(4)]
    B_xst = [DBuf(), DBuf()]
    p_sq = v3(AR.b(16 * 512), 512)
    p_ht = [v3(AR.b(16 * 512), 512) for _ in range(2)]
    p_rs = [AR.f(512) for _ in range(2)]
    p_tmp = [AR.f(512) for _ in range(4)]
    B_psq_ = DBuf()
    B_prs_ = [Buf(), Buf()]
    pend = [None]

    def pro_back():
        if pend[0] is None:
            return
        (xs_, B_xs_, n_, b_, t0_, mi_, rs_, B_rs_, ht_, B_ht_) = pend[0]
        for c in range(16):
            norm_chunk(xs_, B_xs_, n_, c, mi_, 0, rs_, B_rs_, p_tmp, B_ptmp, ht_, B_ht_)
        norm_store(n_, b_, t0_, ht_, B_ht_)
        pend[0] = None
    B_pht = [DBuf(), DBuf()]
    B_ptmp = [Buf() for _ in range(4)]
    groups = [(b_, g0, ntt) for b_ in range(NB) for (g0, ntt) in ((0, 4), (4, 4), (8, 4), (12, 4), (16, 2))]

    def pro_loads(gidx):
        b_, g0, ntt = groups[gidx]
        for q in range(ntt):
            tt = g0 + q
            src = x_in[b_, tt * 128:(tt + 1) * 128, :] if tt < 16 else ctx_in[b_, (tt - 16) * 128:(tt - 15) * 128, :]
            P.dma("sp", xin[q], src, writes=[B_xin[q]])

    pro_loads(0)
    for gidx, (b, g0, ntt) in enumerate(groups):
        XTv0 = XT[b].rearrange("(c p) t -> p c t", p=128)
        sg = gidx % 2
        for q in range(ntt):
            for g in range(4):
                bi = next_bank(0, 8)
                fns = [TRN(banks[bi][:, j * 128:(j + 1) * 128], xin[q][:, (g * 4 + j) * 128:(g * 4 + j + 1) * 128]) for j in range(4)]
                P.group("pe", fns, reads=[B_xin[q], B_const], writes=[BK[bi]])
                evac_copy(xst[sg][:, g * 4:(g + 1) * 4, q * 128:(q + 1) * 128], v3(banks[bi][:, :], 128), [BK[bi]], [B_xst[sg]])
        if gidx + 1 < len(groups):
            pro_loads(gidx + 1)
        P.dma("sp", XTv0[:, :, g0 * 128:(g0 + ntt) * 128], xst[sg][:, :, 0:ntt * 128], reads=[B_xst[sg]], writes=[B_XT[b]])
        norm_squares(xst[sg], B_xst[sg], ntt * 128, p_sq, B_psq_, False)
        norm_stats(ntt * 128, p_sq, B_psq_, p_rs[sg], B_prs_[sg])
        pro_back()
        pend[0] = (xst[sg], B_xst[sg], ntt * 128, b, g0 * 128, (b if g0 < 16 else 2), p_rs[sg], B_prs_[sg], p_ht[sg], B_pht[sg])
    pro_back()
    P.barrier()

    Qr = na_tables()

    for l in range(L):
        last = (l == L - 1)
        AR.reset()
        P.dma("sp", vecs, vec_in[l], writes=[B_lay])
        P.dma("sp", dww, dww_in[l], writes=[B_lay])
        B_mod = B_mod_l[l]
        P.barrier()

        for b in range(NB):
            XTv = XT[b].rearrange("(c p) t -> p c t", p=128)
            yTv = yT[b].rearrange("(c p) t -> p c t", p=128)
            pv = pT[b].rearrange("(c p) t -> p c t", p=128)
            AR.reset()
            u_pad = v3(AR.b(4 * UW), UW)
            dg = v3(AR.b(4 * NT), NT)
            HT_OFF = AR.off
            hT = v3(AR.b(16 * NT), NT)
            BF_BASE = AR.off
            B_u = Buf()
            B_dg = Buf()
            B_hT = [DBuf() for _ in TILES]
            hTdv = hTd[b].rearrange("(c p) t -> p c t", p=128)
            for ti, (t0, n) in enumerate(TILES):
                P.dma("sp", hT[:, :, t0:t0 + n], hTdv[:, :, t0:t0 + n], reads=[B_hTd[b]], writes=[B_hT[ti]])

            AR.reset(BF_BASE)
            wbuf = [v3(AR.b(16 * 256), 256) for _ in range(2)]
            B_wbuf = [Buf(), Buf()]
            stage = [AR.b(NT) for _ in range(3)]
            B_stage = [DBuf() for _ in range(3)]
            vst = [v3(AR.b(18 * 256), 256) for _ in range(2)]
            B_vst = [DBuf(), DBuf()]
            T = [AR.f(NT) for _ in range(4)]
            B_T = [Buf() for _ in range(4)]
            st_rr = [0]
            act_only[0] = True
            MSET("pool", u_pad, 0.0, [B_u])

            winv = win_in[l].rearrange("(c p) n -> p c n", p=128)
            grp = [0]

            def load_group(g):
                s_ = grp[0] % 2
                P.dma("pool", wbuf[s_], winv[:, :, g * 256:(g + 1) * 256], writes=[B_wbuf[s_]])
                grp[0] += 1
                return s_

            ntl = [len(TILES)]

            def proj_block(s_, o, m, handler, kv=False):
                tiles = TILES if (kv or not last) else TILES[:4]
                ntl[0] = len(tiles)
                for ti, (t0, n) in enumerate(tiles):
                    bi = next_bank()
                    P.group("pe", [mm(banks[bi][0:m, 0:n], wbuf[s_][:, kc, o:o + m], hT[:, kc, t0:t0 + n], kc == 0, kc == 15)
                                   for kc in range(16)], reads=[B_wbuf[s_], B_hT[ti]], writes=[BK[bi]])
                    handler(banks[bi][0:m, 0:n], ti, t0, n, BK[bi])

            def h_copy_T(k):
                def h(ps, ti, t0, n, bk):
                    evac_copy(T[k][:, t0:t0 + n], ps, [bk], [B_T[k]])
                return h

            def h_act_T(k, func):
                def h(ps, ti, t0, n, bk):
                    ACT(T[k][:, t0:t0 + n], ps, func, [bk], [B_T[k]])
                return h

            def h_act_ap(dst3, j, func, bufw):
                def h(ps, ti, t0, n, bk):
                    ACT(dst3[:, j, t0:t0 + n], ps, func, [bk], [bufw])
                return h

            def h_dram(row0, m, func=None):
                s_ = st_rr[0] % 3
                st_rr[0] += 1

                def h(ps, ti, t0, n, bk):
                    if func is None:
                        evac_copy(stage[s_][0:m, t0:t0 + n], ps, [bk], [B_stage[s_]])
                    else:
                        ACT(stage[s_][0:m, t0:t0 + n], ps, func, [bk], [B_stage[s_]])
                    if ti == ntl[0] - 1:
                        P.dma("sp", pT[b][row0:row0 + m, :], stage[s_][0:m, :], reads=[B_stage[s_]], writes=[B_pT[b]])
                return h

            KBDv = KBD[b].rearrange("p (h r k) -> p h r k", h=4, r=36)
            VBDv = VBD[b].rearrange("p (h r k) -> p h r k", h=4, r=36)

            def h_kbd(hp):
                s_ = st_rr[0] % 3
                st_rr[0] += 1

                def h(ps, ti, t0, n, bk):
                    evac_copy(stage[s_][:, t0:t0 + n], ps, [bk], [B_stage[s_]])
                    if ti == ntl[0] - 1:
                        for lo in (0, 64):
                            P.dma("sp", KBDv[lo:lo + 64, hp, :, lo:lo + 64], stage[s_][lo:lo + 64, :].rearrange("p (r k) -> p r k", k=64),
                                  reads=[B_stage[s_]], writes=[B_KBD[b]])
                return h

            def unit_ck(g, hp0):
                s = load_group(g)
                proj_block(s, 0, 128, h_kbd(hp0), kv=True)
                proj_block(s, 128, 128, h_kbd(hp0 + 1), kv=True)

            def unit_A(j):
                s = load_group(2 * j)
                proj_block(s, 0, 128, h_copy_T(0))
                proj_block(s, 128, 128, h_copy_T(1))
                s = load_group(2 * j + 1)
                proj_block(s, 0, 128, h_copy_T(2))
                proj_block(s, 128, 128, h_act_T(3, AF.Silu))
                TT("dve", T[0], T[0], T[1], ALU.mult, [B_T[1]], [B_T[0]])
                w0, w1, w2 = (vcol(V_ACW + j * 3 + k) for k in range(3))
                TS("dve", T[1], T[0], w1, ALU.mult, [B_T[0], B_lay], [B_T[1]])
                for (a, n_) in ((0, S), (S, CT)):
                    STT(T[1][:, a + 1:a + n_], T[0][:, a:a + n_ - 1], w0, T[1][:, a + 1:a + n_], ALU.mult, ALU.add, [B_T[0], B_lay], [B_T[1]])
                    STT(T[1][:, a:a + n_ - 1], T[0][:, a + 1:a + n_], w2, T[1][:, a:a + n_ - 1], ALU.mult, ALU.add, [B_T[0], B_lay], [B_T[1]])
                TT("dve", T[2], T[2], T[3], ALU.mult, [B_T[3]], [B_T[2]])
                ss = st_rr[0] % 3
                st_rr[0] += 1
                TT("dve", stage[ss], T[1], T[2], ALU.mult, [B_T[1], B_T[2]], [B_stage[ss]])
                P.dma("sp", yT[b][j * 128:(j + 1) * 128, :], stage[ss], reads=[B_stage[ss]], writes=[B_yT[b]])

            def unit_D(j):
                s = load_group(8 + j)
                proj_block(s, 0, 128, h_copy_T(0))
                proj_block(s, 128, 128, h_act_T(1, AF.Sigmoid))
                TT("dve", u_pad[:, j, U_LAT:U_LAT + S], T[0][:, 0:S], T[1][:, 0:S], ALU.mult, [B_T[0], B_T[1]], [B_u])
                TT("dve", u_pad[:, j, U_CTX:U_CTX + CT], T[0][:, S:NT], T[1][:, S:NT], ALU.mult, [B_T[0], B_T[1]], [B_u])

            def unit_dg(jj):
                s = load_group(12 + jj)
                for q in range(2):
                    proj_block(s, q * 128, 128, h_act_ap(dg, jj * 2 + q, AF.Silu, B_dg))

            def unit_pair(g, row0, func=None):
                s = load_group(g)
                proj_block(s, 0, 128, h_dram(row0, 128, func), kv=(row0 == R_KVA))
                proj_block(s, 128, 128, h_dram(row0 + 128, 128, func), kv=(row0 == R_KVA))

            def unit_qa2():
                s = load_group(15)
                proj_block(s, 0, 128, h_dram(R_QA + 256, 128))
                proj_block(s, 128, 64, h_dram(R_KPE, 64), kv=True)
                proj_block(s, 192, 64, h_dram(R_KSW, 64), kv=True)

            def unit_v(half):
                s = load_group(25 + half)
                for tt in range(18):
                    bi = next_bank()
                    P.group("pe", [mm(banks[bi][:, 0:256], hT[:, kc, tt * 128:(tt + 1) * 128], wbuf[s][:, kc, :], kc == 0, kc == 15)
                                   for kc in range(16)], reads=[B_wbuf[s], B_hT[min(tt // 4, 4)]], writes=[BK[bi]])
                    evac_copy(vst[half][:, tt, :], banks[bi][:, 0:256], [BK[bi]], [B_vst[half]])
                for rl in range(2):
                    for hl in range(4):
                        hh = 4 * half + hl
                        hp_, hf = hh // 2, hh % 2
                        dst = VBDv[hf * 64:(hf + 1) * 64, hp_, :, hf * 64:(hf + 1) * 64].rearrange("k (t r) d -> r k t d", r=2)[rl]
                        src = vst[half][rl * 64:(rl + 1) * 64, :, hl * 64:(hl + 1) * 64]
                        P.dma("sp", dst, src, reads=[B_vst[half]], writes=[B_VBD[b]])

            unit_A(0)
            unit_pair(14, R_QA)
            unit_qa2()
            unit_A(1)
            unit_pair(16, R_KVA)
            unit_pair(17, R_BG, AF.Silu)
            unit_A(2)
            unit_pair(18, R_BG + 256, AF.Silu)
            unit_pair(19, R_CQ)
            unit_A(3)
            unit_pair(20, R_CQ + 256)
            unit_ck(21, 0)
            unit_D(0)
            unit_ck(22, 2)
            unit_D(1)
            unit_pair(23, R_CG, AF.Silu)
            unit_D(2)
            unit_pair(24, R_CG + 256, AF.Silu)
            unit_D(3)
            unit_dg(0)
            unit_dg(1)
            unit_v(0)
            unit_v(1)
            assert grp[0] == 27
            act_only[0] = False
            P.barrier()

            MLA_PF = 147456
            AR.reset(MLA_PF)
            qa = v3(AR.b(3 * NT), NT)
            kva = v3(AR.b(2 * NT), NT)
            kpe = AR.b(NT)
            ksw = AR.b(NT)
            CC = AR.f(NT)
            SSn = AR.f(NT)
            B_ld, B_w, B_bg = Buf(), Buf(), Buf()
            B_cqt = [Buf() for _ in TILES]
            B_ckvt = [Buf() for _ in TILES]
            P.dma("sp", qa, pv[:, 0:3, :], reads=[B_pT[b]], writes=B_cqt)
            P.dma("sp", kva, pv[:, 4:6, :], reads=[B_pT[b]], writes=B_ckvt)
            P.dma("sp", kpe[0:64, :], pT[b][R_KPE:R_KPE + 64, :], reads=[B_pT[b]], writes=[B_ld])
            P.dma("sp", ksw[0:64, :], pT[b][R_KSW:R_KSW + 64, :], reads=[B_pT[b]], writes=[B_ld])
            P.dma("sp", CC[0:64, :], rope_in[0], writes=[B_ld])
            P.dma("sp", SSn[0:64, :], rope_in[1], writes=[B_ld])
            pt1 = [AR.f(512) for _ in range(2)]
            pt2 = AR.f(512)
            B_pt1 = [Buf(), Buf()]
            B_pt2 = Buf()
            assert AR.off <= NW * 4, AR.off
            B_kpe = Buf()
            MSET("pool", kpe[64:128, :], 0.0, [B_kpe])
            AR.reset(HT_OFF)
            diag = AR.b(4 * 31 * 128)
            wpw = v3(AR.b(4 * 512), 512)
            vb2 = [v3(AR.b(4 * 512), 512)] * 2
            sqb2 = [v3(AR.b(4 * 512), 512)] * 2
            zb2 = [v3(AR.b(4 * 512), 512) for _ in range(2)]
            ys = [v3(AR.b(4 * 512), 512) for _ in range(2)]
            B_diagc = [Buf() for _ in range(4)]
            B_wpw = Buf()
            B_vb2, B_sqb2, B_zb2 = [Buf()] * 2, [Buf()] * 2, [Buf(), Buf()]
            B_ys = [Buf(), Buf()]
            vf2 = [v3(AR.f(4 * 512), 512)] * 2
            B_vf2 = [Buf()] * 2
            st2 = [[AR.f(512) for _ in range(3)] + [None] for _ in range(2)]
            psq = [v3(AR.b(3 * 512), 512) for _ in range(2)]
            B_psq = [DBuf(), DBuf()]
            prqs = [AR.f(512) for _ in range(4)]
            B_prqs = [Buf() for _ in range(4)]
            B_mean2, B_var2 = [Buf(), Buf()], [Buf(), Buf()]
            dt2 = [AR.f(512) for _ in range(2)]
            B_dt2 = [Buf(), Buf()]
            P.dma("pool", wpw, wpw_in[l].rearrange("(c p) n -> p c n", p=128), writes=[B_wpw])
            for c in range(4):
                for k in range(31):
                    TS("dve", diag[:, (c * 31 + k) * 128:(c * 31 + k + 1) * 128], ident_f, dww[:, c * 31 + k:c * 31 + k + 1], ALU.mult,
                       [B_const, B_lay], [B_diagc[c]])
            W_PF = AR.off
            wuq = v3(AR.b(3 * 1024), 1024)
            wukv = v3(AR.b(2 * 1024), 1024)
            P.dma("pool", wuq, wuq_in[l].rearrange("(c p) n -> p c n", p=128), writes=[B_w])
            P.dma("pool", wukv, wukv_in[l].rearrange("(c p) n -> p c n", p=128), writes=[B_w])
            assert AR.off <= MLA_PF, AR.off
            dtiles = [(t0, n, t0) for (t0, n) in TILES[:4]] + ([] if last else [(S, CT, U_CTX - 15)])
            dti = [0]

            CB_ = [0, 1, 2, 3]

            def dconv_conv_pe(di):
                t0, n, uo = dtiles[di]
                for c in range(4):
                    bi = CB_[c]
                    P.group("pe", [mm(banks[bi][:, 0:n], diag[:, (c * 31 + k) * 128:(c * 31 + k + 1) * 128], u_pad[:, c, uo + k:uo + k + n], k == 0, k == 30)
                                   for k in range(31)], reads=[B_diagc[c], B_u], writes=[BK[bi]])

            def dconv_conv_evac(di):
                t0, n, uo = dtiles[di]
                p = di % 2
                vf, vb, sqb = vf2[p], vb2[p], sqb2[p]
                for c in range(4):
                    bi = CB_[c]
                    ACT(vf[:, c, 0:n], banks[bi][:, 0:n], AF.Identity, [BK[bi], B_lay], [B_vf2[p]], bias=vcol(V_DWB + c))
                    ACT(sqb[:, c, 0:n], banks[bi][:, 0:n], AF.Square, [BK[bi], B_lay], [B_sqb2[p]], bias=vcol(V_DWB + c))
                    P.op("dve", (lambda o, i_: (lambda e: e.tensor_copy(out=o, in_=i_)))(vb[:, c, 0:n], vf[:, c, 0:n]), reads=[B_vf2[p]], writes=[B_vb2[p]])

            def dconv_stats_pe(di):
                t0, n, uo = dtiles[di]
                p = di % 2
                P.group("pe", [mm(banks[4][:, 0:n], ones_b, vb2[p][:, c, 0:n], c == 0, c == 3) for c in range(4)], reads=[B_vb2[p], B_const], writes=[BK[4]])
                P.group("pe", [mm(banks[5][:, 0:n], ones_b, sqb2[p][:, c, 0:n], c == 0, c == 3) for c in range(4)], reads=[B_sqb2[p], B_const], writes=[BK[5]])

            def dconv_ln(di):
                t0, n, uo = dtiles[di]
                p = di % 2
                vf, zb = vf2[p], zb2[p]
                B_vf, B_zb = B_vf2[p], B_zb2[p]
                mean, msq, var, _unused = st2[p]
                B_mean, B_var = B_mean2[p], B_var2[p]
                TS("dve", mean[:, 0:n], banks[4][:, 0:n], 1.0 / 512, ALU.mult, [BK[4]], [B_mean])
                TT("dve", msq[:, 0:n], mean[:, 0:n], mean[:, 0:n], ALU.mult, [B_mean], [B_var])
                STT(var[:, 0:n], banks[5][:, 0:n], 1.0 / 512, msq[:, 0:n], ALU.mult, ALU.subtract, [BK[5], B_var], [B_var])
                ACT(var[:, 0:n], var[:, 0:n], AF.Sqrt, [B_var, B_const], [B_var], bias=eps_t)
                RCP(var[:, 0:n], var[:, 0:n], [B_var], [B_var])
                for c in range(4):
                    q = dti[0] % 2
                    dti[0] += 1
                    TT("dve", dt2[q][:, 0:n], vf[:, c, 0:n], mean[:, 0:n], ALU.subtract, [B_vf, B_mean], [B_dt2[q]])
                    TT("pool", dt2[q][:, 0:n], dt2[q][:, 0:n], var[:, 0:n], ALU.mult, [B_var], [B_dt2[q]])
                    ACT(zb[:, c, 0:n], dt2[q][:, 0:n], AF.Silu, [B_dt2[q], B_lay], [B_zb], bias=vcol(V_LNB + c), scale=vcol(V_LNG + c))

            def dconv_pw(di):
                t0, n, uo = dtiles[di]
                p = di % 2
                ysb, zb = ys[p], zb2[p]
                for co in range(4):
                    bi = 6 + co % 2
                    P.group("pe", [mm(banks[bi][:, 0:n], wpw[:, ci, co * 128:(co + 1) * 128], zb[:, ci, 0:n], ci == 0, ci == 3) for ci in range(4)],
                            reads=[B_wpw, B_zb2[p]], writes=[BK[bi]])
                    STT(ysb[:, co, 0:n], banks[bi][:, 0:n], vcol(V_PWB + co), dg[:, co, t0:t0 + n], ALU.add, ALU.mult,
                        [BK[bi], B_dg, B_lay], [B_ys[p]])
                P.dma("sp", yTv[:, 12:16, t0:t0 + n], ysb[:, :, 0:n], reads=[B_ys[p]], writes=[B_yT[b]])

            ptr = [0]

            def prep_srcs(ti):
                lst = []
                for k, (src, nchunk, Bs, gidx, dim) in enumerate(((qa, 3, B_cqt[ti], V_QNG, 384.0), (kva, 2, B_ckvt[ti], V_KVG, 256.0))):
                    if last and ti == 4 and k == 0:
                        continue
                    lst.append((k, src, nchunk, Bs, gidx, dim))
                return lst

            def prep_sq(ti):
                t0, n = TILES[ti]
                for (k, src, nchunk, Bs, gidx, dim) in prep_srcs(ti):
                    for c in range(nchunk):
                        ACT(psq[k][:, c, 0:n], src[:, c, t0:t0 + n], AF.Square, [Bs], [B_psq[k]])

            def prep_mm(ti):
                t0, n = TILES[ti]
                for (k, src, nchunk, Bs, gidx, dim) in prep_srcs(ti):
                    bi = 6 + k
                    rq, B_rq = prqs[(ti % 2) * 2 + k], B_prqs[(ti % 2) * 2 + k]
                    P.group("pe", [mm(banks[bi][:, 0:n], ones_b, psq[k][:, c, 0:n], c == 0, c == nchunk - 1) for c in range(nchunk)],
                            reads=[B_psq[k], B_const], writes=[BK[bi]])
                    ACT(rq[:, 0:n], banks[bi][:, 0:n], AF.Sqrt, [BK[bi], B_const], [B_rq], bias=eps_t, scale=1.0 / dim)

            def prep_fin(ti):
                t0, n = TILES[ti]
                for (k, src, nchunk, Bs, gidx, dim) in prep_srcs(ti):
                    rq, B_rq = prqs[(ti % 2) * 2 + k], B_prqs[(ti % 2) * 2 + k]
                    RCP(rq[:, 0:n], rq[:, 0:n], [B_rq], [B_rq])
                    for c in range(nchunk):
                        s_ = ptr[0] % 2
                        ptr[0] += 1
                        TT("dve", pt1[s_][:, 0:n], src[:, c, t0:t0 + n], rq[:, 0:n], ALU.mult, [Bs, B_rq], [B_pt1[s_]])
                        ACT(src[:, c, t0:t0 + n], pt1[s_][:, 0:n], AF.Identity, [B_pt1[s_], B_lay], [Bs], scale=vcol(gidx + c))
                s_ = ptr[0] % 2
                ptr[0] += 1
                TT("dve", pt1[s_][0:64, 0:n], kpe[0:64, t0:t0 + n], CC[0:64, t0:t0 + n], ALU.mult, [B_ld, B_kpe], [B_pt1[s_]])
                TT("pool", pt2[0:64, 0:n], ksw[0:64, t0:t0 + n], SSn[0:64, t0:t0 + n], ALU.mult, [B_ld], [B_pt2])
                TT("dve", kpe[0:64, t0:t0 + n], pt1[s_][0:64, 0:n], pt2[0:64, 0:n], ALU.add, [B_pt1[s_], B_pt2, B_ld], [B_kpe])

            ndt = len(dtiles)
            dconv_conv_pe(0)
            dconv_conv_evac(0)
            prep_sq(0)
            for di in range(ndt):
                dconv_stats_pe(di)
                if di >= 1:
                    dconv_pw(di - 1)
                prep_mm(di)
                if di + 1 < ndt:
                    dconv_conv_pe(di + 1)
                dconv_ln(di)
                if di >= 1:
                    prep_fin(di - 1)
                if di + 1 < ndt:
                    dconv_conv_evac(di + 1)
                if di + 1 < len(TILES):
                    prep_sq(di + 1)
            dconv_pw(ndt - 1)
            prep_fin(ndt - 1)
            for ti in range(ndt, len(TILES)):
                prep_mm(ti)
                prep_fin(ti)
            P.barrier()

            AR.reset()
            bg = v3(AR.b(4 * NT), NT)
            P.dma("sp", bg, pv[:, 6:10, :], reads=[B_pT[b]], writes=[B_bg])
            kn = v3(AR.b(4 * NT), NT)
            vt = v3(AR.b(18 * 512), 512)
            qn = v3(AR.b(4 * NT), NT)
            qr = v3(AR.b(4 * NT), NT)
            kr, B_kr = kpe, B_kpe
            PTb = [AR.b(512) for _ in range(4)]
            ysm = [AR.b(512) for _ in range(2)]
            B_qr = Buf()
            B_kn, B_vt, B_qn = DBuf(), DBuf(), DBuf()
            B_PT = [Buf() for _ in range(4)]
            B_ysm = [Buf(), Buf()]
            t1 = [AR.f(512) for _ in range(2)]
            t2 = [AR.f(512) for _ in range(2)]
            B_t1 = [Buf(), Buf()]
            B_t2 = [Buf(), Buf()]
            rc = [AR.f(512) for _ in range(2)]
            B_rc = [Buf(), Buf()]
            assert AR.off <= W_PF, (AR.off, W_PF)
            MSET("dve", qr, 0.0, [B_qr])
            tr = 0
            for h in range(4):
                for ti, (t0, n) in enumerate(TILES):
                    bi = next_bank()
                    P.group("pe", [mm(banks[bi][:, 0:n], wukv[:, kc, h * 128:(h + 1) * 128], kva[:, kc, t0:t0 + n], kc == 0, kc == 1) for kc in range(2)],
                            reads=[B_w, B_ckvt[ti]], writes=[BK[bi]])
                    evac_copy(kn[:, h, t0:t0 + n], banks[bi][:, 0:n], [BK[bi]], [B_kn])
                    if last and ti == 4:
                        continue
                    bi = next_bank()
                    P.group("pe", [mm(banks[bi][:, 0:n], wuq[:, kc, h * 128:(h + 1) * 128], qa[:, kc, t0:t0 + n], kc == 0, kc == 2) for kc in range(3)],
                            reads=[B_w, B_cqt[ti]], writes=[BK[bi]])
                    evac_copy(qn[:, h, t0:t0 + n], banks[bi][:, 0:n], [BK[bi]], [B_qn])
                    b1 = next_bank()
                    P.group("pe", [mm(banks[b1][0:64, 0:n], wuq[:, kc, 512 + h * 64:512 + (h + 1) * 64], qa[:, kc, t0:t0 + n], kc == 0, kc == 2) for kc in range(3)],
                            reads=[B_w, B_cqt[ti]], writes=[BK[b1]])
                    b2 = next_bank()
                    P.group("pe", [mm(banks[b2][0:64, 0:n], wuq[:, kc, 768 + h * 64:768 + (h + 1) * 64], qa[:, kc, t0:t0 + n], kc == 0, kc == 2) for kc in range(3)],
                            reads=[B_w, B_cqt[ti]], writes=[BK[b2]])
                    s = tr % 2
                    tr += 1
                    TT("dve", t1[s][0:64, 0:n], banks[b1][0:64, 0:n], CC[0:64, t0:t0 + n], ALU.mult, [BK[b1], B_ld], [B_t1[s]])
                    TT("dve", t2[s][0:64, 0:n], banks[b2][0:64, 0:n], SSn[0:64, t0:t0 + n], ALU.mult, [BK[b2], B_ld], [B_t2[s]])
                    TT("pool", qr[0:64, h, t0:t0 + n], t1[s][0:64, 0:n], t2[s][0:64, 0:n], ALU.add, [B_t1[s], B_t2[s]], [B_qr])
            for tt in range(18):
                bi = next_bank()
                P.group("pe", [mm(banks[bi][:, :], kva[:, kc, tt * 128:(tt + 1) * 128], wukv[:, kc, 512:1024], kc == 0, kc == 1) for kc in range(2)],
                        reads=[B_w, B_ckvt[min(tt // 4, 4)]], writes=[BK[bi]])
                evac_copy(vt[:, tt, :], banks[bi][:, :], [BK[bi]], [B_vt])
            ai = 0
            for h in range(4):
                for qi, (q0, n) in enumerate(TILES):
                    if qi == 4 and last:
                        continue
                    kts = [16, 17] + (list(range(16)) if qi < 4 else [])
                    ob, db = 3 + ai % 2, 5 + ai % 2
                    s = ai % 2
                    ai += 1

                    SBK = (0, 1, 2, 7)

                    def score(j):
                        kt = kts[j]
                        sb = SBK[j % 4]
                        P.group("pe", [mm(banks[sb][:, 0:n], kn[:, h, kt * 128:(kt + 1) * 128], qn[:, h, q0:q0 + n], True, False),
                                       mm(banks[sb][:, 0:n], kr[:, kt * 128:(kt + 1) * 128], qr[:, h, q0:q0 + n], False, True)],
                                reads=[B_kn, B_kr, B_qn, B_qr], writes=[BK[sb]])
                    score(0)
                    if len(kts) > 1:
                        score(1)
                    for j in range(len(kts)):
                        if j + 2 < len(kts):
                            score(j + 2)
                        sb = SBK[j % 4]
                        pb = j % 4
                        kt = kts[j]
                        ACT(PTb[pb][:, 0:n], banks[sb][:, 0:n], AF.Exp, [BK[sb]], [B_PT[pb]], scale=MLA_SCALE)
                        first, lastk = (j == 0), (j == len(kts) - 1)
                        P.group("pe", [mm(banks[ob][:, 0:n], vt[:, kt, h * 128:(h + 1) * 128], PTb[pb][:, 0:n], first, lastk),
                                       mm(banks[db][:, 0:n], ones_b, PTb[pb][:, 0:n], first, lastk)],
                                reads=[B_vt, B_PT[pb], B_const], writes=[BK[ob], BK[db]])
                    RCP(rc[s][:, 0:n], banks[db][:, 0:n], [BK[db]], [B_rc[s]])
                    TT("dve", rc[s][:, 0:n], banks[ob][:, 0:n], rc[s][:, 0:n], ALU.mult, [BK[ob]], [B_rc[s]])
                    TT("pool", ysm[s][:, 0:n], rc[s][:, 0:n], bg[:, h, q0:q0 + n], ALU.mult, [B_rc[s], B_bg], [B_ysm[s]])
                    P.dma("sp", yT[b][512 + h * 128:512 + (h + 1) * 128, q0:q0 + n], ysm[s][:, 0:n], reads=[B_ysm[s]], writes=[B_yT[b]])
            P.barrier()

            AR.reset()
            cq = v3(AR.b(4 * NT), NT)
            cg = v3(AR.b(4 * NT), NT)
            kbd = AR.b(4 * 36 * 128)
            vbd = AR.b(4 * 36 * 128)
            ubb = v3(AR.b(4 * 960), 960)
            PTn = [AR.b(512) for _ in range(4)]
            ysn = [AR.b(512) for _ in range(2)]
            B_nl, B_kbd, B_vbd, B_ub = Buf(), Buf(), Buf(), Buf()
            B_PTn = [Buf() for _ in range(4)]
            B_ysn = [Buf(), Buf()]
            rcn = [AR.f(512) for _ in range(2)]
            B_rcn = [Buf(), Buf()]
            kbd4 = kbd.rearrange("p (h r k) -> p h r k", h=4, r=36)
            vbd4 = vbd.rearrange("p (h r k) -> p h r k", h=4, r=36)
            kbdh = kbd.rearrange("p (h q) -> p h q", h=4)
            vbdh = vbd.rearrange("p (h q) -> p h q", h=4)
            KBDh = KBD[b].rearrange("p (h q) -> p h q", h=4)
            VBDh = VBD[b].rearrange("p (h q) -> p h q", h=4)
            B_kbdh = [Buf() for _ in range(4)]
            B_vbdh = [Buf() for _ in range(4)]
            B_cqh = [Buf() for _ in range(4)]
            B_cgl = Buf()
            ubf = v3(AR.f(4 * 960), 960)
            B_ubf = Buf()
            P.dma("sp", ubf, ub_in[l].rearrange("h p n -> p h n"), writes=[B_ubf])
            ACT(ubb, ubf, AF.Copy, [B_ubf], [B_ub], scale=8.0)
            for hp in range(4):
                P.dma("sp", kbdh[:, hp, :], KBDh[:, hp, :], reads=[B_KBD[b]], writes=[B_kbdh[hp]])
                P.dma("sp", cq[:, hp, :], pv[:, 10 + hp, :], reads=[B_pT[b]], writes=[B_cqh[hp]])
                P.dma("sp", vbdh[:, hp, :], VBDh[:, hp, :], reads=[B_VBD[b]], writes=[B_vbdh[hp]])
            P.dma("sp", cg, pv[:, 18:22, :], reads=[B_pT[b]], writes=[B_cgl])
            ai = 0
            qblocks = [(m * 512, 512, m) for m in range(4)] + ([] if last else [(S, CT, -1)])
            for hp in range(4):
                for (q0, n, m) in qblocks:
                    ob, db = 3 + ai % 2, 5 + ai % 2
                    s = ai % 2
                    ai += 1
                    items = [(32 + r, 0, n, None) for r in range(4)]
                    if m >= 0:
                        for kr_ in range(32):
                            qa_ = max(8 * m, Qr[kr_][0])
                            qb_ = min(8 * m + 8, Qr[kr_][1])
                            if qb_ > qa_:
                                j0 = 7 - kr_ + qa_
                                assert 0 <= j0 and j0 + (qb_ - qa_) <= 15
                                items.append((kr_, (qa_ - 8 * m) * 64, (qb_ - qa_) * 64, j0))

                    SBK = (0, 1, 2, 7)

                    def nscore(j):
                        row, c0, w, j0 = items[j]
                        sb = SBK[j % 4]
                        fns = []
                        if j0 is not None:
                            fns.append(mm(banks[sb][:, 0:w], ident_b, ubb[:, hp, j0 * 64:j0 * 64 + w], True, False))
                        fns.append(mm(banks[sb][:, 0:w], kbd4[:, hp, row, :], cq[:, hp, q0 + c0:q0 + c0 + w], j0 is None, True))
                        P.group("pe", fns, reads=[B_kbdh[hp], B_cqh[hp], B_ub, B_const], writes=[BK[sb]])
                    nscore(0)
                    if len(items) > 1:
                        nscore(1)
                    for j in range(len(items)):
                        if j + 2 < len(items):
                            nscore(j + 2)
                        row, c0, w, j0 = items[j]
                        sb = SBK[j % 4]
                        pb = j % 4
                        ACT(PTn[pb][:, 0:w], banks[sb][:, 0:w], AF.Exp, [BK[sb]], [B_PTn[pb]], scale=NA_SCALE)
                        first, lastk = (j == 0), (j == len(items) - 1)
                        P.group("pe", [mm(banks[ob][:, c0:c0 + w], vbd4[:, hp, row, :], PTn[pb][:, 0:w], first, lastk),
                                       mm(banks[db][:, c0:c0 + w], onesbd_b, PTn[pb][:, 0:w], first, lastk)],
                                reads=[B_vbdh[hp], B_PTn[pb], B_const], writes=[BK[ob], BK[db]])
                    RCP(rcn[s][:, 0:n], banks[db][:, 0:n], [BK[db]], [B_rcn[s]])
                    TT("dve", rcn[s][:, 0:n], banks[ob][:, 0:n], rcn[s][:, 0:n], ALU.mult, [BK[ob]], [B_rcn[s]])
                    TT("pool", ysn[s][:, 0:n], rcn[s][:, 0:n], cg[:, hp, q0:q0 + n], ALU.mult, [B_rcn[s], B_cgl], [B_ysn[s]])
                    P.dma("sp", yT[b][1024 + hp * 128:1024 + (hp + 1) * 128, q0:q0 + n], ysn[s][:, 0:n], reads=[B_ysn[s]], writes=[B_yT[b]])
            P.barrier()

            AR.reset()
            wout = v3(AR.b(16 * 2048), 2048)
            yt = [v3(AR.b(16 * 512), 512) for _ in range(2)]
            sqm = v3(AR.b(16 * 512), 512)
            B_woutg = [Buf() for _ in range(4)]
            B_sqm = DBuf()
            B_yt = [Buf(), Buf()]
            xts = [v3(AR.f(16 * 512), 512) for _ in range(2)]
            B_xts = [Buf(), Buf()]
            rsm = AR.f(512)
            B_rsm = Buf()
            if last:
                osbs = [AR.f(2048) for _ in range(2)]
                B_osbs = [DBuf(), DBuf()]
            else:
                m_ht = v3(AR.b(16 * 512), 512)
                B_mht = DBuf()
                m_tmp = [AR.f(512) for _ in range(2)]
                B_mtmp = [Buf(), Buf()]
            woutv = wout_in[l].rearrange("(c p) n -> p c n", p=128)
            for g in range(4):
                P.dma("pool", wout[:, :, g * 512:(g + 1) * 512], woutv[:, :, g * 512:(g + 1) * 512], writes=[B_woutg[g]])
            mtiles = TILES[:4] if last else TILES
            oi = [0]

            def merge_loads(ti_):
                t0_, n_ = mtiles[ti_]
                P.dma("sp", yt[ti_ % 2][:, :, 0:n_], yTv[:, :, t0_:t0_ + n_], reads=[B_yT[b]], writes=[B_yt[ti_ % 2]])
                P.dma("sp", xts[ti_ % 2][:, :, 0:n_], XTv[:, :, t0_:t0_ + n_], reads=[B_XT[b]], writes=[B_xts[ti_ % 2]])

            def merge_block(ti_, db_):
                t0_, n_ = mtiles[ti_]
                mi = b if ti_ < 4 else 2
                xt, B_xt = xts[ti_ % 2], B_xts[ti_ % 2]
                bi = next_bank()
                P.group("pe", [mm(banks[bi][:, 0:n_], wout[:, kc, db_ * 128:(db_ + 1) * 128], yt[ti_ % 2][:, kc, 0:n_], kc == 0, kc == 15) for kc in range(16)],
                        reads=[B_woutg[db_ // 4], B_yt[ti_ % 2]], writes=[BK[bi]])
                STT(xt[:, db_, 0:n_], banks[bi][:, 0:n_], gtv(db_, mi, l), xt[:, db_, 0:n_], ALU.mult, ALU.add, [BK[bi], B_mod], [B_xt])

            def fin_squares(ti_):
                t0_, n_ = mtiles[ti_]
                xt, B_xt = xts[ti_ % 2], B_xts[ti_ % 2]
                for c in range(16):
                    ACT(sqm[:, c, 0:n_], xt[:, c, 0:n_], AF.Square, [B_xt], [B_sqm])

            def fin_stats(ti_):
                t0_, n_ = mtiles[ti_]
                bi = 7
                P.group("pe", [mm(banks[bi][:, 0:n_], ones_b, sqm[:, c, 0:n_], c == 0, c == 15) for c in range(16)], reads=[B_sqm, B_const], writes=[BK[bi]])
                ACT(rsm[:, 0:n_], banks[bi][:, 0:n_], AF.Sqrt, [BK[bi], B_const], [B_rsm], bias=eps_t, scale=1.0 / D)
                RCP(rsm[:, 0:n_], rsm[:, 0:n_], [B_rsm], [B_rsm])

            def fin_scale(ti_, c):
                t0_, n_ = mtiles[ti_]
                xt, B_xt = xts[ti_ % 2], B_xts[ti_ % 2]
                TT("pool", xt[:, c, 0:n_], xt[:, c, 0:n_], rsm[:, 0:n_], ALU.mult, [B_rsm], [B_xt])
                ACT(xt[:, c, 0:n_], xt[:, c, 0:n_], AF.Identity, [B_const], [B_xt], scale=fng[:, c:c + 1])

            def fin_out(ti_):
                t0_, n_ = mtiles[ti_]
                xt, B_xt = xts[ti_ % 2], B_xts[ti_ % 2]
                for sub in range(n_ // 128):
                    o_ = oi[0] % 2
                    oi[0] += 1
                    for g in range(4):
                        bi = next_bank()
                        fns = [TRN(banks[bi][:, j * 128:(j + 1) * 128], xt[:, g * 4 + j, sub * 128:(sub + 1) * 128]) for j in range(4)]
                        P.group("pe", fns, reads=[B_xt, B_const], writes=[BK[bi]])
                        evac_copy(osbs[o_][:, g * 512:(g + 1) * 512], banks[bi][:, :], [BK[bi]], [B_osbs[o_]])
                    P.dma("sp", out[b, t0_ + sub * 128:t0_ + (sub + 1) * 128, :], osbs[o_], reads=[B_osbs[o_]], writes=[])

            merge_loads(0)
            if not last:
                nt0 = len(mtiles)

                def nargs(ti_):
                    t0_, n_ = mtiles[ti_]
                    return (xts[ti_ % 2], B_xts[ti_ % 2], n_)
                merge_loads(1)
                for db_ in range(16):
                    merge_block(0, db_)
                for ti in range(nt0):
                    t0, n = mtiles[ti]
                    mi_ = b if ti < 4 else 2
                    xs_, B_xs_, _n = nargs(ti)
                    P.dma("sp", XTv[:, :, t0:t0 + n], xs_[:, :, 0:n], reads=[B_xs_], writes=[B_XT[b]])
                    norm_squares(xs_, B_xs_, n, sqm, B_sqm, True)
                    if ti + 1 < nt0:
                        for db_ in range(16):
                            merge_block(ti + 1, db_)
                            if db_ == 1:
                                norm_stats(n, sqm, B_sqm, rsm, B_rsm)
                            if 2 <= db_ <= 9:
                                for c in (2 * (db_ - 2), 2 * (db_ - 2) + 1):
                                    norm_chunk(xs_, B_xs_, n, c, mi_, l + 1, rsm, B_rsm, m_tmp, B_mtmp, m_ht, B_mht)
                            if db_ == 9 and ti + 2 < nt0:
                                merge_loads(ti + 2)
                    else:
                        norm_stats(n, sqm, B_sqm, rsm, B_rsm)
                        for c in range(16):
                            norm_chunk(xs_, B_xs_, n, c, mi_, l + 1, rsm, B_rsm, m_tmp, B_mtmp, m_ht, B_mht)
                    norm_store(n, b, t0, m_ht, B_mht)
            else:
                nt_ = len(mtiles)
                merge_loads(1)
                for db_ in range(16):
                    merge_block(0, db_)
                for ti in range(nt_):
                    nxt = ti + 1 < nt_
                    fin_squares(ti)
                    if nxt:
                        for db_ in range(16):
                            merge_block(ti + 1, db_)
                            if db_ == 3:
                                fin_stats(ti)
                            if db_ >= 4:
                                fin_scale(ti, db_ - 4)
                        for c in range(12, 16):
                            fin_scale(ti, c)
                    else:
                        fin_stats(ti)
                        for c in range(16):
                            fin_scale(ti, c)
                    fin_out(ti)
                    if ti + 2 < nt_:
                        merge_loads(ti + 2)
            P.barrier()

    P.barrier()
    P.emit(nc, None, sems)
    es.close()
    return nc


def _fm(v, nchunk):
    return np.ascontiguousarray(np.asarray(v, np.float32).reshape(nchunk, 128).T)


def _host_layout(inp):
    f32 = np.float32
    w_in = inp["w_in"]
    o = dict(a_x=0, a_b=512, a_c=1024, a_g=1536, b_qa=2048, b_kva=2432, b_kpe=2688, b_g=2752,
             c_q=3264, c_k=3776, c_v=4288, c_g=4800, glu_a=5312, glu_g=5824, d_g=6336)
    cols = []

    def blk(name, j, w=128):
        cols.extend(range(o[name] + j * w, o[name] + (j + 1) * w))
    for j in range(4):
        blk("a_x", j); blk("a_c", j); blk("a_b", j); blk("a_g", j)
    for j in range(4):
        blk("glu_a", j); blk("glu_g", j)
    for j in range(4):
        blk("d_g", j)
    for j in range(3):
        blk("b_qa", j)
    cols.extend(range(o["b_kpe"], o["b_kpe"] + 64))
    cols.extend(list(range(o["b_kpe"] + 32, o["b_kpe"] + 64)) + list(range(o["b_kpe"], o["b_kpe"] + 32)))
    for j in range(2):
        blk("b_kva", j)
    for j in range(4):
        blk("b_g", j)
    for nm in ("c_q", "c_k", "c_g", "c_v"):
        for j in range(4):
            blk(nm, j)
    cols = np.asarray(cols)
    assert cols.size == 6912
    win_r = np.ascontiguousarray(w_in[:, :, cols])
    cq = []
    for h in range(4):
        cq.extend(range(h * 192, h * 192 + 128))
    for h in range(4):
        cq.extend(range(h * 192 + 128, h * 192 + 192))
    for h in range(4):
        cq.extend(list(range(h * 192 + 160, h * 192 + 192)) + list(range(h * 192 + 128, h * 192 + 160)))
    wuq_r = np.ascontiguousarray(inp["mla_w_uq"][:, :, np.asarray(cq)])
    ckv = []
    for h in range(4):
        ckv.extend(range(h * 256, h * 256 + 128))
    for h in range(4):
        ckv.extend(range(h * 256 + 128, h * 256 + 256))
    wukv_r = np.ascontiguousarray(inp["mla_w_ukv"][:, :, np.asarray(ckv)])
    vecs = np.zeros((L, 128, 64), f32)
    dww = np.zeros((L, 128, 124), f32)
    ng3 = np.zeros((L, 128, 48), f32)
    bada3 = np.zeros((L, 128, 144), f32)
    for l in range(L):
        acw = inp["a_conv_w"][l]
        for c in range(4):
            for k in range(3):
                vecs[l, :, c * 3 + k] = acw[k, c * 128:(c + 1) * 128]
        vecs[l, :, 12:16] = _fm(inp["d_dw_b"][l], 4)
        vecs[l, :, 16:20] = _fm(inp["d_ln_g"][l], 4)
        vecs[l, :, 20:24] = _fm(inp["d_ln_b"][l], 4)
        vecs[l, :, 24:28] = _fm(inp["d_pw_b"][l], 4)
        vecs[l, :, 28:31] = _fm(inp["mla_q_norm"][l], 3)
        vecs[l, :, 31:33] = _fm(inp["mla_kv_norm"][l], 2)
        dw = inp["d_dw_w"][l]
        for c in range(4):
            dww[l, :, c * 31:(c + 1) * 31] = dw[:, c * 128:(c + 1) * 128].T
        ng3[l] = np.repeat(_fm(inp["norm_g"][l], 16), 3, axis=1)
        bada3[l] = np.repeat(_fm(inp["b_ada"][l], 48), 3, axis=1)
    fng = _fm(inp["final_norm_g"], 16)
    t = np.arange(S)
    row = (t // GW).astype(f32)
    col = (t % GW).astype(f32)
    freqs = (np.float32(10000.0) ** (-(np.arange(16, dtype=f32) * np.float32(2.0) / np.float32(32)))).astype(f32)
    ang = np.concatenate([row[:, None] * freqs, col[:, None] * freqs], axis=-1).astype(f32)
    cos, sin = np.cos(ang).astype(f32), np.sin(ang).astype(f32)
    rope = np.zeros((2, 64, NT), f32)
    rope[0, :, S:] = 1.0
    rope[0, 0:32, :S] = cos.T
    rope[0, 32:64, :S] = cos.T
    rope[1, 0:32, :S] = -sin.T
    rope[1, 32:64, :S] = sin.T
    rpb = inp["na_rpb"]
    kc = np.arange(64)[:, None]
    qc = np.arange(64)[None, :]
    cstart = np.clip(qc - 8, 0, 48)
    valid = (kc >= cstart) & (kc < cstart + 16)
    dc = np.clip(kc - qc + 15, 0, 30)
    ub = np.full((L, 4, 128, 15, 64), NEG, f32)
    for l in range(L):
        for h in range(8):
            for j in range(15):
                dr = 7 - j
                tab = rpb[l, h, dr + 7][dc]
                ub[l, h // 2, (h % 2) * 64:(h % 2) * 64 + 64, j, :] = np.where(valid, tab, np.float32(NEG))
    ub = ub.reshape(L, 4, 128, 960)
    ident = np.eye(128, dtype=f32)
    onesbd = np.zeros((128, 128), f32)
    onesbd[0:64, 0:64] = 1.0
    onesbd[64:128, 64:128] = 1.0
    shared = dict(ng3=ng3, bada3=bada3, w_ada=np.ascontiguousarray(inp["w_ada"], f32), win_r=win_r,
                  w_out=np.ascontiguousarray(inp["w_out"], f32), wuq_r=wuq_r, wukv_r=wukv_r,
                  d_pw_w=np.ascontiguousarray(inp["d_pw_w"], f32), vecs=vecs, dww=dww, fng=fng, rope=rope, ub=ub,
                  ident=ident, onesbd=onesbd)
    in_maps = []
    for core in range(NCORES):
        bs = [core * NB + i for i in range(NB)]
        cT = np.zeros((128, 16, 3), f32)
        for i, bb in enumerate(bs):
            cT[:, :, i] = _fm(inp["c"][bb], 16)
        cT[:, :, 2] = _fm(inp["c_ctx"], 16)
        m = dict(shared)
        m["x"] = np.ascontiguousarray(inp["x"][bs[0]:bs[0] + NB], f32)
        m["ctx"] = np.ascontiguousarray(inp["ctx"][bs[0]:bs[0] + NB], f32)
        m["cT"] = cT.reshape(128, 48)
        in_maps.append(m)
    return in_maps


_NC_CACHE = {}


def kernel(**inputs):
    inp = {k: np.asarray(v) for k, v in inputs.items()}
    in_maps = _host_layout(inp)
    if "nc" not in _NC_CACHE:
        _NC_CACHE["nc"] = build_nc()
    nc = _NC_CACHE["nc"]
    res = run_bass_kernel_spmd(nc, in_maps, core_ids=list(range(NCORES)))
    outs = [np.asarray(r["out"]) for r in res.results]
    return np.concatenate(outs, axis=0).astype(np.float32)
```
